# Optimizing a Trainium2 kernel written in Bass

```python
import math
import jax
import jax.numpy as jnp
from jax import lax
import numpy as np

D_MODEL = 1024
BATCH = 4
SEQ = 4096
DEPTH = 2
DEC_BATCH = 32
DEC_SEQ = 8
PAST_LEN = 8192
PAGE_SIZE = 128

HEAD_DIM = 64
GLA_WIDTH = D_MODEL // 4
NSA_WIDTH = D_MODEL // 2
ML_WIDTH = D_MODEL - GLA_WIDTH - NSA_WIDTH
D_MIX = GLA_WIDTH + NSA_WIDTH + ML_WIDTH

GLA_HEADS = GLA_WIDTH // HEAD_DIM
GLA_DK = HEAD_DIM // 2
GLA_DV = HEAD_DIM
GLA_RANK = 16
GLA_TAU = 16.0
GLA_CHUNK = 64

NSA_HEADS = NSA_WIDTH // HEAD_DIM
NSA_KV_HEADS = 2
NSA_HPG = NSA_HEADS // NSA_KV_HEADS
CMP_BLOCK = 32
CMP_STRIDE = 16
SEL_BLOCK = 64
N_SELECT = 16
WINDOW = 512
Q_BLOCK = 128
N_KV_SLOTS = 4
ROT_DIM = HEAD_DIM // 4
ROPE_THETA = 500000.0
ATTN_SCALE = HEAD_DIM ** -0.5

ML_HEADS = ML_WIDTH // HEAD_DIM
ML_DH = HEAD_DIM
ML_CHUNK = 64
CONV_W = 4

SPLIT_SIZES = (GLA_HEADS * GLA_DK, GLA_HEADS * GLA_DK, GLA_WIDTH, GLA_RANK, GLA_WIDTH,
               NSA_WIDTH, 6 * NSA_KV_HEADS * HEAD_DIM, 3 * NSA_HEADS, NSA_WIDTH,
               2 * ML_WIDTH, ML_WIDTH, 2 * ML_HEADS, ML_WIDTH, ML_WIDTH)
D_IN = sum(SPLIT_SIZES)

kernel_name = 'hybrid_gla_nsa_mlstm_step'


def _rms_norm(x, g, eps=1e-6):
    xf = x.astype(jnp.float32)
    y = xf * lax.rsqrt(jnp.mean(xf * xf, axis=-1, keepdims=True) + eps)
    return (y * g.astype(jnp.float32)).astype(x.dtype)


def _rope(x, pos):
    half = ROT_DIM // 2
    inv = jnp.exp(-math.log(ROPE_THETA) * jnp.arange(half, dtype=jnp.float32) * 2.0 / ROT_DIM)
    ang = pos.astype(jnp.float32)[:, None] * inv[None, :]
    cos = jnp.cos(ang)[:, None, :]
    sin = jnp.sin(ang)[:, None, :]
    xf = x.astype(jnp.float32)
    x1, x2, rest = xf[..., :half], xf[..., half:ROT_DIM], xf[..., ROT_DIM:]
    return jnp.concatenate([x1 * cos - x2 * sin, x2 * cos + x1 * sin, rest], axis=-1).astype(x.dtype)


def _masked_softmax(s, mask):
    s = jnp.where(mask, s, -jnp.inf)
    m = jnp.max(s, axis=-1, keepdims=True)
    m = jnp.where(jnp.isfinite(m), m, 0.0)
    e = jnp.where(mask, jnp.exp(s - m), 0.0)
    return e / jnp.maximum(jnp.sum(e, axis=-1, keepdims=True), 1e-30)


def _to_chunks(x, c):
    b, l = x.shape[:2]
    x = x.reshape((b, l // c, c) + x.shape[2:])
    x = jnp.moveaxis(x, 1, 0)
    return jnp.moveaxis(x, 2, 3)


def _from_chunks(x):
    n, b, h, c, d = x.shape
    return jnp.transpose(x, (1, 0, 3, 2, 4)).reshape(b, n * c, h, d)


def _gla_chunked(q, k, v, log_a, s0):
    l = q.shape[1]
    c = math.gcd(l, GLA_CHUNK)
    qc, kc, vc, gc = (_to_chunks(t.astype(jnp.float32), c) for t in (q, k, v, log_a))
    tri = jnp.tril(jnp.ones((c, c), dtype=bool))

    def step(s, inp):
        qi, ki, vi, gi = inp
        bcum = jnp.cumsum(gi, axis=2)
        diff = bcum[:, :, :, None, :] - bcum[:, :, None, :, :]
        dec = jnp.exp(jnp.where(tri[:, :, None], diff, -jnp.inf))
        a = jnp.einsum('bhid,bhjd,bhijd->bhij', qi, ki, dec)
        o = (jnp.einsum('bhid,bhde->bhie', qi * jnp.exp(bcum), s)
             + jnp.einsum('bhij,bhje->bhie', a, vi))
        blast = bcum[:, :, -1]
        s = (jnp.exp(blast)[..., None] * s
             + jnp.einsum('bhjd,bhje->bhde', ki * jnp.exp(blast[:, :, None] - bcum), vi))
        return s, o

    s, o = lax.scan(step, s0.astype(jnp.float32), (qc, kc, vc, gc))
    return _from_chunks(o), s


def _mlstm_chunked(q, k, v, i_pre, logf, c0, n0, m0):
    l = q.shape[1]
    c = math.gcd(l, ML_CHUNK)
    qc, kc, vc = (_to_chunks(t.astype(jnp.float32), c) for t in (q, k, v))
    ic, fc = (_to_chunks(t.astype(jnp.float32), c) for t in (i_pre, logf))
    tri = jnp.tril(jnp.ones((c, c), dtype=bool))

    def step(carry, inp):
        cs, ns, ms = carry
        qi, ki, vi, ii, fi = inp
        fcum = jnp.cumsum(fi, axis=-1)
        dmat = jnp.where(tri, fcum[..., :, None] - fcum[..., None, :] + ii[..., None, :], -jnp.inf)
        inter = fcum + ms[..., None]
        m = jnp.maximum(inter, jnp.max(dmat, axis=-1))
        w_int = jnp.exp(inter - m)
        sij = jnp.einsum('bhid,bhjd->bhij', qi, ki) * jnp.exp(dmat - m[..., None])
        num = (w_int[..., None] * jnp.einsum('bhed,bhid->bhie', cs, qi)
               + jnp.einsum('bhij,bhje->bhie', sij, vi))
        den = w_int * jnp.einsum('bhd,bhid->bhi', ns, qi) + jnp.sum(sij, axis=-1)
        h = num / jnp.maximum(jnp.abs(den), jnp.exp(-m))[..., None]
        m_last = m[..., -1]
        f_last = fcum[..., -1]
        decay = jnp.exp(f_last + ms - m_last)
        wj = jnp.exp(f_last[..., None] - fcum + ii - m_last[..., None])
        cs = decay[..., None, None] * cs + jnp.einsum('bhj,bhje,bhjd->bhed', wj, vi, ki)
        ns = decay[..., None] * ns + jnp.einsum('bhj,bhjd->bhd', wj, ki)
        return (cs, ns, m_last), h

    init = (c0.astype(jnp.float32), n0.astype(jnp.float32), m0.astype(jnp.float32))
    (cs, ns, ms), h = lax.scan(step, init, (qc, kc, vc, ic, fc))
    return _from_chunks(h), cs, ns, ms


def _cmp_summaries(x, pe, w):
    b, t, g, d = x.shape
    n_half = t // CMP_STRIDE
    halves = x[:, :n_half * CMP_STRIDE].astype(jnp.float32).reshape(b, n_half, CMP_STRIDE, g, d)
    wf = w.astype(jnp.float32)
    first = jnp.einsum('bnsgd,sde->bnge', halves[:, :-1], wf[:CMP_STRIDE])
    second = jnp.einsum('bnsgd,sde->bnge', halves[:, 1:], wf[CMP_STRIDE:])
    bias = jnp.einsum('sd,sde->e', pe.astype(jnp.float32), wf)
    return first + second + bias


def _overlap_matrix(n_cmp, n_sel):
    a = SEL_BLOCK // CMP_STRIDE
    bb = CMP_BLOCK // CMP_STRIDE
    i = jnp.arange(n_cmp)[:, None]
    j = jnp.arange(n_sel)[None, :]
    s = i - a * j + (bb - 1)
    cnt = jnp.minimum(jnp.minimum(s + 1, a + bb - 1 - s), min(a, bb))
    return jnp.maximum(cnt, 0).astype(jnp.float32)


def _nsa_selected(q, k_hist, v_hist, idx, pos):
    b, l = q.shape[:2]
    t_len = k_hist.shape[1]
    n_sel = -(-t_len // SEL_BLOCK)
    n_k = idx.shape[-1]
    pad = n_sel * SEL_BLOCK - t_len

    def blocks(x):
        x = jnp.pad(x.astype(jnp.float32), ((0, 0), (0, pad), (0, 0), (0, 0)))
        return x.reshape(b, n_sel, SEL_BLOCK, NSA_KV_HEADS, HEAD_DIM).transpose(0, 3, 1, 2, 4)

    kb, vb = blocks(k_hist), blocks(v_hist)
    qb = math.gcd(l, Q_BLOCK)
    nq = l // qb
    qx = q.astype(jnp.float32).reshape(b * nq, qb, NSA_KV_HEADS, NSA_HPG, HEAD_DIM)
    ix = idx.reshape(b, NSA_KV_HEADS, nq, qb, n_k).transpose(0, 2, 1, 3, 4).reshape(b * nq, NSA_KV_HEADS, qb, n_k)
    px = jnp.tile(pos.reshape(nq, qb), (b, 1))
    bx = jnp.repeat(jnp.arange(b), nq)
    g_ix = jnp.arange(NSA_KV_HEADS)[:, None, None]
    offs = jnp.arange(SEL_BLOCK)
    n_keys = n_k * SEL_BLOCK

    def attend(args):
        qi, ii, pi, bi = args
        ks = kb[bi][g_ix, ii]
        vs = vb[bi][g_ix, ii]
        kpos = ii[..., None] * SEL_BLOCK + offs
        mask = (kpos <= pi[None, :, None, None]).reshape(NSA_KV_HEADS, 1, qb, n_keys)
        s = jnp.einsum('qghd,gqkjd->ghqkj', qi, ks).reshape(NSA_KV_HEADS, NSA_HPG, qb, n_keys) * ATTN_SCALE
        p = _masked_softmax(s, mask)
        return jnp.einsum('ghqn,gqnd->qghd', p, vs.reshape(NSA_KV_HEADS, qb, n_keys, HEAD_DIM))

    o = lax.map(attend, (qx, ix, px, bx))
    return o.reshape(b, l, NSA_HEADS, HEAD_DIM)


def _nsa_window(q, k_ctx, v_ctx, pos0):
    b, l = q.shape[:2]
    lc = k_ctx.shape[1]
    padw = ((0, 0), (WINDOW, 0), (0, 0), (0, 0))
    kp = jnp.pad(k_ctx.astype(jnp.float32), padw)
    vp = jnp.pad(v_ctx.astype(jnp.float32), padw)
    qb = math.gcd(l, Q_BLOCK)
    nq = l // qb
    span = WINDOW + qb
    qx = jnp.moveaxis(q.astype(jnp.float32).reshape(b, nq, qb, NSA_KV_HEADS, NSA_HPG, HEAD_DIM), 1, 0)
    starts = jnp.arange(nq) * qb
    first_pos = pos0 + l - lc

    def attend(args):
        qi, i0 = args
        kk = lax.dynamic_slice_in_dim(kp, i0 + lc - l, span, axis=1)
        vv = lax.dynamic_slice_in_dim(vp, i0 + lc - l, span, axis=1)
        kpos = pos0 + i0 - WINDOW + jnp.arange(span)
        qpos = pos0 + i0 + jnp.arange(qb)
        mask = ((kpos[None, :] >= first_pos) & (kpos[None, :] <= qpos[:, None])
                & (kpos[None, :] > qpos[:, None] - WINDOW))
        s = jnp.einsum('bqghd,bkgd->bghqk', qi, kk) * ATTN_SCALE
        p = _masked_softmax(s, mask)
        return jnp.einsum('bghqk,bkgd->bqghd', p, vv)

    o = lax.map(attend, (qx, starts))
    return jnp.moveaxis(o, 0, 1).reshape(b, l, NSA_HEADS, HEAD_DIM)


def _nsa_mixer(n_q, n_kv, n_g, past_kv, win_buf, win_keep, pos0, q_g, k_g, cmp_pos, cmp_w):
    b, l, _ = n_q.shape
    pos = pos0 + jnp.arange(l)
    q = _rms_norm(n_q.reshape(b, l, NSA_HEADS, HEAD_DIM), q_g)
    q_rot = _rope(q, pos)
    kv = n_kv.reshape(b, l, 6, NSA_KV_HEADS, HEAD_DIM)
    k_slc = _rope(_rms_norm(kv[:, :, 2], k_g[1]), pos)
    k_win = _rope(_rms_norm(kv[:, :, 4], k_g[2]), pos)
    new_rows = jnp.stack([kv[:, :, 0], kv[:, :, 1], k_slc, kv[:, :, 3]], axis=2)
    new_win = jnp.stack([k_win, kv[:, :, 5]], axis=2)
    hist = jnp.concatenate([past_kv.astype(new_rows.dtype), new_rows], axis=1)
    ctx = jnp.concatenate([win_buf.astype(new_win.dtype), new_win], axis=1)
    t_len = hist.shape[1]

    kc = _rms_norm(_cmp_summaries(hist[:, :, 0], cmp_pos[0], cmp_w[0]), k_g[0])
    vc = _cmp_summaries(hist[:, :, 1], cmp_pos[1], cmp_w[1])
    n_cmp = kc.shape[1]
    qg = q.astype(jnp.float32).reshape(b, l, NSA_KV_HEADS, NSA_HPG, HEAD_DIM)
    s = jnp.einsum('blghd,bngd->bghln', qg, kc) * ATTN_SCALE
    cmp_end = jnp.arange(n_cmp) * CMP_STRIDE + CMP_BLOCK - 1
    p_cmp = _masked_softmax(s, cmp_end[None, :] <= pos[:, None])
    o_cmp = jnp.einsum('bghln,bngd->blghd', p_cmp, vc).reshape(b, l, NSA_HEADS, HEAD_DIM)

    n_sel = -(-t_len // SEL_BLOCK)
    imp = jnp.einsum('bghln,nj->bglj', p_cmp, _overlap_matrix(n_cmp, n_sel))
    blk = jnp.arange(n_sel)[None, :]
    cur = (pos // SEL_BLOCK)[:, None]
    valid = blk * SEL_BLOCK <= pos[:, None]
    forced = (blk == 0) | (blk == cur) | (blk == cur - 1)
    score = jnp.where(forced, jnp.inf, jnp.where(valid, imp, -jnp.inf))
    _, idx = lax.top_k(score, min(N_SELECT, n_sel))
    o_slc = _nsa_selected(q_rot, hist[:, :, 2], hist[:, :, 3], idx, pos)

    o_win = _nsa_window(q_rot, ctx[:, :, 0], ctx[:, :, 1], pos0)

    gate = jax.nn.sigmoid(n_g.astype(jnp.float32)).reshape(b, l, 3, NSA_HEADS, 1)
    o = gate[:, :, 0] * o_cmp + gate[:, :, 1] * o_slc + gate[:, :, 2] * o_win
    return o.reshape(b, l, NSA_WIDTH), new_rows, ctx[:, ctx.shape[1] - win_keep:]


def _hybrid_layer(x, pos0, past_kv, win_buf, win_keep, s_gla, c_ml, n_ml, m_ml, conv_ml,
                  norm_g, w_in, w_out, gla_w_gate, gla_b_gate, gla_norm_g,
                  nsa_q_norm_g, nsa_k_norm_g, nsa_cmp_pos, nsa_cmp_w,
                  ml_conv_w, ml_conv_b, ml_gate_b, ml_norm_g):
    b, l, _ = x.shape
    u = _rms_norm(x, norm_g) @ w_in
    points = np.cumsum(SPLIT_SIZES)[:-1].tolist()
    (g_q, g_k, g_v, g_a, g_z, n_q, n_kv, n_g, n_z,
     m_qk, m_v, m_if, m_o, m_z) = jnp.split(u, points, axis=-1)

    gq = g_q.reshape(b, l, GLA_HEADS, GLA_DK) * GLA_DK ** -0.5
    gk = g_k.reshape(b, l, GLA_HEADS, GLA_DK)
    gv = g_v.reshape(b, l, GLA_HEADS, GLA_DV)
    log_a = jax.nn.log_sigmoid((g_a @ gla_w_gate + gla_b_gate).astype(jnp.float32)).reshape(b, l, GLA_HEADS, GLA_DK) / GLA_TAU
    o_a, s_new = _gla_chunked(gq, gk, gv, log_a, s_gla)
    o_a = _rms_norm(o_a, gla_norm_g).reshape(b, l, GLA_WIDTH) * jax.nn.silu(g_z.astype(jnp.float32))

    o_b, new_rows, new_win = _nsa_mixer(n_q, n_kv, n_g, past_kv, win_buf, win_keep, pos0,
                                        nsa_q_norm_g, nsa_k_norm_g, nsa_cmp_pos, nsa_cmp_w)
    o_b = o_b * jax.nn.silu(n_z.astype(jnp.float32))

    xp = jnp.concatenate([conv_ml.astype(m_qk.dtype), m_qk], axis=1)
    conv = sum(xp[:, w:w + l] * ml_conv_w[w] for w in range(CONV_W)) + ml_conv_b
    qk = jax.nn.silu(conv)
    mq = qk[..., :ML_WIDTH].reshape(b, l, ML_HEADS, ML_DH)
    mk = qk[..., ML_WIDTH:].reshape(b, l, ML_HEADS, ML_DH) * ML_DH ** -0.5
    mv = m_v.reshape(b, l, ML_HEADS, ML_DH)
    gates = m_if.astype(jnp.float32).reshape(b, l, 2, ML_HEADS) + ml_gate_b.astype(jnp.float32)
    h, c_new, n_new, m_new = _mlstm_chunked(mq, mk, mv, gates[:, :, 0], jax.nn.log_sigmoid(gates[:, :, 1]),
                                            c_ml, n_ml, m_ml)
    o_gate = jax.nn.sigmoid(m_o.astype(jnp.float32)).reshape(b, l, ML_HEADS, ML_DH)
    o_c = (_rms_norm(h, ml_norm_g) * o_gate).reshape(b, l, ML_WIDTH) * jax.nn.silu(m_z.astype(jnp.float32))
    conv_new = xp[:, l:]

    mixed = jnp.concatenate([o_a, o_b, o_c], axis=-1).astype(x.dtype)
    y = x + mixed @ w_out
    return y, new_rows, new_win, s_new, c_new, n_new, m_new, conv_new


def setup_inputs(seed: int = 0) -> dict:
    key = jax.random.key(seed)
    ks = jax.random.split(key, 24)
    f32 = jnp.float32
    nrm = jax.random.normal
    n_pages = PAST_LEN // PAGE_SIZE
    n_used = DEC_BATCH * n_pages
    n_pool = n_used + max(1, n_used // 4)
    win_buf = min(WINDOW, PAST_LEN)
    page_table = jax.random.permutation(ks[9], n_pool)[:n_used].reshape(DEC_BATCH, n_pages).astype(jnp.int32)
    f_bias = jnp.stack([jnp.zeros((ML_HEADS,), f32), jnp.linspace(3.0, 6.0, ML_HEADS, dtype=f32)])
    return {
        'x_prompt': nrm(ks[0], (BATCH, SEQ, D_MODEL), f32),
        'x_sample': nrm(ks[1], (DEC_BATCH, DEC_SEQ, D_MODEL), f32),
        'cache_nsa_kv': nrm(ks[2], (DEPTH, n_pool, PAGE_SIZE, N_KV_SLOTS, NSA_KV_HEADS, HEAD_DIM), f32),
        'state_nsa_win': nrm(ks[3], (DEPTH, DEC_BATCH, win_buf, 2, NSA_KV_HEADS, HEAD_DIM), f32),
        'state_gla': 0.3 * nrm(ks[4], (DEPTH, DEC_BATCH, GLA_HEADS, GLA_DK, GLA_DV), f32),
        'state_mlstm_C': 0.3 * nrm(ks[5], (DEPTH, DEC_BATCH, ML_HEADS, ML_DH, ML_DH), f32),
        'state_mlstm_n': 0.3 * nrm(ks[6], (DEPTH, DEC_BATCH, ML_HEADS, ML_DH), f32),
        'state_mlstm_m': nrm(ks[7], (DEPTH, DEC_BATCH, ML_HEADS), f32),
        'state_mlstm_conv': nrm(ks[8], (DEPTH, DEC_BATCH, CONV_W - 1, 2 * ML_WIDTH), f32),
        'page_table': page_table,
        'norm_g': 1.0 + 0.02 * nrm(ks[10], (DEPTH, D_MODEL), f32),
        'w_in': nrm(ks[11], (DEPTH, D_MODEL, D_IN), f32) * D_MODEL ** -0.5,
        'w_out': nrm(ks[12], (DEPTH, D_MIX, D_MODEL), f32) * D_MIX ** -0.5,
        'gla_w_gate': nrm(ks[13], (DEPTH, GLA_RANK, GLA_HEADS * GLA_DK), f32) * GLA_RANK ** -0.5,
        'gla_b_gate': 0.1 * nrm(ks[14], (DEPTH, GLA_HEADS * GLA_DK), f32),
        'gla_norm_g': 1.0 + 0.02 * nrm(ks[15], (DEPTH, GLA_DV), f32),
        'nsa_q_norm_g': 1.0 + 0.02 * nrm(ks[16], (DEPTH, HEAD_DIM), f32),
        'nsa_k_norm_g': 1.0 + 0.02 * nrm(ks[17], (DEPTH, 3, HEAD_DIM), f32),
        'nsa_cmp_pos': 0.1 * nrm(ks[18], (DEPTH, 2, CMP_BLOCK, HEAD_DIM), f32),
        'nsa_cmp_w': nrm(ks[19], (DEPTH, 2, CMP_BLOCK, HEAD_DIM, HEAD_DIM), f32) * (CMP_BLOCK * HEAD_DIM) ** -0.5,
        'ml_conv_w': nrm(ks[20], (DEPTH, CONV_W, 2 * ML_WIDTH), f32) * CONV_W ** -0.5,
        'ml_conv_b': 0.01 * nrm(ks[21], (DEPTH, 2 * ML_WIDTH), f32),
        'ml_gate_b': f_bias[None] + 0.1 * nrm(ks[22], (DEPTH, 2, ML_HEADS), f32),
        'ml_norm_g': 1.0 + 0.02 * nrm(ks[23], (DEPTH, ML_DH), f32),
    }


def reference(x_prompt, x_sample, cache_nsa_kv, state_nsa_win, state_gla, state_mlstm_C, state_mlstm_n,
              state_mlstm_m, state_mlstm_conv, page_table, norm_g, w_in, w_out, gla_w_gate, gla_b_gate,
              gla_norm_g, nsa_q_norm_g, nsa_k_norm_g, nsa_cmp_pos, nsa_cmp_w, ml_conv_w, ml_conv_b,
              ml_gate_b, ml_norm_g):
    f32 = jnp.float32
    bp, sp, _ = x_prompt.shape
    bs, _, _ = x_sample.shape
    n_pages = page_table.shape[1]
    past_len = n_pages * cache_nsa_kv.shape[2]
    dt = x_prompt.dtype
    empty_kv = jnp.zeros((bp, 0, N_KV_SLOTS, NSA_KV_HEADS, HEAD_DIM), dt)
    empty_win = jnp.zeros((bp, 0, 2, NSA_KV_HEADS, HEAD_DIM), dt)
    zero_gla = jnp.zeros((bp, GLA_HEADS, GLA_DK, GLA_DV), f32)
    zero_c = jnp.zeros((bp, ML_HEADS, ML_DH, ML_DH), f32)
    zero_n = jnp.zeros((bp, ML_HEADS, ML_DH), f32)
    zero_m = jnp.zeros((bp, ML_HEADS), f32)
    zero_conv = jnp.zeros((bp, CONV_W - 1, 2 * ML_WIDTH), dt)
    keep_p = min(WINDOW, sp)
    keep_s = state_nsa_win.shape[2]

    y_prompt, y_sample = x_prompt, x_sample
    p_layers, s_layers = [], []
    for layer in range(DEPTH):
        w = (norm_g[layer], w_in[layer], w_out[layer], gla_w_gate[layer], gla_b_gate[layer], gla_norm_g[layer],
             nsa_q_norm_g[layer], nsa_k_norm_g[layer], nsa_cmp_pos[layer], nsa_cmp_w[layer],
             ml_conv_w[layer], ml_conv_b[layer], ml_gate_b[layer], ml_norm_g[layer])
        y_prompt, *p_new = _hybrid_layer(y_prompt, 0, empty_kv, empty_win, keep_p, zero_gla, zero_c, zero_n,
                                         zero_m, zero_conv, *w)
        past = cache_nsa_kv[layer][page_table].reshape(bs, past_len, N_KV_SLOTS, NSA_KV_HEADS, HEAD_DIM)
        y_sample, *s_new = _hybrid_layer(y_sample, past_len, past, state_nsa_win[layer], keep_s,
                                         state_gla[layer], state_mlstm_C[layer], state_mlstm_n[layer],
                                         state_mlstm_m[layer], state_mlstm_conv[layer], *w)
        p_layers.append(p_new)
        s_layers.append(s_new)
    p_kv, p_win, p_gla, p_c, p_n, p_m, p_conv = [jnp.stack(z) for z in zip(*p_layers)]
    s_kv, s_win, s_gla, s_c, s_n, s_m, s_conv = [jnp.stack(z) for z in zip(*s_layers)]
    return (y_prompt, y_sample, p_kv, s_kv, p_win, s_win, p_gla, s_gla, p_c, s_c, p_n, s_n, p_m, s_m, p_conv, s_conv)
```

```python
import math
import numpy as np
from contextlib import ExitStack
import concourse.bass as bass
import concourse.mybir as mybir
from concourse.bass_utils import run_bass_kernel_spmd

F32 = mybir.dt.float32
BF16 = mybir.dt.bfloat16
I32 = mybir.dt.int32
AF = mybir.ActivationFunctionType
ALU = mybir.AluOpType
AX = mybir.AxisListType

D = 1024
SEQ = 4096
DEPTH = 2
NEG = -30000.0
EPS = 1e-6
O_GQ, O_GK, O_GV, O_GA, O_GZ = 0, 128, 256, 512, 528
O_NQ, O_NKV, O_NG, O_NZ = 784, 1296, 2064, 2088
O_MQK, O_MV, O_MIF, O_MO, O_MZ = 2600, 3112, 3368, 3376, 3632
D_IN = 3888
ATT_SCALE = 0.125


class Buf:
    __slots__ = ("w", "rd", "excl")

    def __init__(self, excl=False):
        self.w = None
        self.rd = []
        self.excl = excl


class V:
    __slots__ = ("ap", "bs")

    def __init__(self, ap, bs):
        self.ap = ap
        self.bs = bs if isinstance(bs, (list, tuple)) else [bs]

    def __getitem__(self, key):
        return V(self.ap[key], self.bs)

    def re(self, pattern_, **kw):
        return V(self.ap.rearrange(pattern_, **kw), self.bs)

    def bc(self, shape):
        return V(self.ap.to_broadcast(list(shape)), self.bs)

    def un(self, axis):
        return V(self.ap.unsqueeze(axis), self.bs)

    def wb(self, b):
        return V(self.ap, b)


class Em:
    def __init__(self, nc, st):
        self.nc = nc
        self.st = st
        self.E = {"pe": nc.tensor, "act": nc.scalar, "dve": nc.vector, "pool": nc.gpsimd, "sp": nc.sync}
        self.sem = {}
        self.cnt = {}
        for k in list(self.E) + ["d_sp", "d_act", "d_pool"]:
            self.sem[k] = st.enter_context(nc.semaphore("s_" + k))
            self.cnt[k] = 0
        self.seen = {k: {} for k in self.E}
        self.dkey = {"sp": "d_sp", "act": "d_act", "pool": "d_pool"}
        self.serbuf = {}
        self.nrot = 0
        self.n_ins = 0
        import os as _os
        self.limit = int(_os.environ.get("OP_LIMIT", "100000000"))

    def _deps(self, eng, reads, writes):
        deps = {}
        for v in reads:
            for b in v.bs:
                if b.w is not None:
                    deps[b.w[0]] = max(deps.get(b.w[0], 0), b.w[1])
                if b.excl:
                    for r in b.rd:
                        if r[0] != eng:
                            deps[r[0]] = max(deps.get(r[0], 0), r[1])
        for v in writes:
            for b in v.bs:
                if b.w is not None:
                    deps[b.w[0]] = max(deps.get(b.w[0], 0), b.w[1])
                for r in b.rd:
                    deps[r[0]] = max(deps.get(r[0], 0), r[1])
        for sk, val in deps.items():
            if sk == eng and eng == "pe":
                continue
            if self.seen[eng].get(sk, 0) < val:
                self.E[eng].wait_ge(self.sem[sk], val)
                self.seen[eng][sk] = val

    def _mark(self, tok, reads, writes):
        for v in reads:
            for b in v.bs:
                b.rd.append(tok)
                if len(b.rd) > 64:
                    mx = {}
                    for r in b.rd:
                        mx[r[0]] = max(mx.get(r[0], 0), r[1])
                    b.rd = list(mx.items())
        for v in writes:
            for b in v.bs:
                b.w = tok
                b.rd = []

    def rotate(self, q):
        self.nrot += 1
        k = "d_%s#%d" % (q, self.nrot)
        self.sem[k] = self.st.enter_context(self.nc.semaphore("s_" + k.replace("#", "_")))
        self.cnt[k] = 0
        self.dkey[q] = k

    def op(self, eng, build, reads=(), writes=()):
        if self.n_ins >= self.limit:
            return None
        self._deps(eng, reads, writes)
        ins = build(self.E[eng])
        self.cnt[eng] += 1
        ins.then_inc(self.sem[eng], 1)
        self._mark((eng, self.cnt[eng]), reads, writes)
        self.n_ins += 1
        return ins

    def lane_key(self, q, lane):
        if lane is None:
            return self.dkey[q]
        if lane not in self.dkey:
            self.rotate(lane)
        return self.dkey[lane]

    def dma(self, q, out, in_, lane=None, serial=False, **kw):
        reads = [in_] if isinstance(in_, V) else []
        writes = [out] if isinstance(out, V) else []
        if serial:
            lane = lane or ("ser_" + q)
            if lane not in self.serbuf:
                self.serbuf[lane] = V(None, Buf())
            writes = writes + [self.serbuf[lane]]
        if self.n_ins >= self.limit:
            return None
        self._deps(q, reads, writes)
        o = out.ap if isinstance(out, V) else out
        i = in_.ap if isinstance(in_, V) else in_
        ins = self.E[q].dma_start(out=o, in_=i, **kw)
        k = self.lane_key(q, lane)
        self.cnt[k] += 16
        ins.then_inc(self.sem[k], 16)
        self._mark((k, self.cnt[k]), reads, writes)
        self.n_ins += 1
        return ins

    def mm(self, out, lhsT, rhs, start=True, stop=True):
        return self.op("pe", lambda e: e.matmul(out.ap, lhsT=lhsT.ap, rhs=rhs.ap, start=start, stop=stop),
                       reads=[lhsT, rhs], writes=[out])

    def tr(self, out, in_, ident):
        return self.op("pe", lambda e: e.transpose(out=out.ap, in_=in_.ap, identity=ident.ap),
                       reads=[in_, ident], writes=[out])

    def act(self, out, in_, func, bias=None, scale=1.0, accum=None, eng="act"):
        reads = [in_]
        kw = {}
        if bias is not None:
            if isinstance(bias, V):
                reads.append(bias)
                kw["bias"] = bias.ap
            else:
                kw["bias"] = bias
        if isinstance(scale, V):
            reads.append(scale)
            kw["scale"] = scale.ap
        else:
            kw["scale"] = scale
        writes = [out]
        if accum is not None:
            kw["accum_out"] = accum.ap
            writes.append(accum)
        return self.op(eng, lambda e: e.activation(out=out.ap, in_=in_.ap, func=func, **kw), reads=reads, writes=writes)

    def tt(self, out, a, b, op, eng="dve"):
        return self.op(eng, lambda e: e.tensor_tensor(out=out.ap, in0=a.ap, in1=b.ap, op=op), reads=[a, b], writes=[out])

    def ts(self, out, a, s1, op0, s2=None, op1=None, eng="dve"):
        reads = [a]
        s1a = s1.ap if isinstance(s1, V) else s1
        s2a = s2.ap if isinstance(s2, V) else s2
        if isinstance(s1, V):
            reads.append(s1)
        if isinstance(s2, V):
            reads.append(s2)
        if op1 is None:
            return self.op(eng, lambda e: e.tensor_scalar(out=out.ap, in0=a.ap, scalar1=s1a, scalar2=None, op0=op0),
                           reads=reads, writes=[out])
        return self.op(eng, lambda e: e.tensor_scalar(out=out.ap, in0=a.ap, scalar1=s1a, scalar2=s2a, op0=op0, op1=op1),
                       reads=reads, writes=[out])

    def stt(self, out, a, s, b, op0, op1, eng="dve"):
        reads = [a, b]
        sa = s.ap if isinstance(s, V) else s
        if isinstance(s, V):
            reads.append(s)
        return self.op(eng, lambda e: e.scalar_tensor_tensor(out=out.ap, in0=a.ap, scalar=sa, in1=b.ap, op0=op0, op1=op1),
                       reads=reads, writes=[out])

    def cp(self, out, in_, eng="dve"):
        if eng == "act":
            return self.op("act", lambda e: e.copy(out=out.ap, in_=in_.ap), reads=[in_], writes=[out])
        return self.op(eng, lambda e: e.tensor_copy(out=out.ap, in_=in_.ap), reads=[in_], writes=[out])

    def red(self, out, in_, op=ALU.add, eng="dve"):
        return self.op(eng, lambda e: e.tensor_reduce(out=out.ap, in_=in_.ap, axis=AX.X, op=op), reads=[in_], writes=[out])

    def memset(self, out, val, eng="pool"):
        return self.op(eng, lambda e: e.memset(out.ap, val), writes=[out])

    def asel(self, out, in_, pattern, cmp, fill, base, cm):
        return self.op("pool", lambda e: e.affine_select(out=out.ap, in_=in_.ap, pattern=pattern, compare_op=cmp,
                                                         fill=fill, base=base, channel_multiplier=cm),
                       reads=[in_], writes=[out])

    def recip(self, out, in_):
        return self.op("dve", lambda e: e.reciprocal(out=out.ap, in_=in_.ap), reads=[in_], writes=[out])


class KB:
    def __init__(self, nt_prompt=32, n_dec=4, layers=2, npool=2560):
        self.NPOOL = npool
        self.NT = nt_prompt
        self.NDEC = n_dec
        self.LAYERS = layers
        self.nc = bass.Bass("TRN2", target_bir_lowering=False)
        self.st = ExitStack()
        self.out_views = []

    def sb(self, name, shape, dt=F32):
        t = self.st.enter_context(self.nc.sbuf_tensor(name, list(shape), dt))
        return V(t[:], Buf())

    def ps(self, name, shape, dt=F32):
        t = self.st.enter_context(self.nc.psum_tensor(name, list(shape), dt))
        return V(t[:], Buf(excl=True))

    def din(self, name, shape, dt=F32):
        return self.nc.dram_tensor(name, list(shape), dt, kind="ExternalInput").ap()

    def dout(self, name, shape, dt=F32):
        return self.nc.dram_tensor(name, list(shape), dt, kind="ExternalOutput").ap()

    def dint(self, name, shape, dt=F32):
        return self.nc.dram_tensor(name, list(shape), dt, kind="Internal").ap()

    def build(self):
        nc = self.nc
        NT = self.NT
        with self.st:
            em = self.em = Em(nc, self.st)
            self.declare_io()
            self.setup_consts()
            self.alloc_prompt_bufs()
            for layer in range(self.LAYERS):
                self.load_layer_weights(layer)
                if NT > 0:
                    self.prompt_layer(layer)
                if self.NDEC > 0:
                    self.dec_layer(layer)
            em._deps("sp", [self.out_tok], [])
            for k in list(em.cnt):
                if k.startswith("d_") and em.cnt[k] > 0:
                    nc.sync.wait_ge(em.sem[k], em.cnt[k])
        return nc

    def declare_io(self):
        T = self.NT * 128
        self.T = T
        d = self
        self.x_prompt = d.din("x_prompt", [SEQ, D])
        self.w_in = d.din("w_in", [DEPTH, D, D_IN])
        self.w_out = d.din("w_out", [DEPTH, D, D])
        self.norm_g = d.din("norm_g", [DEPTH, D])
        self.gla_w_gate = d.din("gla_w_gate", [DEPTH, 16, 128])
        self.gla_b_gate = d.din("gla_b_gate", [DEPTH, 128])
        self.gla_norm_g = d.din("gla_norm_g", [DEPTH, 64])
        self.nsa_q_norm_g = d.din("nsa_q_norm_g", [DEPTH, 64])
        self.nsa_k_norm_g = d.din("nsa_k_norm_g", [DEPTH, 3, 64])
        self.nsa_cmp_pos = d.din("nsa_cmp_pos", [DEPTH, 2, 32, 64])
        self.nsa_cmp_w = d.din("nsa_cmp_w", [DEPTH, 2, 32, 64, 64])
        self.ml_conv_w = d.din("ml_conv_w", [DEPTH, 4, 512])
        self.ml_conv_b = d.din("ml_conv_b", [DEPTH, 512])
        self.ml_gate_b = d.din("ml_gate_b", [DEPTH, 8])
        self.ml_norm_g = d.din("ml_norm_g", [DEPTH, 64])
        self.c_rope = d.din("c_rope", [SEQ, 16])
        self.c_ovl = d.din("c_ovl", [256, 64])
        self.c_selb = d.din("c_selb", [32, 128, 64])
        self.y_prompt = d.dout("y_prompt", [SEQ, D])
        self.p_rows = d.dout("p_rows", [DEPTH, SEQ, 512])
        self.p_win = d.dout("p_win", [DEPTH, 512, 256])
        self.p_gla = d.dout("p_gla", [DEPTH, 128, 64])
        self.p_c = d.dout("p_c", [DEPTH, 4, 64, 64])
        self.p_n = d.dout("p_n", [DEPTH, 4, 64])
        self.p_m = d.dout("p_m", [DEPTH, 4])
        self.p_conv = d.dout("p_conv", [DEPTH, 3, 512])
        if self.NDEC > 0:
            NPOOL = self.NPOOL
            self.x_sample = d.din("x_sample", [4, 8, D])
            self.cache = [d.din("cache_nsa_kv%d" % l_, [NPOOL * 128 * 2, 256]) for l_ in range(DEPTH)]
            self.st_win = d.din("state_nsa_win", [DEPTH, 4, 512, 256])
            self.st_gla = d.din("state_gla", [DEPTH, 4, 128, 64])
            self.st_c = d.din("state_mlstm_C", [DEPTH, 4, 4, 64, 64])
            self.st_n = d.din("state_mlstm_n", [DEPTH, 4, 4, 64])
            self.st_m = d.din("state_mlstm_m", [DEPTH, 4, 4])
            self.st_conv = d.din("state_mlstm_conv", [DEPTH, 4, 3, 512])
            self.page_table = d.din("page_table", [4, 64], I32)
            self.c_rope_d = d.din("c_rope_d", [8, 16])
            self.c_ovl_d = d.din("c_ovl_d", [512, 130])
            self.c_selb_d = d.din("c_selb_d", [8, 136])
            self.y_sample = d.dout("y_sample", [4, 8, D])
            self.s_rows = d.dout("s_rows", [DEPTH, 4, 8, 512])
            self.s_win = d.dout("s_win", [DEPTH, 4, 512, 256])
            self.s_gla = d.dout("s_gla", [DEPTH, 4, 128, 64])
            self.s_c = d.dout("s_c", [DEPTH, 4, 4, 64, 64])
            self.s_n = d.dout("s_n", [DEPTH, 4, 4, 64])
            self.s_m = d.dout("s_m", [DEPTH, 4, 4])
            self.s_conv = d.dout("s_conv", [DEPTH, 4, 3, 512])
            self.yd1 = d.dint("yd1_scratch", [4, 8, D])
        self.y1 = d.dint("y1_scratch", [SEQ, D])
        self.out_tok = V(None, Buf())

    def setup_consts(self):
        em = self.em
        sb = self.sb
        c = self.c = {}
        stage = self.stage = sb("w_stage", [128, 1024])

        c["ident_f"] = sb("ident_f", [128, 128])
        em.memset(c["ident_f"], 0.0)
        em.asel(c["ident_f"], c["ident_f"], [[-1, 128]], ALU.not_equal, 1.0, 0, 1)
        c["ident_b"] = sb("ident_b", [128, 128], BF16)
        em.cp(c["ident_b"], c["ident_f"])
        c["U"] = sb("U", [128, 128])
        em.memset(c["U"], 1.0)
        em.asel(c["U"], c["U"], [[1, 128]], ALU.is_ge, 0.0, 0, -1)
        c["ones"] = sb("ones", [128, 128])
        em.memset(c["ones"], 1.0)
        c["negm"] = sb("negm", [128, 128])
        em.memset(c["negm"], 0.0)
        em.asel(c["negm"], c["negm"], [[-1, 128]], ALU.is_ge, NEG, 0, 1)
        tmpf = stage[:, 0:128]
        tmpf2 = stage[:, 128:256]
        em.memset(tmpf, 0.0)
        em.asel(tmpf, tmpf, [[1, 128]], ALU.is_ge, NEG, 0, -1)
        c["causal4"] = sb("causal4", [128, 4, 128], BF16)
        for h in range(4):
            em.cp(c["causal4"][:, h, :], tmpf)
        em.memset(tmpf2, 0.0)
        em.asel(tmpf2, tmpf2, [[-1, 128]], ALU.is_ge, NEG, -1, 1)
        c["anti4"] = sb("anti4", [128, 4, 128], BF16)
        for h in range(4):
            em.cp(c["anti4"][:, h, :], tmpf2)
        c["hmask"] = sb("hmask", [128, 4])
        em.memset(c["hmask"], 1.0)
        em.asel(c["hmask"], c["hmask"], [[-32, 4]], ALU.is_ge, 0.0, 0, 1)
        em.asel(c["hmask"], c["hmask"], [[32, 4]], ALU.is_ge, 0.0, 31, -1)
        c["pmask"] = sb("pmask", [128, 2])
        em.tt(c["pmask"][:, 0:1], c["hmask"][:, 0:1], c["hmask"][:, 2:3], ALU.add)
        em.tt(c["pmask"][:, 1:2], c["hmask"][:, 1:2], c["hmask"][:, 3:4], ALU.add)
        c["pmask2"] = sb("pmask2", [128, 2])
        em.tt(c["pmask2"][:, 0:1], c["hmask"][:, 0:1], c["hmask"][:, 1:2], ALU.add)
        em.tt(c["pmask2"][:, 1:2], c["hmask"][:, 2:3], c["hmask"][:, 3:4], ALU.add)
        c["ident4"] = sb("ident4", [128, 4, 128], BF16)
        for h in range(4):
            em.cp(c["ident4"][:, h, :], c["ident_f"])
        ii = self.iota_i32 = sb("iota_i32", [128, 128], I32)
        em.op("pool", lambda e: e.iota(ii.ap, pattern=[[1, 128]], base=0, channel_multiplier=0), writes=[ii])
        c["iota_i"] = sb("iota_i", [128, 128])
        em.cp(c["iota_i"], ii)
        pc_ = sb("pcol_i32", [128, 1], I32)
        em.op("pool", lambda e: e.iota(pc_.ap, pattern=[[0, 1]], base=31, channel_multiplier=16), writes=[pc_])
        c["pcol"] = sb("pcol", [128, 1])
        em.cp(c["pcol"], pc_)
        c["ovl"] = sb("ovl", [128, 2, 64])
        em.dma("sp", c["ovl"], self.c_ovl.rearrange("(t p) j -> p t j", p=128), serial=True)
        c["ovl_b"] = sb("ovl_b", [128, 2, 64], BF16)
        em.cp(c["ovl_b"], c["ovl"])
        c["rope"] = sb("rope", [128, 16])
        self.pA = self.ps("pA", [128, 512])
        self.pB = self.ps("pB", [128, 512])
        self.pT0 = self.ps("pT0", [128, 1024], BF16)
        self.pT1 = self.ps("pT1", [128, 1024], BF16)
        self.pS0 = self.ps("pS0", [128, 512])
        self.pS1 = self.ps("pS1", [128, 512])
        self.pO0 = self.ps("pO0", [128, 512])
        self.pO1 = self.ps("pO1", [128, 512])
        self._pab = 0
        self._pt = 0
        self._psx = 0
        self._po = 0
        w = self.w = {}
        w["win"] = sb("w_in_b", [128, 8, D_IN], BF16)
        w["wout"] = sb("w_out_b", [128, 8, D], BF16)
        w["stage"] = self.stage
        w["normg"] = sb("normg_bc", [128, D])
        w["wgate"] = sb("wgate_b", [16, 128], BF16)
        w["bgate"] = sb("bgate_bc", [128, 128])
        w["glag"] = sb("glag_bc", [128, 64])
        w["qg"] = sb("qg_bc", [128, 64])
        w["kg"] = sb("kg_bc", [128, 3, 64])
        w["mlg"] = sb("mlg_bc", [128, 64])
        w["gateb"] = sb("gateb_bc", [128, 8])
        w["convw"] = sb("convw_col", [128, 4, 4])
        w["convb"] = sb("convb_col", [128, 4])
        w["cmpw"] = sb("cmpw_blk", [128, 2, 32, 128], BF16)
        w["cmpb_tm"] = sb("cmpb_tm", [8, 2, 128])

    def nextAB(self):
        self._pab ^= 1
        return self.pA if self._pab else self.pB

    def nextT(self):
        self._pt ^= 1
        return self.pT0 if self._pt else self.pT1

    def nextS(self):
        self._psx ^= 1
        return self.pS0 if self._psx else self.pS1

    def nextO(self):
        self._po ^= 1
        return self.pO0 if self._po else self.pO1

    def load_layer_weights(self, L):
        em, w, c = self.em, self.w, self.c
        win_v = self.w_in[L].rearrange("(k p) n -> k p n", p=128)
        for k in range(8):
            for c0 in range(0, D_IN, 1024):
                cw = min(1024, D_IN - c0)
                em.dma("sp" if k % 2 == 0 else "act", w["stage"][:, 0:cw], win_v[k, :, c0:c0 + cw], lane="wst", serial=True)
                em.cp(w["win"][:, k, c0:c0 + cw], w["stage"][:, 0:cw], eng=("dve" if k % 2 == 0 else "pool"))
        wout_v = self.w_out[L].rearrange("(k p) n -> k p n", p=128)
        for k in range(8):
            em.dma("sp" if k % 2 == 0 else "act", w["stage"][:, 0:D], wout_v[k], lane="wst", serial=True)
            em.cp(w["wout"][:, k, :], w["stage"][:, 0:D], eng=("dve" if k % 2 == 0 else "pool"))
        em.dma("sp", w["normg"], self.norm_g[L:L + 1, :].to_broadcast([128, D]), serial=True)
        em.dma("sp", w["bgate"], self.gla_b_gate[L:L + 1, :].to_broadcast([128, 128]), serial=True)
        em.dma("sp", w["glag"], self.gla_norm_g[L:L + 1, :].to_broadcast([128, 64]), serial=True)
        em.dma("sp", w["qg"], self.nsa_q_norm_g[L:L + 1, :].to_broadcast([128, 64]), serial=True)
        em.dma("sp", w["kg"], self.nsa_k_norm_g[L:L + 1].to_broadcast([128, 3, 64]), serial=True)
        em.dma("sp", w["mlg"], self.ml_norm_g[L:L + 1, :].to_broadcast([128, 64]), serial=True)
        em.dma("sp", w["gateb"], self.ml_gate_b[L:L + 1, :].to_broadcast([128, 8]), serial=True)
        em.dma("sp", w["stage"][0:16, 0:128], self.gla_w_gate[L], serial=True)
        em.cp(w["wgate"], w["stage"][0:16, 0:128])
        with self.nc.allow_non_contiguous_dma("tiny per-feature column loads"):
            for tap in range(4):
                em.dma("sp", w["convw"][:, :, tap], self.ml_conv_w[L, tap].rearrange("(c p) -> p c", p=128), serial=True)
            em.dma("sp", w["convb"], self.ml_conv_b[L].rearrange("(c p) -> p c", p=128), serial=True)
        em.memset(w["cmpw"], 0.0)
        for kv in range(2):
            for g in range(2):
                for sh in range(2):
                    stg = w["stage"][g * 64:(g + 1) * 64, 0:1024].re("p (s e) -> p s e", e=64)
                    em.dma("sp", stg, self.nsa_cmp_w[L, kv, sh * 16:(sh + 1) * 16].rearrange("s d e -> d s e"), serial=True)
                    em.cp(w["cmpw"][g * 64:(g + 1) * 64, kv, sh * 16:(sh + 1) * 16, g * 64:(g + 1) * 64], stg)
        with self.nc.allow_non_contiguous_dma("tiny transposed pos-emb load"):
            for g in range(2):
                for kk in range(2):
                    em.dma("sp", w["stage"][g * 64:(g + 1) * 64, kk * 32:(kk + 1) * 32],
                           self.nsa_cmp_pos[L, kk].rearrange("s d -> d s"), serial=True)
        for r in range(8):
            em.cp(w["pe_rep"][:, :, :, r], w["stage"][:, 0:64].re("p (k s) -> p k s", k=2))
        for kv in range(2):
            pp = self.nextAB()
            for s in range(32):
                em.mm(pp[0:8, 0:128], w["pe_rep"][:, kv, s, :], w["cmpw"][:, kv, s, :], start=(s == 0), stop=(s == 31))
            em.cp(w["cmpb_tm"][:, kv, :], pp[0:8, 0:128])

    def prompt_layer(self, L):
        em, w, c, sb = self.em, self.w, self.c, self.sb
        NT = self.NT
        P = 128
        last = (L == self.LAYERS - 1)
        b = self.pb
        em.memset(b["S"], 0.0)
        em.memset(b["S_b"], 0.0)
        em.memset(b["Ct"], 0.0)
        em.memset(b["Ct_b"], 0.0)
        em.memset(b["Mrep"], 0.0)
        em.memset(b["qkT"], 0.0)
        em.memset(b["rawT"], 0.0)
        em.memset(b["KcT"], 0.0)
        em.memset(b["VcT"], 0.0)
        em.memset(b["first_prev"], 0.0)
        if L > 0:
            if self.NDEC > 0:
                em.memset(self.dec_allk, 0.0)
                em.memset(self.dec_allv, 0.0)
                em.memset(b["Vslc"][:, :, :, 64:65].wb(self.dec_allv.bs), 1.0)
            else:
                for k_ in ("KslcT", "Vslc"):
                    vv = b[k_].wb(list(b[k_].bs) + self.hb[k_])
                    em.memset(vv, 0.0)
                em.memset(b["Vslc"][:, :, :, 64:65].wb(list(b["Vslc"].bs) + self.hb["Vslc"]), 1.0)
        if not hasattr(self, "y1_v"):
            self.y1_v = [V(self.y1[t_ * 128:(t_ + 1) * 128, :], Buf()) for t_ in range(32)]
        x_src = self.x_prompt if L == 0 else self.y1_v
        y_dst = self.y_prompt if last else self.y1_v
        for t in range(NT):
            self.tile_step(L, t, x_src, y_dst)
        self.write_states(L)

    def alloc_prompt_bufs(self):
        sb = self.sb
        b = self.pb = {}
        b["x"] = sb("x_t", [128, D])
        b["xn"] = sb("xn_b", [128, D], BF16)
        self.w["pe_rep"] = b["xn"][:, 0:512].re("p (k s r) -> p k s r", k=2, s=32)
        b["xnT"] = sb("xnT", [128, 8, 128], BF16)
        b["u"] = sb("u_t", [128, D_IN])
        b["st1"] = sb("stat1", [128, 64])
        b["S"] = sb("gla_S", [128, 64])
        b["S_b"] = sb("gla_S_b", [128, 64], BF16)
        b["ga_b"] = sb("ga_b", [128, 16], BF16)
        b["gaT"] = sb("gaT", [16, 128], BF16)
        b["g1"] = sb("gla_g1", [128, 128])
        b["loga"] = sb("gla_loga", [128, 128])
        b["eb"] = sb("gla_eb", [128, 128])
        b["enb"] = sb("gla_enb", [128, 128])
        b["ekb"] = sb("gla_ekb", [128, 128])
        b["edec"] = sb("gla_edec", [128, 1])
        b["qt_"] = sb("gla_qt", [128, 128], BF16)
        b["kt_"] = sb("gla_kt", [128, 128], BF16)
        b["kh_"] = sb("gla_kh", [128, 128], BF16)
        b["qT_"] = sb("gla_qT", [128, 4, 128], BF16)
        b["kT_"] = sb("gla_kT", [128, 128], BF16)
        b["gv"] = sb("gla_v", [128, 256], BF16)
        b["At"] = sb("gla_At", [128, 4, 128], BF16)
        b["dSc"] = sb("gla_dSc", [128, 64])
        b["go"] = sb("gla_o", [128, 4, 64])
        b["mixed"] = sb("mixed", [128, D], BF16)
        b["mixT"] = b["xnT"]
        b["nrm_sq"] = sb("nrm_sq", [128, 512])
        b["nrm_ss"] = sb("nrm_ss", [128, 8])
        b["silu"] = sb("silu_t", [128, 512])
        b["qkT"] = sb("ml_qkT", [128, 4, 3 + 128])
        b["conv"] = b["silu"].re("p (c i) -> p c i", c=4)
        b["qkc"] = sb("ml_qkc", [128, 4, 128], BF16)
        b["qm"] = sb("ml_qm", [128, 4, 128], BF16)
        b["k_tm"] = sb("ml_k_tm", [128, 256], BF16)
        b["vext"] = sb("ml_vext", [128, 4, 65], BF16)
        b["vw"] = sb("ml_vw", [128, 4, 65], BF16)
        b["gates"] = sb("ml_gates", [128, 8])
        b["fi"] = sb("ml_fi", [128, 4])
        b["F"] = sb("ml_F", [128, 4])
        b["gj"] = sb("ml_g", [128, 4])
        b["diag"] = None
        b["tmp"] = sb("ml_tmp", [128, 4, 128])
        b["dSm"] = b["tmp"].re("p c i -> p (c i)")[:, 0:256].re("p (h e) -> p h e", h=4)
        b["Dm"] = b["tmp"]
        b["sij"] = sb("ml_sij", [128, 4, 128], BF16)
        b["sijT"] = sb("ml_sijT", [128, 4, 128], BF16)
        b["mx"] = sb("ml_mx", [128, 4])
        b["nmx"] = sb("ml_nmx", [128, 4])
        b["m"] = sb("ml_m", [128, 8])
        b["wint"] = sb("ml_wint", [128, 4])
        b["Mrep"] = sb("ml_Mrep", [128, 4])
        b["lastrep"] = sb("ml_lastrep", [128, 8])
        b["decay"] = sb("ml_decay", [128, 4])
        b["wj"] = sb("ml_wj", [128, 4])
        b["Ct"] = sb("ml_Ct", [128, 4, 65])
        b["Ct_b"] = sb("ml_Ct_b", [128, 4, 65], BF16)
        b["num"] = sb("ml_num", [128, 4, 65])
        b["den"] = sb("ml_den", [128, 4])
        b["den2"] = sb("ml_den2", [128, 4])
        b["hh"] = sb("ml_h", [128, 4, 64])
        b["sel_last"] = sb("sel_last", [128, 128])
        em = self.em
        em.memset(b["sel_last"], 0.0)
        em.asel(b["sel_last"], b["sel_last"], [[0, 128]], ALU.not_equal, 1.0, -127, 1)
        b["sel_last8"] = self.sb("sel_last8", [8, 128])
        em.memset(b["sel_last8"], 0.0)
        em.asel(b["sel_last8"], b["sel_last8"], [[0, 128]], ALU.not_equal, 1.0, -7, 1)
        b["qn"] = b["silu"].re("p (h e) -> p h e", h=8)
        b["qr"] = sb("nsa_qr", [128, 8, 64], BF16)
        b["qn_b"] = sb("nsa_qn_b", [128, 8, 64], BF16)
        b["rt1"] = sb("rope_t1", [128, 8, 8])
        b["rt2"] = sb("rope_t2", [128, 8, 8])
        b["rows"] = sb("nsa_rows", [128, 4, 128])
        b["win"] = sb("nsa_win", [128, 2, 128])
        b["kk_b"] = sb("nsa_kk_b", [128, 4, 128], BF16)
        b["QTr"] = sb("nsa_QTr", [128, 2, 4, 128], BF16)
        b["QTn"] = sb("nsa_QTn", [128, 2, 4, 128], BF16)
        b["KslcT"] = sb("KslcT", [128, SEQ], BF16)
        b["KwinT"] = sb("KwinT", [128, 5 * 128], BF16)
        b["Vslc"] = sb("Vslc", [128, 32, 2, 65], BF16)
        b["Vwin"] = sb("Vwin", [128, 5, 2, 65], BF16)
        b["rawT"] = sb("rawT", [128, 2, 16 + 128], BF16)
        b["KcT"] = sb("KcT", [128, 256], BF16)
        b["VcT"] = sb("VcT", [128, 256], BF16)
        b["Vc"] = sb("Vc", [128, 2, 2, 129], BF16)
        b["kc_tm"] = sb("kc_tm", [8, 2, 128])
        b["kc_n"] = sb("kc_n", [8, 2, 64], BF16)
        b["vc_b"] = sb("vc_b", [8, 128], BF16)
        b["first_prev"] = sb("first_prev", [8, 1])
        b["cmpmask"] = sb("cmpmask", [128, 128])
        b["cmpmask4"] = sb("cmpmask4", [128, 4, 128], BF16)
        b["PT"] = sb("PT", [128, 4, 128], BF16)
        b["PT2"] = sb("PT2", [128, 4, 128], BF16)
        b["ocmp"] = b["tmp"].re("p c i -> p (c i)").re("p (h e) -> p h e", h=8)
        b["rden"] = sb("rden", [128, 8])
        b["imp"] = sb("imp", [128, 2, 64])
        b["impt"] = sb("impt", [128, 4, 64])
        b["selb"] = sb("selb", [128, 64])
        b["score"] = sb("score", [128, 2, 64])
        b["score2"] = sb("score2", [128, 64])
        b["max8"] = sb("max8", [128, 8])
        b["selbias"] = sb("selbias", [128, 2, 64], BF16)
        b["gate"] = sb("nsa_gate", [128, 24])
        b["selx"] = sb("selx", [128, 2, 2, 64], BF16)
        b["ob"] = sb("nsa_ob", [128, 8, 64])
        b["diag"] = b["ob"].re("p h e -> p (h e)").re("p (c i) -> p c i", c=4)
        self.hb = {k: [Buf() for _ in range(33)] for k in ("KslcT", "KwinT", "Vslc", "Vwin")}
        def allb(k):
            return b[k].wb(list(b[k].bs) + self.hb[k])
        em.memset(allb("Vslc"), 0.0)
        em.memset(allb("Vwin"), 0.0)
        em.memset(allb("Vslc")[:, :, :, 64:65], 1.0)
        em.memset(allb("Vwin")[:, :, :, 64:65], 1.0)
        em.memset(allb("KslcT"), 0.0)
        em.memset(allb("KwinT"), 0.0)
        em.memset(b["Vc"], 0.0)
        em.memset(b["Vc"][:, :, :, 64:65], 1.0)
        for g in range(2):
            em.cp(b["Vc"][:, :, g, 65:129], self.c["ovl_b"])
        em.memset(b["vext"][:, :, 64:65], 1.0)
        em.memset(b["mixed"], 0.0)

    def rms_heads(self, out, in_, H, gain_bc, P=128, extra_scale=1.0):
        em, b = self.em, self.pb
        sq = b["nrm_sq"][0:P, 0:H * 64].re("p (h e) -> p h e", e=64)
        ss = b["nrm_ss"][0:P, 0:H]
        em.tt(sq, in_, in_, ALU.mult)
        em.red(ss, sq)
        em.ts(ss, ss, 1.0 / 64, ALU.mult, EPS, ALU.add)
        em.act(ss, ss, AF.Sqrt)
        em.recip(ss, ss)
        em.tt(sq, in_, ss.un(2).bc([P, H, 64]), ALU.mult)
        if extra_scale != 1.0:
            em.stt(out, sq, extra_scale, gain_bc.un(1).bc([P, H, 64]), ALU.mult, ALU.mult)
        else:
            em.tt(out, sq, gain_bc.un(1).bc([P, H, 64]), ALU.mult)

    def rope(self, out, in_, H, cs, P=128):
        em, b = self.em, self.pb
        cos = cs[:, 0:8].un(1).bc([P, H, 8])
        sin = cs[:, 8:16].un(1).bc([P, H, 8])
        t1 = b["rt1"][0:P, 0:H, :]
        t2 = b["rt2"][0:P, 0:H, :]
        x1 = in_[:, :, 0:8]
        x2 = in_[:, :, 8:16]
        em.cp(out[:, :, 16:64], in_[:, :, 16:64], eng="pool")
        em.tt(t1, x1, cos, ALU.mult)
        em.tt(t2, x2, sin, ALU.mult)
        em.tt(out[:, :, 0:8], t1, t2, ALU.subtract)
        em.tt(t1, x2, cos, ALU.mult)
        em.tt(t2, x1, sin, ALU.mult)
        em.tt(out[:, :, 8:16], t1, t2, ALU.add)

    def out_dma(self, q, dst, src):
        self.em.dma(q, dst, src, lane="out_" + q)
        pass

    def tile_step(self, L, t, x_src, y_dst):
        em, w, c, b = self.em, self.w, self.c, self.pb
        P = 128
        r0 = t * 128
        last_tile = (t == self.NT - 1)
        x, u = b["x"], b["u"]
        em.dma("sp", x, x_src[r0:r0 + 128, :] if not isinstance(x_src, list) else x_src[t], lane="x")
        ss = b["st1"][:, 0:1]
        em.act(u[:, 0:D], x, AF.Square, accum=ss)
        em.ts(ss, ss, 1.0 / D, ALU.mult, EPS, ALU.add)
        em.act(ss, ss, AF.Sqrt)
        em.recip(ss, ss)
        em.stt(b["xn"], x, ss, w["normg"], ALU.mult, ALU.mult)
        for k in range(8):
            pt = self.nextT()
            em.tr(pt[:, 0:128], b["xn"][:, k * 128:(k + 1) * 128], c["ident_b"])
            em.cp(b["xnT"][:, k, :], pt[:, 0:128], eng=("act" if k % 2 else "dve"))
        chunks = [(0, 512), (512, 512), (1024, 512), (1536, 512), (2048, 512), (2560, 40), (3112, 512), (3624, 264)]
        for ci, (c0, cw) in enumerate(chunks):
            pp = self.nextAB()
            for k in range(8):
                em.mm(pp[:, 0:cw], b["xnT"][:, k, :], w["win"][:, k, c0:c0 + cw], start=(k == 0), stop=(k == 7))
            em.cp(u[:, c0:c0 + cw], pp[:, 0:cw], eng=("act" if ci % 2 else "dve"))
        em.cp(b["qkT"][:, :, 0:3], b["qkT"][:, :, 128:131], eng="pool")
        for cch in range(4):
            pp = self.nextAB()
            for k in range(8):
                em.mm(pp[:, 0:128], w["win"][:, k, O_MQK + cch * 128:O_MQK + (cch + 1) * 128], b["xnT"][:, k, :],
                      start=(k == 0), stop=(k == 7))
            em.cp(b["qkT"][:, cch, 3:131], pp[:, 0:128], eng=("act" if cch % 2 else "dve"))
        import os as _os
        if not _os.environ.get("SKIP_GLA"):
            self.gla_tile(L, P)
        if not _os.environ.get("SKIP_ML"):
            self.mlstm_tile(L, P)
        if not _os.environ.get("SKIP_NSA"):
            self.nsa_tile(L, t)
        for k in range(8):
            pt = self.nextT()
            em.tr(pt[:, 0:128], b["mixed"][:, k * 128:(k + 1) * 128], c["ident_b"])
            em.cp(b["mixT"][:, k, :], pt[:, 0:128], eng=("act" if k % 2 else "dve"))
        for half in range(2):
            pp = self.nextAB()
            for k in range(8):
                em.mm(pp[:, 0:512], b["mixT"][:, k, :], w["wout"][:, k, half * 512:(half + 1) * 512],
                      start=(k == 0), stop=(k == 7))
            em.tt(x[:, half * 512:(half + 1) * 512], pp[:, 0:512], x[:, half * 512:(half + 1) * 512], ALU.add)
        self.out_dma("sp", y_dst[r0:r0 + 128, :] if not isinstance(y_dst, list) else y_dst[t], x)
        if last_tile:
            with self.nc.allow_non_contiguous_dma("tiny conv state"):
                for r in range(3):
                    self.out_dma("sp", self.p_conv[L, r].rearrange("(c p) -> p c", p=128), b["qkT"][:, :, 128 + r])

    def gla_tile(self, L, P):
        em, w, c, b = self.em, self.w, self.c, self.pb
        u = b["u"]
        em.cp(b["ga_b"][0:P], u[0:P, O_GA:O_GA + 16])
        pt = self.nextT()
        em.tr(pt[0:16, 0:P], b["ga_b"][0:P], c["ident_b"][0:P, 0:P])
        em.cp(b["gaT"][:, 0:P], pt[0:16, 0:P])
        pp = self.nextAB()
        em.mm(pp[0:P, 0:128], b["gaT"][:, 0:P], w["wgate"])
        g1 = b["g1"][0:P]
        em.tt(g1, pp[0:P, 0:128], w["bgate"][0:P], ALU.add)
        em.act(g1, g1, AF.Exp, scale=-1.0)
        em.act(g1, g1, AF.Ln, bias=1.0)
        loga = b["loga"][0:P]
        em.ts(loga, g1, -1.0 / 16.0, ALU.mult)
        pp = self.nextAB()
        em.mm(pp[0:P, 0:128], c["U"][0:P, 0:P], loga)
        em.mm(pp[0:P, 128:256], c["ones"][0:P, 0:P], loga)
        em.mm(pp[:, 256:257], loga, c["ones"][0:P, 0:1])
        em.act(b["eb"][0:P], pp[0:P, 0:128], AF.Exp)
        em.act(b["enb"][0:P], pp[0:P, 0:128], AF.Exp, scale=-1.0)
        em.cp(g1, pp[0:P, 0:128], eng="act")
        em.tt(b["ekb"][0:P], pp[0:P, 128:256], g1, ALU.subtract)
        em.act(b["ekb"][0:P], b["ekb"][0:P], AF.Exp)
        em.act(b["edec"], pp[:, 256:257], AF.Exp)
        em.stt(b["qt_"][0:P], u[0:P, O_GQ:O_GQ + 128], 32 ** -0.5, b["eb"][0:P], ALU.mult, ALU.mult)
        em.tt(b["kt_"][0:P], u[0:P, O_GK:O_GK + 128], b["enb"][0:P], ALU.mult)
        em.tt(b["kh_"][0:P], u[0:P, O_GK:O_GK + 128], b["ekb"][0:P], ALU.mult)
        em.cp(b["gv"][0:P], u[0:P, O_GV:O_GV + 256], eng="pool")
        pt = self.nextT()
        em.tr(pt[:, 0:P], b["qt_"][0:P], c["ident_b"][0:P, 0:P])
        em.tr(pt[:, 128:128 + P], b["kt_"][0:P], c["ident_b"][0:P, 0:P])
        for h in range(4):
            em.ts(b["qT_"][:, h, 0:P], pt[:, 0:P], c["hmask"][:, h:h + 1], ALU.mult)
        em.cp(b["kT_"][:, 0:P], pt[:, 128:128 + P])
        pa = self.nextAB()
        pav = pa[0:P, 0:4 * P].re("p (h i) -> p h i", h=4)
        for h in range(4):
            em.mm(pav[:, h, :], b["kT_"][:, 0:P], b["qT_"][:, h, 0:P])
        At = b["At"][0:P, :, 0:P]
        em.tt(At, pav, c["U"][0:P, 0:P].un(1).bc([P, 4, P]), ALU.mult)
        po = self.nextAB()
        pov = po[0:P, 0:256].re("p (h e) -> p h e", h=4)
        for h in range(4):
            em.mm(pov[:, h, :], At[:, h, :], b["gv"][0:P, h * 64:(h + 1) * 64], start=True, stop=False)
            em.mm(pov[:, h, :], b["qT_"][:, h, 0:P], b["S_b"], start=False, stop=True)
        go = b["go"][0:P]
        em.cp(go, pov)
        pd = self.nextAB()
        em.mm(pd[:, 0:256], b["kh_"][0:P], b["gv"][0:P])
        em.tt(b["dSm"], pd[:, 0:256].re("p (h e) -> p h e", h=4), c["hmask"].un(2).bc([128, 4, 64]), ALU.mult)
        em.red(b["dSc"], b["dSm"].re("p h e -> p e h"))
        em.stt(b["S"], b["S"], b["edec"], b["dSc"], ALU.mult, ALU.add)
        em.cp(b["S_b"], b["S"], eng="act")
        mixv = b["mixed"][0:P, 0:256].re("p (h e) -> p h e", h=4)
        sl = b["silu"][0:P, 0:256]
        em.act(sl, u[0:P, O_GZ:O_GZ + 256], AF.Silu)
        self.rms_heads(go, go, 4, w["glag"][0:P], P)
        em.tt(mixv, go, sl.re("p (h e) -> p h e", h=4), ALU.mult)

    def mlstm_tile(self, L, P):
        em, w, c, b = self.em, self.w, self.c, self.pb
        u = b["u"]
        for cch in range(4):
            cv = b["conv"][:, cch, 0:P]
            em.ts(cv, b["qkT"][:, cch, 0:P], w["convw"][:, cch, 0:1], ALU.mult)
            for tap in range(1, 4):
                em.stt(cv, b["qkT"][:, cch, tap:tap + P], w["convw"][:, cch, tap:tap + 1], cv, ALU.mult, ALU.add)
            em.act(b["qkc"][:, cch, 0:P], cv, AF.Silu, bias=w["convb"][:, cch:cch + 1])
        for h in range(4):
            em.ts(b["qm"][:, h, 0:P], b["qkc"][:, h // 2, 0:P], c["pmask2"][:, h % 2:h % 2 + 1], ALU.mult)
        pt = self.nextT()
        for j in range(2):
            em.tr(pt[0:P, j * 128:(j + 1) * 128], b["qkc"][:, 2 + j, 0:P], c["ident_b"])
        em.ts(b["k_tm"][0:P], pt[0:P, 0:256], 0.125, ALU.mult)
        em.cp(b["vext"][0:P, :, 0:64], u[0:P, O_MV:O_MV + 256].re("p (h e) -> p h e", h=4), eng="pool")
        gt = b["gates"][0:P]
        em.tt(gt, u[0:P, O_MIF:O_MIF + 8], w["gateb"][0:P], ALU.add)
        fi = b["fi"][0:P]
        em.act(fi, gt[:, 4:8], AF.Exp, scale=-1.0)
        em.act(fi, fi, AF.Ln, bias=1.0)
        em.ts(fi, fi, -1.0, ALU.mult)
        pp = self.nextAB()
        em.mm(pp[0:P, 0:4], c["U"][0:P, 0:P], fi)
        Fm = b["m"][0:P]
        em.cp(Fm[:, 0:4], pp[0:P, 0:4])
        gj = b["gj"][0:P]
        em.tt(gj, gt[:, 0:4], Fm[:, 0:4], ALU.subtract)
        dg = b["diag"][0:P, :, 0:P]
        em.tt(dg, c["ident_f"][0:P, 0:P].un(1).bc([P, 4, P]), gj.un(2).bc([P, 4, P]), ALU.mult)
        pg = self.nextAB()
        pgv = pg[0:P, 0:4 * P].re("p (h j) -> p h j", h=4)
        em.mm(pgv, c["ones"][0:P, 0:P], dg)
        tmp = b["tmp"][0:P, :, 0:P]
        em.tt(tmp, pgv, c["negm"][0:P, 0:P].un(1).bc([P, 4, P]), ALU.add)
        mx = b["mx"][0:P]
        em.red(mx, tmp, op=ALU.max)
        em.tt(mx, mx, b["Mrep"][0:P], ALU.max)
        em.tt(Fm[:, 4:8], Fm[:, 0:4], mx, ALU.add)
        nmx = b["nmx"][0:P]
        em.ts(nmx, mx, -1.0, ALU.mult)
        Dm = b["Dm"][0:P, :, 0:P]
        for h in range(4):
            em.act(Dm[:, h, :], tmp[:, h, :], AF.Exp, bias=nmx[:, h:h + 1])
        wint = b["wint"][0:P]
        em.tt(wint, b["Mrep"][0:P], mx, ALU.subtract)
        em.act(wint, wint, AF.Exp)
        psc = self.nextS()
        pscv = psc[0:P, 0:4 * P].re("p (h j) -> p h j", h=4)
        for h in range(4):
            em.mm(pscv[:, h, :], b["qm"][:, h, 0:P], b["qkc"][:, 2 + h // 2, 0:P])
        sij = b["sij"][0:P, :, 0:P]
        em.stt(sij, pscv, 0.125, Dm, ALU.mult, ALU.mult)
        pt = self.nextT()
        ptv = pt[0:P, 0:4 * P].re("p (h i) -> p h i", h=4)
        for h in range(4):
            em.tr(ptv[:, h, :], sij[:, h, :], c["ident_b"][0:P, 0:P])
        sijT = b["sijT"][0:P, :, 0:P]
        em.cp(sijT, ptv)
        pn = self.nextO()
        pnv = pn[0:P, 0:260].re("p (h e) -> p h e", h=4)
        pi_ = self.nextO()
        piv = pi_[0:P, 0:260].re("p (h e) -> p h e", h=4)
        for h in range(4):
            em.mm(pnv[:, h, :], sijT[:, h, :], b["vext"][0:P, h, :])
            em.mm(piv[:, h, :], b["qm"][:, h, 0:P], b["Ct_b"][:, h, :])
        num = b["num"][0:P]
        em.tt(num, piv, wint.un(2).bc([P, 4, 65]), ALU.mult)
        em.tt(num, num, pnv, ALU.add)
        den = b["den"][0:P]
        em.stt(den, num[:, :, 64], -1.0, num[:, :, 64], ALU.mult, ALU.max)
        den2 = b["den2"][0:P]
        em.act(den2, Fm[:, 4:8], AF.Exp, scale=-1.0)
        em.tt(den, den, den2, ALU.max)
        em.recip(den, den)
        hh = b["hh"][0:P]
        em.tt(hh, num[:, :, 0:64], den.un(2).bc([P, 4, 64]), ALU.mult)
        pl = self.nextAB()
        em.mm(pl[0:128, 0:8], (b["sel_last"] if P == 128 else b["sel_last8"])[0:P, :], Fm)
        lr = b["lastrep"]
        em.cp(lr, pl[:, 0:8])
        dec = b["decay"]
        em.tt(dec, lr[:, 0:4], b["Mrep"], ALU.add)
        em.tt(dec, dec, lr[:, 4:8], ALU.subtract)
        em.act(dec, dec, AF.Exp)
        wj = b["wj"][0:P]
        em.tt(wj, gj, lr[0:P, 0:4], ALU.add)
        em.tt(wj, wj, lr[0:P, 4:8], ALU.subtract)
        em.act(wj, wj, AF.Exp)
        em.tt(b["vw"][0:P], b["vext"][0:P], wj.un(2).bc([P, 4, 65]), ALU.mult)
        pc = self.nextAB()
        pcv = pc[:, 0:260].re("p (h e) -> p h e", h=4)
        for h in range(4):
            em.mm(pcv[:, h, :], b["k_tm"][0:P, (h // 2) * 128:(h // 2 + 1) * 128], b["vw"][0:P, h, :])
        em.tt(b["Ct"], b["Ct"], dec.un(2).bc([128, 4, 65]), ALU.mult)
        em.tt(b["Ct"], b["Ct"], pcv, ALU.add)
        em.cp(b["Ct_b"], b["Ct"], eng="act")
        em.cp(b["Mrep"], lr[:, 4:8])
        self.rms_heads(hh, hh, 4, w["mlg"][0:P], P)
        sl = b["silu"][0:P, 0:256]
        em.act(sl, u[0:P, O_MO:O_MO + 256], AF.Sigmoid)
        em.tt(hh, hh, sl.re("p (h e) -> p h e", h=4), ALU.mult)
        em.act(sl, u[0:P, O_MZ:O_MZ + 256], AF.Silu)
        em.tt(b["mixed"][0:P, 768:1024].re("p (h e) -> p h e", h=4), hh, sl.re("p (h e) -> p h e", h=4), ALU.mult)

    def attn_block(self, KT, QT, extra, Vr, PTb, pacc, first, last_):
        em = self.em
        ps_ = self.nextS()
        nk = KT.ap.shape[-1]
        psv = ps_[0:nk, 0:512]
        em.mm(psv, KT, QT.re("p h i -> p (h i)"), start=True, stop=(len(extra) == 0))
        for ei, (lt, rh) in enumerate(extra):
            em.mm(psv, lt, rh, start=False, stop=(ei == len(extra) - 1))
        em.act(PTb[0:nk].re("p h i -> p (h i)"), psv, AF.Exp, scale=ATT_SCALE)
        em.mm(pacc[0:65, 0:512], Vr, PTb[0:nk].re("p h i -> p (h i)"), start=first, stop=last_)

    def attn_finish(self, pacc, g, gate_off, skip=False):
        em, c, b = self.em, self.c, self.pb
        ot = b["nrm_sq"][0:65, 0:512]
        em.cp(ot, pacc[0:65, 0:512])
        pp = self.nextAB()
        ppv = pp[:, 0:260].re("p (h e) -> p h e", h=4)
        for hh in range(4):
            em.tr(ppv[:, hh, :], ot[:, hh * 128:(hh + 1) * 128], c["ident_f"][0:65, 0:65])
        rd = b["rden"][:, 0:4]
        em.recip(rd, ppv[:, :, 64])
        t_ = b["ocmp"][:, 0:4, :]
        em.tt(t_, ppv[:, :, 0:64], rd.un(2).bc([128, 4, 64]), ALU.mult)
        em.tt(t_, t_, b["gate"][:, gate_off + g * 4:gate_off + g * 4 + 4].un(2).bc([128, 4, 64]), ALU.mult)
        if not skip:
            em.tt(b["ob"][:, g * 4:(g + 1) * 4, :], b["ob"][:, g * 4:(g + 1) * 4, :], t_, ALU.add)

    def nsa_tile(self, L, t):
        import os as _os
        em, w, c, b, hb = self.em, self.w, self.c, self.pb, self.hb
        u = b["u"]
        P = 128
        r0 = t * 128
        cs = c["rope"]
        em.dma("sp", cs, self.c_rope[r0:r0 + 128, :], lane="cst")
        qn = b["qn"]
        self.rms_heads(qn, u[:, O_NQ:O_NQ + 512].re("p (h e) -> p h e", h=8), 8, w["qg"])
        em.cp(b["qn_b"].re("p (hh g) e -> p g hh e", g=2), qn.re("p (g hh) e -> p g hh e", g=2), eng="pool")
        self.rope(b["ocmp"], qn, 8, cs)
        em.cp(b["qr"].re("p (hh g) e -> p g hh e", g=2), b["ocmp"].re("p (g hh) e -> p g hh e", g=2))
        kv = u[:, O_NKV:O_NKV + 768].re("p (s g e) -> p s g e", s=6, g=2)
        rows = b["rows"].re("p s (g e) -> p s g e", g=2)
        winr = b["win"].re("p s (g e) -> p s g e", g=2)
        em.cp(rows[:, 0], kv[:, 0], eng="pool")
        em.cp(rows[:, 1], kv[:, 1], eng="pool")
        em.cp(rows[:, 3], kv[:, 3], eng="pool")
        em.cp(winr[:, 1], kv[:, 5], eng="pool")
        ktmp = b["ocmp"][:, 0:2, :]
        self.rms_heads(ktmp, kv[:, 2], 2, w["kg"][:, 1, :])
        self.rope(rows[:, 2], ktmp, 2, cs)
        self.rms_heads(ktmp, kv[:, 4], 2, w["kg"][:, 2, :])
        self.rope(winr[:, 0], ktmp, 2, cs)
        self.out_dma("act", self.p_rows[L, r0:r0 + 128, :], b["rows"].re("p s f -> p (s f)"))
        if r0 >= SEQ - 512:
            w0 = r0 - (SEQ - 512)
            self.out_dma("act", self.p_win[L, w0:w0 + 128, :], b["win"].re("p s f -> p (s f)"))
        kkb = b["kk_b"]
        em.cp(kkb[:, 0:2, :], b["rows"][:, 0:2, :])
        em.cp(kkb[:, 2, :], b["rows"][:, 2, :])
        em.cp(kkb[:, 3, :], b["win"][:, 0, :])
        em.cp(b["Vslc"][:, t, :, 0:64].wb(hb["Vslc"][t]), rows[:, 3], eng="pool")
        em.cp(b["Vwin"][:, t % 5, :, 0:64].wb(hb["Vwin"][t % 5]), winr[:, 1], eng="pool")
        pt = self.nextT()
        for j in range(4):
            em.tr(pt[:, j * 128:(j + 1) * 128], kkb[:, j, :], c["ident_b"])
        em.cp(b["rawT"][:, :, 0:16], b["rawT"][:, :, 128:144], eng="pool")
        em.cp(b["rawT"][:, :, 16:144], pt[:, 0:256].re("p (k i) -> p k i", k=2))
        em.cp(b["KslcT"][:, r0:r0 + 128].wb(hb["KslcT"][t]), pt[:, 256:384], eng="act")
        em.cp(b["KwinT"][:, (t % 5) * 128:(t % 5 + 1) * 128].wb(hb["KwinT"][t % 5]), pt[:, 384:512], eng="act")
        pt = self.nextT()
        qr4 = b["qr"].re("p (hh g) e -> p hh (g e)", g=2)
        qn4 = b["qn_b"].re("p (hh g) e -> p hh (g e)", g=2)
        for hh in range(4):
            em.tr(pt[:, hh * 128:(hh + 1) * 128], qr4[:, hh, :], c["ident_b"])
            em.tr(pt[:, 512 + hh * 128:512 + (hh + 1) * 128], qn4[:, hh, :], c["ident_b"])
        for g in range(2):
            em.ts(b["QTr"][:, g].re("p h i -> p (h i)"), pt[:, 0:512], c["pmask2"][:, g:g + 1], ALU.mult)
            em.ts(b["QTn"][:, g].re("p h i -> p (h i)"), pt[:, 512:1024], c["pmask2"][:, g:g + 1], ALU.mult)
        for kvi in range(2):
            pp = self.nextAB()
            for s in range(32):
                em.mm(pp[0:8, 0:128], b["rawT"][:, kvi, s:s + 113:16], w["cmpw"][:, kvi, s, :], start=(s == 0), stop=(s == 31))
            em.tt(b["kc_tm"][:, kvi, :], pp[0:8, 0:128], w["cmpb_tm"][:, kvi, :], ALU.add)
        self.rms_heads(b["kc_n"], b["kc_tm"][:, 0, :].re("p (g e) -> p g e", g=2), 2, w["kg"][0:8, 0, :], P=8)
        em.cp(b["vc_b"], b["kc_tm"][:, 1, :])
        pt = self.nextT()
        em.tr(pt[:, 0:8], b["kc_n"].re("p g e -> p (g e)"), c["ident_b"][0:8, 0:8])
        em.tr(pt[:, 8:16], b["vc_b"], c["ident_b"][0:8, 0:8])
        n0 = 8 * t - 1
        if t == 0:
            em.cp(b["KcT"][:, 0:7], pt[:, 1:8])
            em.cp(b["VcT"][:, 0:7], pt[:, 9:16])
        else:
            em.cp(b["KcT"][:, n0:n0 + 8], pt[:, 0:8])
            em.cp(b["VcT"][:, n0:n0 + 8], pt[:, 8:16])
        nvis = 8 * t + 7
        ntl = 1 if nvis <= 128 else 2
        fr = ntl - 1
        for rt in sorted({fr} | ({0} if t == 16 else set())):
            pt = self.nextT()
            em.tr(pt[:, 0:128], b["VcT"][:, rt * 128:(rt + 1) * 128], c["ident_b"])
            em.cp(b["Vc"][:, rt, :, 0:64], pt[:, 0:128].re("p (g e) -> p g e", g=2))
        for nt_ in range(ntl):
            if 16 * (nt_ * 128 + 127) + 31 <= 128 * t:
                continue
            em.ts(b["cmpmask"], c["iota_i"], c["pcol"], ALU.subtract, float(2048 * nt_ - 128 * t), ALU.is_lt)
            em.ts(b["cmpmask"], b["cmpmask"], NEG, ALU.mult)
            for hh in range(2):
                em.cp(b["cmpmask4"][:, nt_ * 2 + hh, :], b["cmpmask"], eng=("dve" if hh % 2 else "pool"))
        em.act(b["gate"], u[:, O_NG:O_NG + 24], AF.Sigmoid)
        for g in range(2):
            for half in range(2):
                pov = b["nrm_sq"][:, 0:258].re("p (h e) -> p h e", h=2)
                for nt_ in range(ntl):
                    ps_ = self.nextS()
                    psv = ps_[:, 0:256]
                    qsl = b["QTn"][:, g, half * 2:half * 2 + 2, :].re("p h i -> p (h i)")
                    masked = not (16 * (nt_ * 128 + 127) + 31 <= 128 * t)
                    em.mm(psv, b["KcT"][:, nt_ * 128:(nt_ + 1) * 128], qsl, start=True, stop=not masked)
                    if masked:
                        em.mm(psv, c["ident_b"], b["cmpmask4"][:, nt_ * 2:nt_ * 2 + 2, :].re("p h i -> p (h i)"), start=False, stop=True)
                    PTb = b["PT"] if nt_ == 0 else b["PT2"]
                    em.act(PTb[:, 0:2, :].re("p h i -> p (h i)"), psv, AF.Exp, scale=ATT_SCALE)
                    po = self.nextO()
                    pcv = po[:, 0:258].re("p (h e) -> p h e", h=2)
                    for h2 in range(2):
                        em.mm(pcv[:, h2, :], PTb[:, h2, :], b["Vc"][:, nt_, g, :], start=True, stop=True)
                    if nt_ == 0:
                        em.cp(pov, pcv)
                    else:
                        em.tt(pov, pov, pcv, ALU.add)
                rd = b["rden"][:, 0:2]
                em.ts(rd, pov[:, :, 64], 1e-30, ALU.max)
                em.recip(rd, rd)
                h0 = g * 4 + half * 2
                em.tt(b["ocmp"][:, 0:2, :], pov[:, :, 0:64], rd.un(2).bc([128, 2, 64]), ALU.mult)
                em.tt(b["ob"][:, h0:h0 + 2, :], b["ocmp"][:, 0:2, :], b["gate"][:, h0:h0 + 2].un(2).bc([128, 2, 64]), ALU.mult)
                em.tt(b["impt"][:, half * 2:half * 2 + 2, :], pov[:, :, 65:129], rd.un(2).bc([128, 2, 64]), ALU.mult)
            em.red(b["imp"][:, g, :], b["impt"].re("p h j -> p j h"))
        em.dma("sp", b["selb"], self.c_selb[t], lane="cst2")
        for g in range(2):
            sc = b["score"][:, g, :]
            em.tt(sc, b["imp"][:, g, :], b["selb"], ALU.add)
            em.op("dve", lambda e: e.max(out=b["max8"].ap, in_=sc.ap), reads=[sc], writes=[b["max8"]])
            em.op("dve", lambda e: e.match_replace(out=b["score2"].ap, in_to_replace=b["max8"].ap, in_values=sc.ap,
                                                   imm_value=-3.0e38), reads=[sc, b["max8"]], writes=[b["score2"]])
            em.op("dve", lambda e: e.max(out=b["max8"].ap, in_=b["score2"].ap), reads=[b["score2"]], writes=[b["max8"]])
            em.ts(b["score2"], sc, b["max8"][:, 7:8], ALU.is_ge, -1.0, ALU.add)
            em.ts(b["selbias"][:, g, :], b["score2"], -NEG, ALU.mult)
        for g in range(2):
            QT = b["QTr"][:, g]
            for kt in range(t + 1):
                sx = b["selx"][:, kt % 2]
                em.cp(sx, b["selbias"][:, g, 2 * kt:2 * kt + 2].un(2).bc([128, 2, 64]), eng=("pool" if kt % 2 else "dve"))
                extra = [(sx.re("p a b -> p (a b)"), c["ident4"].re("p h i -> p (h i)"))]
                if kt == t:
                    extra.append((c["ident_b"], c["causal4"].re("p h i -> p (h i)")))
                PTb = b["PT"] if kt % 2 == 0 else b["PT2"]
                self.attn_block(b["KslcT"][:, kt * 128:(kt + 1) * 128].wb(hb["KslcT"][kt]), QT, extra,
                                b["Vslc"][:, kt, g, :].wb(hb["Vslc"][kt]), PTb, self.pO0, kt == 0, kt == t)
            self.attn_finish(self.pO0, g, 8, skip=bool(_os.environ.get("NSA_NO_SLC")))
            k0 = max(0, t - 4)
            for kt in range(k0, t + 1):
                extra = []
                if kt == t:
                    extra.append((c["ident_b"], c["causal4"].re("p h i -> p (h i)")))
                elif kt == t - 4:
                    extra.append((c["ident_b"], c["anti4"].re("p h i -> p (h i)")))
                PTb = b["PT"] if kt % 2 == 0 else b["PT2"]
                self.attn_block(b["KwinT"][:, (kt % 5) * 128:(kt % 5 + 1) * 128].wb(hb["KwinT"][kt % 5]), QT, extra,
                                b["Vwin"][:, kt % 5, g, :].wb(hb["Vwin"][kt % 5]), PTb, self.pO1, kt == k0, kt == t)
            self.attn_finish(self.pO1, g, 16, skip=bool(_os.environ.get("NSA_NO_WIN")))
        ob = b["ob"]
        sl = b["silu"]
        em.act(sl, u[:, O_NZ:O_NZ + 512], AF.Silu)
        em.tt(b["mixed"][:, 256:768], ob.re("p h e -> p (h e)"), sl, ALU.mult)

    def write_states(self, L):
        em, c, b = self.em, self.c, self.pb
        self.out_dma("sp", self.p_gla[L], b["S"])
        cc = b["tmp"][0:64, :, 0:64]
        for h in range(4):
            pp = self.nextAB()
            o_ = (h % 2) * 64
            em.tr(pp[0:64, 0:64], b["Ct"][o_:o_ + 64, h, 0:64], c["ident_f"][o_:o_ + 64, o_:o_ + 64])
            em.cp(cc[:, h, :], pp[0:64, 0:64])
        self.out_dma("sp", self.p_c[L].rearrange("h e d -> e h d"), cc)
        with self.nc.allow_non_contiguous_dma("tiny state vectors"):
            for h in range(4):
                o_ = (h % 2) * 64
                self.out_dma("sp", self.p_n[L, h].rearrange("(d o) -> d o", o=1), b["Ct"][o_:o_ + 64, h, 64:65])
        self.out_dma("sp", self.p_m[L:L + 1, :], b["Mrep"][0:1, :])


    def alloc_dec_bufs(self):
        em, c, b, sb = self.em, self.c, self.pb, self.sb
        d = self.db = {}
        kb_ = b["KslcT"]
        o = [0]

        def carve(n):
            v = kb_[:, o[0]:o[0] + n]
            o[0] += n
            return v
        self.dbufs = {}
        kbufs, vbufs = [], []

        def own(name, v, n=1, lst=None):
            bl = [Buf() for _ in range(n)]
            self.dbufs[name] = bl
            lst.extend(bl)
            return v.wb(bl)
        d["rawT"] = own("rawT", carve(2 * 528).re("p (k i) -> p k i", k=2), 1, kbufs)
        d["KcT"] = own("KcT", carve(512), 1, kbufs)
        d["VcT"] = own("VcT", carve(512), 1, kbufs)
        d["PTall"] = own("PTall", carve(128).re("p (n q) -> p n q", n=4), 1, kbufs)
        d["KTpg"] = own("KTpg", carve(256).re("p (a k) -> p a k", a=2), 2, kbufs)
        d["pgb"] = own("pgb", carve(512).re("p (a f) -> p a f", a=2), 2, kbufs)
        d["PTpg"] = own("PTpg", carve(128).re("p (a q) -> p a q", a=2), 2, kbufs)
        d["Vpg"] = own("Vpg", carve(2 * 2 * 65).re("p (a g e) -> p a g e", a=2, g=2), 2, kbufs)
        assert o[0] <= 4096
        vb_ = b["Vslc"].re("p t g e -> p (t g e)")
        d["Vc"] = own("Vc", vb_[:, 0:4 * 2 * 194].re("p (n g e) -> p n g e", n=4, g=2), 1, vbufs)
        d["stg"] = b["silu"].re("p (a f) -> p a f", a=2)
        d["kc32"] = b["num"].re("p h e -> p (h e)")[0:32, 0:256].re("p (k f) -> p k f", k=2)
        stage = self.stage
        d["cmpb32"] = stage[0:32, 0:256].re("p (k f) -> p k f", k=2)
        d["ptf"] = stage[:, 256:320]
        d["OT"] = stage[0:65, 320:384].re("p (g q) -> p g q", g=2)
        d["imp"] = stage[0:8, 384:656].re("p (g j) -> p g j", g=2)
        d["impt"] = stage[0:8, 656:916].re("p (g j) -> p g j", g=2)
        d["pcolf"] = stage[:, 916:917]
        uh = b["u"][0:8, 2600:3112]
        d["selb"] = uh[:, 0:136]
        d["score"] = uh[:, 136:408].re("p (g j) -> p g j", g=2)
        d["score2"] = b["go"].re("p h e -> p (h e)")[0:8, 0:136]
        d["rope"] = sb("rope_d", [8, 16])
        d["idx"] = self.iota_i32.re("p (a k) -> p a k", a=2)
        d["pti"] = sb("pt_i", [128, 64], I32)
        o2 = [4 * 2 * 194]

        def carve2(n):
            v = vb_[:, o2[0]:o2[0] + n]
            o2[0] += n
            return v
        d["ovl"] = own("ovl", carve2(520).re("p (n j) -> p n j", n=4), 1, vbufs)
        d["selbias"] = own("selbias", carve2(272)[0:8].re("p (g j) -> p g j", g=2), 1, vbufs)
        d["sx"] = own("sx", carve2(512)[0:8].re("p (a g b c) -> p a g b c", a=2, g=2, b=2), 2, vbufs)
        d["Vnew"] = own("Vnew", carve2(260)[0:8].re("p (a g e) -> p a g e", a=2, g=2), 1, vbufs)
        d["anti8"] = own("anti8", carve2(128)[0:8], 1, vbufs)
        d["QTr"] = own("QTr", carve2(64).re("p (g h i) -> p g h i", g=2, h=4), 1, vbufs)
        d["QTn"] = own("QTn", carve2(64).re("p (g h i) -> p g h i", g=2, h=4), 1, vbufs)
        d["KTnew"] = own("KTnew", carve2(16).re("p (j i) -> p j i", j=2), 1, vbufs)
        d["id4_8"] = own("id4_8", carve2(32)[0:8].re("p (h i) -> p h i", h=4), 1, vbufs)
        d["causal8"] = own("causal8", carve2(32)[0:8].re("p (h i) -> p h i", h=4), 1, vbufs)
        assert o2[0] <= 4160
        self.dec_allk = b["KslcT"].wb(list(b["KslcT"].bs) + self.hb["KslcT"] + kbufs)
        self.dec_allv = b["Vslc"].wb(list(b["Vslc"].bs) + self.hb["Vslc"] + vbufs)
        self.pc2 = sb("pcol2_i32", [128, 1], I32)
        em.op("pool", lambda e: e.iota(self.pc2.ap, pattern=[[0, 1]], base=0, channel_multiplier=1), writes=[self.pc2])
        em.dma("sp", d["rope"], self.c_rope_d, serial=True)

    def dec_consts(self):
        em, c, b, d = self.em, self.c, self.pb, self.db
        stage = self.stage
        ov = b["x"][:, 0:520].re("p (n j) -> p n j", n=4)
        em.dma("sp", ov, self.c_ovl_d.rearrange("(n p) j -> p n j", p=128), serial=True)
        em.cp(d["ovl"], ov)
        em.dma("sp", d["selb"], self.c_selb_d, serial=True)
        for h in range(4):
            em.cp(d["id4_8"][:, h, :], c["ident_f"][0:8, 0:8])
            em.cp(d["causal8"][:, h, :], c["causal4"][0:8, 0, 0:8])
        a8 = b["x"][0:8, 600:728]
        em.memset(a8, 0.0)
        em.asel(a8, a8, [[1, 128]], ALU.is_gt, NEG, 0, -1)
        em.cp(d["anti8"], a8)
        em.cp(d["pcolf"], self.pc2)
        em.memset(d["Vpg"][:, :, :, 64:65], 1.0)

    def dv(self, name, a):
        return self.db[name][:, a].wb(self.dbufs[name][a])

    def dec_layer(self, L):
        em, w, c, b = self.em, self.w, self.c, self.pb
        if L == 0:
            self.alloc_dec_bufs()
        d = self.db
        em.memset(self.dec_allk, 0.0)
        em.memset(self.dec_allv, 0.0)
        self.dec_consts()
        for kv in range(2):
            pp = self.nextAB()
            em.mm(pp[0:32, 0:128], c["ones"][0:1, 0:32], w["cmpb_tm"][0:1, kv, :])
            em.cp(d["cmpb32"][:, kv, :], pp[0:32, 0:128])
        for g in range(2):
            em.cp(d["Vc"][:, :, g, 64:194], d["ovl"])
        if not hasattr(self, "yd1_v"):
            self.yd1_v = [V(self.yd1[s_], Buf()) for s_ in range(4)]
        x_src = self.x_sample if L == 0 else self.yd1_v
        y_dst = self.y_sample if L == self.LAYERS - 1 else self.yd1_v
        for s in range(self.NDEC):
            self.dec_seq(L, s, x_src, y_dst)

    def dec_seq(self, L, s, x_src, y_dst):
        em, w, c, b, d = self.em, self.w, self.c, self.pb, self.db
        P = 8
        em.rotate("g0")
        em.rotate("g1")
        em.rotate("w0")
        em.rotate("w1")
        x, u = b["x"], b["u"]
        em.dma("sp", b["S"], self.st_gla[L, s], serial=True)
        em.cp(b["S_b"], b["S"], eng="act")
        em.memset(b["Ct"], 0.0)
        cst = b["tmp"][0:64, :, :].re("p c i -> p (c i)")[:, 0:256].re("p (h d) -> p h d", h=4)
        em.dma("sp", cst, self.st_c[L, s].rearrange("h e d -> e h d"), serial=True)
        for j in range(2):
            pp = self.nextAB()
            em.tr(pp[:, 0:64], cst[:, 2 * j:2 * j + 2, :].re("p h d -> p (h d)"), c["ident_f"][0:64, 0:64])
            em.cp(b["Ct"][0:64, 2 * j, 0:64], pp[0:64, 0:64])
            em.cp(b["Ct"][64:128, 2 * j + 1, 0:64], pp[64:128, 0:64])
        with self.nc.allow_non_contiguous_dma("tiny state vectors"):
            for h in range(4):
                o_ = (h % 2) * 64
                em.dma("sp", b["Ct"][o_:o_ + 64, h, 64:65], self.st_n[L, s, h].rearrange("(d o) -> d o", o=1), serial=True)
            for r in range(3):
                em.dma("sp", b["qkT"][:, :, r], self.st_conv[L, s, r].rearrange("(c p) -> p c", p=128), serial=True)
        em.cp(b["Ct_b"], b["Ct"], eng="act")
        em.dma("sp", b["Mrep"], self.st_m[L, s:s + 1, :].to_broadcast([128, 4]), serial=True)
        em.dma("sp", x[0:P], x_src[s], lane="x")
        ss = b["st1"][0:P, 0:1]
        em.act(u[0:P, 0:D], x[0:P], AF.Square, accum=ss)
        em.ts(ss, ss, 1.0 / D, ALU.mult, EPS, ALU.add)
        em.act(ss, ss, AF.Sqrt)
        em.recip(ss, ss)
        em.stt(b["xn"][0:P], x[0:P], ss, w["normg"][0:P], ALU.mult, ALU.mult)
        for k in range(8):
            pt = self.nextT()
            em.tr(pt[:, 0:P], b["xn"][0:P, k * 128:(k + 1) * 128], c["ident_b"][0:P, 0:P])
            em.cp(b["xnT"][:, k, 0:P], pt[:, 0:P], eng=("act" if k % 2 else "dve"))
        chunks = [(0, 512), (512, 512), (1024, 512), (1536, 512), (2048, 512), (2560, 40), (3112, 512), (3624, 264)]
        for ci, (c0, cw) in enumerate(chunks):
            pp = self.nextAB()
            for k in range(8):
                em.mm(pp[0:P, 0:cw], b["xnT"][:, k, 0:P], w["win"][:, k, c0:c0 + cw], start=(k == 0), stop=(k == 7))
            em.cp(u[0:P, c0:c0 + cw], pp[0:P, 0:cw], eng=("act" if ci % 2 else "dve"))
        for cch in range(4):
            pp = self.nextAB()
            for k in range(8):
                em.mm(pp[:, 0:P], w["win"][:, k, O_MQK + cch * 128:O_MQK + (cch + 1) * 128], b["xnT"][:, k, 0:P],
                      start=(k == 0), stop=(k == 7))
            em.cp(b["qkT"][:, cch, 3:3 + P], pp[:, 0:P], eng=("act" if cch % 2 else "dve"))
        import os as _os
        self.gla_tile(L, P)
        self.mlstm_tile(L, P)
        if not _os.environ.get("SKIP_NSAD"):
            self.nsa_dec(L, s)
        for k in range(8):
            pt = self.nextT()
            em.tr(pt[:, 0:P], b["mixed"][0:P, k * 128:(k + 1) * 128], c["ident_b"][0:P, 0:P])
            em.cp(b["mixT"][:, k, 0:P], pt[:, 0:P], eng=("act" if k % 2 else "dve"))
        for half in range(2):
            pp = self.nextAB()
            for k in range(8):
                em.mm(pp[0:P, 0:512], b["mixT"][:, k, 0:P], w["wout"][:, k, half * 512:(half + 1) * 512],
                      start=(k == 0), stop=(k == 7))
            em.tt(x[0:P, half * 512:(half + 1) * 512], pp[0:P, 0:512], x[0:P, half * 512:(half + 1) * 512], ALU.add)
        self.out_dma("sp", y_dst[s], x[0:P])
        with self.nc.allow_non_contiguous_dma("tiny conv state"):
            for r in range(3):
                self.out_dma("sp", self.s_conv[L, s, r].rearrange("(c p) -> p c", p=128), b["qkT"][:, :, P + r])
        self.out_dma("sp", self.s_gla[L, s], b["S"])
        cc = b["tmp"][0:64, :, 0:64]
        for h in range(4):
            pp = self.nextAB()
            o_ = (h % 2) * 64
            em.tr(pp[0:64, 0:64], b["Ct"][o_:o_ + 64, h, 0:64], c["ident_f"][o_:o_ + 64, o_:o_ + 64])
            em.cp(cc[:, h, :], pp[0:64, 0:64])
        self.out_dma("sp", self.s_c[L, s].rearrange("h e d -> e h d"), cc)
        with self.nc.allow_non_contiguous_dma("tiny state vectors"):
            for h in range(4):
                o_ = (h % 2) * 64
                self.out_dma("sp", self.s_n[L, s, h].rearrange("(d o) -> d o", o=1), b["Ct"][o_:o_ + 64, h, 64:65])
        self.out_dma("sp", self.s_m[L, s:s + 1, :], b["Mrep"][0:1, :])

    def dec_finish_branch(self, g, gate_off):
        em, c, b, d = self.em, self.c, self.pb, self.db
        po = self.pO0 if g == 0 else self.pO1
        em.cp(d["OT"][:, g, :], po[0:65, 0:32])
        pp = self.nextAB()
        ppv = pp[0:8, 0:260].re("p (h e) -> p h e", h=4)
        for hh in range(4):
            em.tr(ppv[:, hh, :], d["OT"][:, g, hh * 8:(hh + 1) * 8], c["ident_f"][0:65, 0:65])
        rd = b["rden"][0:8, 0:4]
        em.recip(rd, ppv[:, :, 64])
        t_ = b["ocmp"][0:8, 0:4, :]
        em.tt(t_, ppv[:, :, 0:64], rd.un(2).bc([8, 4, 64]), ALU.mult)
        em.tt(t_, t_, b["gate"][0:8, gate_off + g * 4:gate_off + g * 4 + 4].un(2).bc([8, 4, 64]), ALU.mult)
        import os as _os
        if not _os.environ.get("NSA_NO_SLC" if gate_off == 8 else "NSA_NO_WIN"):
            em.tt(b["ob"][0:8, g * 4:(g + 1) * 4, :], b["ob"][0:8, g * 4:(g + 1) * 4, :], t_, ALU.add)

    def nsa_dec(self, L, s):
        em, w, c, b, d = self.em, self.w, self.c, self.pb, self.db
        u = b["u"]
        P = 8
        cs = d["rope"]
        em.dma("act", d["pti"], self.page_table[s:s + 1, :].to_broadcast([128, 64]), serial=True)
        em.cp(d["ptf"], d["pti"])
        em.ts(d["ptf"], d["ptf"], 128.0, ALU.mult, d["pcolf"], ALU.add)
        em.ts(d["ptf"], d["ptf"], 2.0, ALU.mult)
        em.cp(d["idx"][:, 0, :], d["ptf"])
        em.ts(d["ptf"], d["ptf"], 1.0, ALU.add)
        em.cp(d["idx"][:, 1, :], d["ptf"])
        qn = b["qn"][0:P]
        self.rms_heads(qn, u[0:P, O_NQ:O_NQ + 512].re("p (h e) -> p h e", h=8), 8, w["qg"][0:P], P=P)
        em.cp(b["qn_b"][0:P].re("p (hh g) e -> p g hh e", g=2), qn.re("p (g hh) e -> p g hh e", g=2), eng="pool")
        self.rope(b["ocmp"][0:P], qn, 8, cs, P=P)
        em.cp(b["qr"][0:P].re("p (hh g) e -> p g hh e", g=2), b["ocmp"][0:P].re("p (g hh) e -> p g hh e", g=2))
        kv = u[0:P, O_NKV:O_NKV + 768].re("p (s g e) -> p s g e", s=6, g=2)
        rows = b["rows"][0:P].re("p s (g e) -> p s g e", g=2)
        winr = b["win"][0:P].re("p s (g e) -> p s g e", g=2)
        em.cp(rows[:, 0], kv[:, 0], eng="pool")
        em.cp(rows[:, 1], kv[:, 1], eng="pool")
        em.cp(rows[:, 3], kv[:, 3], eng="pool")
        em.cp(winr[:, 1], kv[:, 5], eng="pool")
        ktmp = b["ocmp"][0:P, 0:2, :]
        self.rms_heads(ktmp, kv[:, 2], 2, w["kg"][0:P, 1, :], P=P)
        self.rope(rows[:, 2], ktmp, 2, cs, P=P)
        self.rms_heads(ktmp, kv[:, 4], 2, w["kg"][0:P, 2, :], P=P)
        self.rope(winr[:, 0], ktmp, 2, cs, P=P)
        self.out_dma("act", self.s_rows[L, s], b["rows"][0:P].re("p s f -> p (s f)"))
        self.out_dma("act", self.s_win[L, s, 0:504, :], self.st_win[L, s, 8:512, :])
        self.out_dma("act", self.s_win[L, s, 504:512, :], b["win"][0:P].re("p s f -> p (s f)"))
        kkb = b["kk_b"]
        em.cp(kkb[0:P, 0, :], b["rows"][0:P, 2, :])
        em.cp(kkb[0:P, 1, :], b["win"][0:P, 0, :])
        pt = self.nextT()
        for j in range(2):
            em.tr(pt[:, j * 8:(j + 1) * 8], kkb[0:P, j, :], c["ident_b"][0:P, 0:P])
        em.cp(d["KTnew"], pt[:, 0:16].re("p (j i) -> p j i", j=2))
        em.memset(d["Vnew"][:, :, :, 64:65], 1.0)
        em.cp(d["Vnew"][:, 0, :, 0:64], rows[:, 3])
        em.cp(d["Vnew"][:, 1, :, 0:64], winr[:, 1])
        pt = self.nextT()
        qr4 = b["qr"][0:P].re("p (hh g) e -> p hh (g e)", g=2)
        qn4 = b["qn_b"][0:P].re("p (hh g) e -> p hh (g e)", g=2)
        for hh in range(4):
            em.tr(pt[:, hh * 8:(hh + 1) * 8], qr4[:, hh, :], c["ident_b"][0:P, 0:P])
            em.tr(pt[:, 32 + hh * 8:32 + (hh + 1) * 8], qn4[:, hh, :], c["ident_b"][0:P, 0:P])
        for g in range(2):
            em.ts(d["QTr"][:, g].re("p h i -> p (h i)"), pt[:, 0:32], c["pmask2"][:, g:g + 1], ALU.mult)
            em.ts(d["QTn"][:, g].re("p h i -> p (h i)"), pt[:, 32:64], c["pmask2"][:, g:g + 1], ALU.mult)
        em.act(b["gate"][0:P], u[0:P, O_NG:O_NG + 24], AF.Sigmoid)
        cacheL = self.cache[L]
        em.memset(d["rawT"], 0.0)
        em.memset(d["KcT"], 0.0)
        em.memset(d["VcT"], 0.0)
        for bq in range(16):
            em.cp(d["rawT"][:, :, 0:16], d["rawT"][:, :, 512:528], eng="pool")
            for pg in range(4):
                kt = bq * 4 + pg
                a = kt % 2
                stg = d["stg"][:, a, :]
                self.gather(stg, cacheL, d["idx"][:, 0, kt:kt + 1], lane="g%d" % a)
                pgb = self.dv("pgb", a)
                em.cp(pgb, stg, eng=("act" if a else "dve"))
                pt = self.nextT()
                for j in range(2):
                    em.tr(pt[:, j * 128:(j + 1) * 128], pgb[:, j * 128:(j + 1) * 128], c["ident_b"])
                em.cp(d["rawT"][:, :, 16 + pg * 128:16 + (pg + 1) * 128], pt[:, 0:256].re("p (k i) -> p k i", k=2),
                      eng=("dve" if a else "act"))
            for kvi in range(2):
                pp = self.nextAB()
                for s_ in range(32):
                    em.mm(pp[0:32, 0:128], d["rawT"][:, kvi, s_:s_ + 497:16], w["cmpw"][:, kvi, s_, :],
                          start=(s_ == 0), stop=(s_ == 31))
                em.tt(d["kc32"][:, kvi, :], pp[0:32, 0:128], d["cmpb32"][:, kvi, :], ALU.add)
            kcn = b["kk_b"][0:32, 2, :].re("p (g e) -> p g e", g=2)
            self.rms_heads(kcn, d["kc32"][:, 0, :].re("p (g e) -> p g e", g=2), 2, w["kg"][0:32, 0, :], P=32)
            vcb = b["kk_b"][0:32, 3, :]
            em.cp(vcb, d["kc32"][:, 1, :])
            pt = self.nextT()
            em.tr(pt[:, 0:32], kcn.re("p g e -> p (g e)"), c["ident_b"][0:32, 0:32])
            em.tr(pt[:, 32:64], vcb, c["ident_b"][0:32, 0:32])
            n0 = 32 * bq - 1
            if bq == 0:
                em.cp(d["KcT"][:, 0:31], pt[:, 1:32])
                em.cp(d["VcT"][:, 0:31], pt[:, 33:64])
            else:
                em.cp(d["KcT"][:, n0:n0 + 32], pt[:, 0:32])
                em.cp(d["VcT"][:, n0:n0 + 32], pt[:, 32:64])
        for nt_ in range(4):
            pt = self.nextT()
            em.tr(pt[:, 0:128], d["VcT"][:, nt_ * 128:(nt_ + 1) * 128], c["ident_b"])
            em.cp(d["Vc"][:, nt_, :, 0:64], pt[:, 0:128].re("p (g e) -> p g e", g=2))
        for g in range(2):
            ps_ = self.nextS()
            for nt_ in range(4):
                em.mm(ps_[:, nt_ * 32:(nt_ + 1) * 32], d["KcT"][:, nt_ * 128:(nt_ + 1) * 128],
                      d["QTn"][:, g].re("p h i -> p (h i)"))
            em.act(d["PTall"].re("p n q -> p (n q)"), ps_[:, 0:128], AF.Exp, scale=ATT_SCALE)
            for half in range(2):
                po = self.nextAB()
                pov = po[0:8, 0:388].re("p (h e) -> p h e", h=2)
                for h2 in range(2):
                    hh = half * 2 + h2
                    for nt_ in range(4):
                        em.mm(pov[:, h2, :], d["PTall"][:, nt_, hh * 8:(hh + 1) * 8], d["Vc"][:, nt_, g, :],
                              start=(nt_ == 0), stop=(nt_ == 3))
                rd = b["rden"][0:8, 0:2]
                em.ts(rd, pov[:, :, 64], 1e-30, ALU.max)
                em.recip(rd, rd)
                h0 = g * 4 + half * 2
                em.tt(b["ocmp"][0:8, 0:2, :], pov[:, :, 0:64], rd.un(2).bc([8, 2, 64]), ALU.mult)
                em.tt(b["ob"][0:8, h0:h0 + 2, :], b["ocmp"][0:8, 0:2, :], b["gate"][0:8, h0:h0 + 2].un(2).bc([8, 2, 64]), ALU.mult)
                em.tt(d["impt"][:, :, 0:129], pov[:, :, 65:194], rd.un(2).bc([8, 2, 129]), ALU.mult)
                if half == 0:
                    em.tt(d["imp"][:, g, 0:129], d["impt"][:, 0, 0:129], d["impt"][:, 1, 0:129], ALU.add)
                else:
                    em.tt(d["imp"][:, g, 0:129], d["imp"][:, g, 0:129], d["impt"][:, 0, 0:129], ALU.add)
                    em.tt(d["imp"][:, g, 0:129], d["imp"][:, g, 0:129], d["impt"][:, 1, 0:129], ALU.add)
        em.memset(d["imp"][:, :, 129:136], 0.0)
        for g in range(2):
            sc = d["score"][:, g, :]
            em.tt(sc, d["imp"][:, g, :], d["selb"], ALU.add)
            mx8 = b["max8"][0:8]
            em.op("dve", lambda e: e.max(out=mx8.ap, in_=sc.ap), reads=[sc], writes=[mx8])
            em.op("dve", lambda e: e.match_replace(out=d["score2"].ap, in_to_replace=mx8.ap, in_values=sc.ap,
                                                   imm_value=-3.0e38), reads=[sc, mx8], writes=[d["score2"]])
            em.op("dve", lambda e: e.max(out=mx8.ap, in_=d["score2"].ap), reads=[d["score2"]], writes=[mx8])
            em.ts(d["score2"], sc, mx8[:, 7:8], ALU.is_ge, -1.0, ALU.add)
            em.ts(d["selbias"][:, g, :], d["score2"], -NEG, ALU.mult)
        for kt in range(65):
            a = kt % 2
            newt = (kt == 64)
            if not newt:
                stg = d["stg"][:, a, :]
                self.gather(stg, cacheL, d["idx"][:, 1, kt:kt + 1], lane="g%d" % a)
                pgb = self.dv("pgb", a)[:, 0:128]
                em.cp(pgb, stg[:, 0:128], eng=("act" if a else "dve"))
                em.cp(self.dv("Vpg", a)[:, :, 0:64], stg[:, 128:256].re("p (g e) -> p g e", g=2), eng="pool")
                pt = self.nextT()
                em.tr(pt[:, 0:128], pgb, c["ident_b"])
                KT = self.dv("KTpg", a)
                em.cp(KT, pt[:, 0:128], eng=("dve" if a else "act"))
                nk = 128
            else:
                KT = d["KTnew"][:, 0, :]
                nk = 8
            ps_ = self.nextS()
            sx = self.dv("sx", a)
            for g in range(2):
                if not newt:
                    em.cp(sx[:, g], d["selbias"][:, g, 2 * kt:2 * kt + 2].un(2).bc([8, 2, 64]), eng="pool")
                    lt = sx[:, g].re("p a b -> p (a b)")
                else:
                    em.cp(sx[:, g, 0, 0:8], d["selbias"][:, g, 128:129].bc([8, 8]), eng="pool")
                    lt = sx[:, g, 0, 0:8]
                psv = ps_[0:nk, g * 32:(g + 1) * 32]
                em.mm(psv, KT, d["QTr"][:, g].re("p h i -> p (h i)"), start=True, stop=False)
                em.mm(psv, lt, d["id4_8"].re("p h i -> p (h i)"), start=False, stop=not newt)
                if newt:
                    em.mm(psv, c["ident_b"][0:8, 0:8], d["causal8"].re("p h i -> p (h i)"), start=False, stop=True)
            PT = self.dv("PTpg", a)[0:nk]
            em.act(PT, ps_[0:nk, 0:64], AF.Exp, scale=ATT_SCALE)
            for g in range(2):
                po = self.pO0 if g == 0 else self.pO1
                Vr = self.dv("Vpg", a)[:, g, :] if not newt else d["Vnew"][:, 0, g, :]
                em.mm(po[0:65, 0:32], Vr, PT[:, g * 32:(g + 1) * 32], start=(kt == 0), stop=newt)
        for g in range(2):
            self.dec_finish_branch(g, 8)
        for kt in range(5):
            a = kt % 2
            newt = (kt == 4)
            if not newt:
                stg = d["stg"][:, a, :]
                em.dma("sp", stg, self.st_win[L, s, kt * 128:(kt + 1) * 128, :], lane="w%d" % a)
                pgb = self.dv("pgb", a)[:, 0:128]
                em.cp(pgb, stg[:, 0:128], eng=("act" if a else "dve"))
                em.cp(self.dv("Vpg", a)[:, :, 0:64], stg[:, 128:256].re("p (g e) -> p g e", g=2), eng="pool")
                pt = self.nextT()
                em.tr(pt[:, 0:128], pgb, c["ident_b"])
                KT = self.dv("KTpg", a)
                em.cp(KT, pt[:, 0:128], eng=("dve" if a else "act"))
                nk = 128
            else:
                KT = d["KTnew"][:, 1, :]
                nk = 8
            ps_ = self.nextS()
            for g in range(2):
                psv = ps_[0:nk, g * 32:(g + 1) * 32]
                extra = (kt == 0) or newt
                em.mm(psv, KT, d["QTr"][:, g].re("p h i -> p (h i)"), start=True, stop=not extra)
                if kt == 0:
                    em.mm(psv, d["anti8"], d["id4_8"].re("p h i -> p (h i)"), start=False, stop=True)
                if newt:
                    em.mm(psv, c["ident_b"][0:8, 0:8], d["causal8"].re("p h i -> p (h i)"), start=False, stop=True)
            PT = self.dv("PTpg", a)[0:nk]
            em.act(PT, ps_[0:nk, 0:64], AF.Exp, scale=ATT_SCALE)
            for g in range(2):
                po = self.pO0 if g == 0 else self.pO1
                Vr = self.dv("Vpg", a)[:, g, :] if not newt else d["Vnew"][:, 1, g, :]
                em.mm(po[0:65, 0:32], Vr, PT[:, g * 32:(g + 1) * 32], start=(kt == 0), stop=newt)
        for g in range(2):
            self.dec_finish_branch(g, 16)
        sl = b["silu"][0:P]
        em.act(sl, u[0:P, O_NZ:O_NZ + 512], AF.Silu)
        em.tt(b["mixed"][0:P, 256:768], b["ob"][0:P].re("p h e -> p (h e)"), sl, ALU.mult)

    def gather(self, out_v, src_rows, idx_v, lane=None):
        em = self.em
        if em.n_ins >= em.limit:
            return
        em._deps("pool", [idx_v], [out_v])
        ins = self.nc.gpsimd.indirect_dma_start(out=out_v.ap, out_offset=None, in_=src_rows[:, :],
                                                in_offset=bass.IndirectOffsetOnAxis(ap=idx_v.ap, axis=0))
        k = em.lane_key("pool", lane)
        em.cnt[k] += 16
        ins.then_inc(em.sem[k], 16)
        em._mark((k, em.cnt[k]), [idx_v], [out_v])
        em.n_ins += 1


def _host_consts():
    half = 8
    inv = np.exp(-math.log(500000.0) * np.arange(half, dtype=np.float32) * 2.0 / 16).astype(np.float32)
    pos = np.arange(SEQ, dtype=np.float32)
    ang = (pos[:, None] * inv[None, :]).astype(np.float32)
    rope = np.concatenate([np.cos(ang), np.sin(ang)], axis=1).astype(np.float32)
    i = np.arange(256)[:, None]
    j = np.arange(64)[None, :]
    s = i - 4 * j + 1
    cnt = np.minimum(np.minimum(s + 1, 4 + 2 - 1 - s), 2)
    ovl = np.maximum(cnt, 0).astype(np.float32)
    ovl[255] = 0.0
    selb = np.zeros((32, 128, 64), np.float32)
    for t in range(32):
        p = t * 128 + np.arange(128)
        cur = (p // 64)[:, None]
        blk = np.arange(64)[None, :]
        valid = blk * 64 <= p[:, None]
        forced = (blk == 0) | (blk == cur) | (blk == cur - 1)
        bias = np.where(forced, 1.0e9 + 1.0e6 * blk, np.where(valid, 0.0, -1.0e9 - 1.0e6 * blk))
        selb[t] = bias
    return rope, ovl, selb


def _host_consts_dec():
    half = 8
    inv = np.exp(-math.log(500000.0) * np.arange(half, dtype=np.float32) * 2.0 / 16).astype(np.float32)
    pos = (8192 + np.arange(8)).astype(np.float32)
    ang = (pos[:, None] * inv[None, :]).astype(np.float32)
    rope = np.concatenate([np.cos(ang), np.sin(ang)], axis=1).astype(np.float32)
    i = np.arange(512)[:, None]
    j = np.arange(129)[None, :]
    s = i - 4 * j + 1
    cnt = np.minimum(np.minimum(s + 1, 4 + 2 - 1 - s), 2)
    ovl = np.maximum(cnt, 0).astype(np.float32)
    ovl[511] = 0.0
    valid = np.ones((512, 1), np.float32)
    valid[511] = 0.0
    ovl_d = np.concatenate([valid, ovl], axis=1).astype(np.float32)
    selb = np.zeros((8, 136), np.float32)
    p = 8192 + np.arange(8)
    cur = (p // 64)[:, None]
    blk = np.arange(136)[None, :]
    valid_b = (blk * 64 <= p[:, None]) & (blk < 129)
    forced = ((blk == 0) | (blk == cur) | (blk == cur - 1)) & (blk < 129)
    selb[:] = np.where(forced, 1.0e9 + 1.0e6 * blk, np.where(valid_b, 0.0, -1.0e9 - 1.0e6 * blk))
    return rope, ovl_d, selb


_CACHE = {}


def _get_nc(nt, ndec, layers):
    key = (nt, ndec, layers)
    if key not in _CACHE:
        kb = KB(nt, ndec, layers)
        _CACHE[key] = kb.build()
    return _CACHE[key]


def kernel(x_prompt, x_sample, cache_nsa_kv, state_nsa_win, state_gla, state_mlstm_C, state_mlstm_n,
           state_mlstm_m, state_mlstm_conv, page_table, norm_g, w_in, w_out, gla_w_gate, gla_b_gate,
           gla_norm_g, nsa_q_norm_g, nsa_k_norm_g, nsa_cmp_pos, nsa_cmp_w, ml_conv_w, ml_conv_b,
           ml_gate_b, ml_norm_g, _nt=32, _cores=8, _layers=2, _ndec=4):
    f = lambda a: np.ascontiguousarray(np.asarray(a, dtype=np.float32))
    rope, ovl, selb = _host_consts()
    rope_d, ovl_d, selb_d = _host_consts_dec()
    nc = _get_nc(_nt, _ndec, _layers)
    shared = {
        "w_in": f(w_in), "w_out": f(w_out), "norm_g": f(norm_g), "gla_w_gate": f(gla_w_gate),
        "gla_b_gate": f(gla_b_gate), "gla_norm_g": f(gla_norm_g), "nsa_q_norm_g": f(nsa_q_norm_g),
        "nsa_k_norm_g": f(nsa_k_norm_g), "nsa_cmp_pos": f(nsa_cmp_pos), "nsa_cmp_w": f(nsa_cmp_w),
        "ml_conv_w": f(ml_conv_w), "ml_conv_b": f(ml_conv_b), "ml_gate_b": f(ml_gate_b).reshape(DEPTH, 8),
        "ml_norm_g": f(ml_norm_g), "c_rope": rope, "c_ovl": ovl, "c_selb": selb,
        "c_rope_d": rope_d, "c_ovl_d": ovl_d, "c_selb_d": selb_d,
    }
    cache = f(cache_nsa_kv)
    for l_ in range(DEPTH):
        shared["cache_nsa_kv%d" % l_] = cache[l_].reshape(-1, 256)
    pt_ = np.ascontiguousarray(np.asarray(page_table, dtype=np.int32))
    in_maps = []
    for cidx in range(_cores):
        m = dict(shared)
        m["x_prompt"] = f(x_prompt[cidx % 4])
        sl = slice(4 * cidx, 4 * cidx + 4)
        m["x_sample"] = f(x_sample[sl])
        m["state_nsa_win"] = f(state_nsa_win[:, sl]).reshape(DEPTH, 4, 512, 256)
        m["state_gla"] = f(state_gla[:, sl]).reshape(DEPTH, 4, 128, 64)
        m["state_mlstm_C"] = f(state_mlstm_C[:, sl])
        m["state_mlstm_n"] = f(state_mlstm_n[:, sl])
        m["state_mlstm_m"] = f(state_mlstm_m[:, sl])
        m["state_mlstm_conv"] = f(state_mlstm_conv[:, sl])
        m["page_table"] = np.ascontiguousarray(pt_[sl])
        in_maps.append(m)
    res = run_bass_kernel_spmd(nc, in_maps, core_ids=list(range(_cores)))
    R = res.results
    nb = min(4, _cores)
    y_prompt = np.stack([R[i]["y_prompt"] for i in range(nb)])
    p_rows = np.stack([R[i]["p_rows"] for i in range(nb)], axis=1).reshape(DEPTH, nb, SEQ, 4, 2, 64)
    p_win = np.stack([R[i]["p_win"] for i in range(nb)], axis=1).reshape(DEPTH, nb, 512, 2, 2, 64)
    p_gla = np.stack([R[i]["p_gla"] for i in range(nb)], axis=1).reshape(DEPTH, nb, 4, 32, 64)
    p_c = np.stack([R[i]["p_c"] for i in range(nb)], axis=1)
    p_n = np.stack([R[i]["p_n"] for i in range(nb)], axis=1)
    p_m = np.stack([R[i]["p_m"] for i in range(nb)], axis=1)
    p_conv = np.stack([R[i]["p_conv"] for i in range(nb)], axis=1)
    cat = lambda k, ax: np.concatenate([R[i][k] for i in range(_cores)], axis=ax)
    nd = 4 * _cores
    y_sample = cat("y_sample", 0)
    s_rows = cat("s_rows", 1).reshape(DEPTH, nd, 8, 4, 2, 64)
    s_win = cat("s_win", 1).reshape(DEPTH, nd, 512, 2, 2, 64)
    s_gla = cat("s_gla", 1).reshape(DEPTH, nd, 4, 32, 64)
    s_c = cat("s_c", 1)
    s_n = cat("s_n", 1)
    s_m = cat("s_m", 1)
    s_conv = cat("s_conv", 1)
    return (y_prompt, y_sample, p_rows, s_rows, p_win, s_win, p_gla, s_gla, p_c, s_c, p_n, s_n, p_m, s_m,
            p_conv, s_conv)
```

```python
import math
import numpy as np
from contextlib import ExitStack
import concourse.bass as bass
import concourse.mybir as mybir
from concourse.bass_utils import run_bass_kernel_spmd

F32 = mybir.dt.float32
BF16 = mybir.dt.bfloat16
I32 = mybir.dt.int32
AF = mybir.ActivationFunctionType
ALU = mybir.AluOpType
AX = mybir.AxisListType

D = 1024
SEQ = 4096
DEPTH = 2
NEG = -30000.0
EPS = 1e-6
O_GQ, O_GK, O_GV, O_GA, O_GZ = 0, 128, 256, 512, 528
O_NQ, O_NKV, O_NG, O_NZ = 784, 1296, 2064, 2088
O_MQK, O_MV, O_MIF, O_MO, O_MZ = 2600, 3112, 3368, 3376, 3632
D_IN = 3888
ATT_SCALE = 0.125


class Buf:
    __slots__ = ("w", "rd", "excl")

    def __init__(self, excl=False):
        self.w = None
        self.rd = []
        self.excl = excl


class V:
    __slots__ = ("ap", "bs")

    def __init__(self, ap, bs):
        self.ap = ap
        self.bs = bs if isinstance(bs, (list, tuple)) else [bs]

    def __getitem__(self, key):
        return V(self.ap[key], self.bs)

    def re(self, pattern_, **kw):
        return V(self.ap.rearrange(pattern_, **kw), self.bs)

    def bc(self, shape):
        return V(self.ap.to_broadcast(list(shape)), self.bs)

    def un(self, axis):
        return V(self.ap.unsqueeze(axis), self.bs)

    def wb(self, b):
        return V(self.ap, b)


class Em:
    def __init__(self, nc, st):
        self.nc = nc
        self.st = st
        self.E = {"pe": nc.tensor, "act": nc.scalar, "dve": nc.vector, "pool": nc.gpsimd, "sp": nc.sync}
        self.sem = {}
        self.cnt = {}
        for k in list(self.E) + ["d_sp", "d_act", "d_pool"]:
            self.sem[k] = st.enter_context(nc.semaphore("s_" + k))
            self.cnt[k] = 0
        self.seen = {k: {} for k in self.E}
        self.dkey = {"sp": "d_sp", "act": "d_act", "pool": "d_pool"}
        self.serbuf = {}
        self.redirect_pool = False
        self.nrot = 0
        self.n_ins = 0
        import os as _os
        self.limit = int(_os.environ.get("OP_LIMIT", "100000000"))

    def _deps(self, eng, reads, writes):
        deps = {}
        for v in reads:
            for b in v.bs:
                if b.w is not None:
                    deps[b.w[0]] = max(deps.get(b.w[0], 0), b.w[1])
                if b.excl:
                    for r in b.rd:
                        if r[0] != eng:
                            deps[r[0]] = max(deps.get(r[0], 0), r[1])
        for v in writes:
            for b in v.bs:
                if b.w is not None:
                    deps[b.w[0]] = max(deps.get(b.w[0], 0), b.w[1])
                for r in b.rd:
                    deps[r[0]] = max(deps.get(r[0], 0), r[1])
        for sk, val in deps.items():
            if sk == eng and eng == "pe":
                continue
            if self.seen[eng].get(sk, 0) < val:
                self.E[eng].wait_ge(self.sem[sk], val)
                self.seen[eng][sk] = val

    def _mark(self, tok, reads, writes):
        for v in reads:
            for b in v.bs:
                b.rd.append(tok)
                if len(b.rd) > 64:
                    mx = {}
                    for r in b.rd:
                        mx[r[0]] = max(mx.get(r[0], 0), r[1])
                    b.rd = list(mx.items())
        for v in writes:
            for b in v.bs:
                b.w = tok
                b.rd = []

    def rotate(self, q):
        self.nrot += 1
        k = "d_%s#%d" % (q, self.nrot)
        self.sem[k] = self.st.enter_context(self.nc.semaphore("s_" + k.replace("#", "_")))
        self.cnt[k] = 0
        self.dkey[q] = k

    def op(self, eng, build, reads=(), writes=(), generic=False):
        if generic and eng == "pool" and self.redirect_pool:
            eng = "dve"
        if self.n_ins >= self.limit:
            return None
        self._deps(eng, reads, writes)
        ins = build(self.E[eng])
        self.cnt[eng] += 1
        ins.then_inc(self.sem[eng], 1)
        self._mark((eng, self.cnt[eng]), reads, writes)
        self.n_ins += 1
        return ins

    def lane_key(self, q, lane):
        if lane is None:
            return self.dkey[q]
        if lane not in self.dkey:
            self.rotate(lane)
        return self.dkey[lane]

    def dma(self, q, out, in_, lane=None, serial=False, **kw):
        reads = [in_] if isinstance(in_, V) else []
        writes = [out] if isinstance(out, V) else []
        if serial:
            lane = lane or ("ser_" + q)
            if lane not in self.serbuf:
                self.serbuf[lane] = V(None, Buf())
            writes = writes + [self.serbuf[lane]]
        if self.n_ins >= self.limit:
            return None
        self._deps(q, reads, writes)
        o = out.ap if isinstance(out, V) else out
        i = in_.ap if isinstance(in_, V) else in_
        ins = self.E[q].dma_start(out=o, in_=i, **kw)
        k = self.lane_key(q, lane)
        self.cnt[k] += 16
        ins.then_inc(self.sem[k], 16)
        self._mark((k, self.cnt[k]), reads, writes)
        self.n_ins += 1
        return ins

    def mm(self, out, lhsT, rhs, start=True, stop=True):
        return self.op("pe", lambda e: e.matmul(out.ap, lhsT=lhsT.ap, rhs=rhs.ap, start=start, stop=stop),
                       reads=[lhsT, rhs], writes=[out])

    def tr(self, out, in_, ident):
        return self.op("pe", lambda e: e.transpose(out=out.ap, in_=in_.ap, identity=ident.ap),
                       reads=[in_, ident], writes=[out])

    def act(self, out, in_, func, bias=None, scale=1.0, accum=None, eng="act"):
        reads = [in_]
        kw = {}
        if bias is not None:
            if isinstance(bias, V):
                reads.append(bias)
                kw["bias"] = bias.ap
            else:
                kw["bias"] = bias
        if isinstance(scale, V):
            reads.append(scale)
            kw["scale"] = scale.ap
        else:
            kw["scale"] = scale
        writes = [out]
        if accum is not None:
            kw["accum_out"] = accum.ap
            writes.append(accum)
        return self.op(eng, lambda e: e.activation(out=out.ap, in_=in_.ap, func=func, **kw), reads=reads, writes=writes)

    def tt(self, out, a, b, op, eng="dve"):
        return self.op(eng, lambda e: e.tensor_tensor(out=out.ap, in0=a.ap, in1=b.ap, op=op), reads=[a, b], writes=[out],
                       generic=True)

    def ts(self, out, a, s1, op0, s2=None, op1=None, eng="dve"):
        reads = [a]
        s1a = s1.ap if isinstance(s1, V) else s1
        s2a = s2.ap if isinstance(s2, V) else s2
        if isinstance(s1, V):
            reads.append(s1)
        if isinstance(s2, V):
            reads.append(s2)
        if op1 is None:
            return self.op(eng, lambda e: e.tensor_scalar(out=out.ap, in0=a.ap, scalar1=s1a, scalar2=None, op0=op0),
                           reads=reads, writes=[out])
        return self.op(eng, lambda e: e.tensor_scalar(out=out.ap, in0=a.ap, scalar1=s1a, scalar2=s2a, op0=op0, op1=op1),
                       reads=reads, writes=[out])

    def stt(self, out, a, s, b, op0, op1, eng="dve"):
        reads = [a, b]
        sa = s.ap if isinstance(s, V) else s
        if isinstance(s, V):
            reads.append(s)
        return self.op(eng, lambda e: e.scalar_tensor_tensor(out=out.ap, in0=a.ap, scalar=sa, in1=b.ap, op0=op0, op1=op1),
                       reads=reads, writes=[out])

    def cp(self, out, in_, eng="dve"):
        if eng == "act":
            return self.op("act", lambda e: e.copy(out=out.ap, in_=in_.ap), reads=[in_], writes=[out])
        return self.op(eng, lambda e: e.tensor_copy(out=out.ap, in_=in_.ap), reads=[in_], writes=[out], generic=True)

    def red(self, out, in_, op=ALU.add, eng="dve"):
        return self.op(eng, lambda e: e.tensor_reduce(out=out.ap, in_=in_.ap, axis=AX.X, op=op), reads=[in_], writes=[out])

    def memset(self, out, val, eng="pool"):
        return self.op(eng, lambda e: e.memset(out.ap, val), writes=[out], generic=True)

    def asel(self, out, in_, pattern, cmp, fill, base, cm):
        return self.op("pool", lambda e: e.affine_select(out=out.ap, in_=in_.ap, pattern=pattern, compare_op=cmp,
                                                         fill=fill, base=base, channel_multiplier=cm),
                       reads=[in_], writes=[out])

    def recip(self, out, in_):
        return self.op("dve", lambda e: e.reciprocal(out=out.ap, in_=in_.ap), reads=[in_], writes=[out])


class KB:
    def __init__(self, nt_prompt=32, n_dec=4, layers=2, npool=2560):
        self.NPOOL = npool
        self.NT = nt_prompt
        self.NDEC = n_dec
        self.LAYERS = layers
        self.nc = bass.Bass("TRN2", target_bir_lowering=False)
        self.st = ExitStack()
        self.out_views = []

    def sb(self, name, shape, dt=F32):
        t = self.st.enter_context(self.nc.sbuf_tensor(name, list(shape), dt))
        return V(t[:], Buf())

    def ps(self, name, shape, dt=F32):
        t = self.st.enter_context(self.nc.psum_tensor(name, list(shape), dt))
        return V(t[:], Buf(excl=True))

    def din(self, name, shape, dt=F32):
        return self.nc.dram_tensor(name, list(shape), dt, kind="ExternalInput").ap()

    def dout(self, name, shape, dt=F32):
        return self.nc.dram_tensor(name, list(shape), dt, kind="ExternalOutput").ap()

    def dint(self, name, shape, dt=F32):
        return self.nc.dram_tensor(name, list(shape), dt, kind="Internal").ap()

    def build(self):
        nc = self.nc
        NT = self.NT
        with self.st:
            em = self.em = Em(nc, self.st)
            self.declare_io()
            self.setup_consts()
            self.alloc_prompt_bufs()
            for layer in range(self.LAYERS):
                self.load_layer_weights(layer)
                if NT > 0:
                    self.prompt_layer(layer)
                if self.NDEC > 0:
                    self.dec_layer(layer)
            em._deps("sp", [self.out_tok], [])
            for k in list(em.cnt):
                if k.startswith("d_") and em.cnt[k] > 0:
                    nc.sync.wait_ge(em.sem[k], em.cnt[k])
        return nc

    def declare_io(self):
        T = self.NT * 128
        self.T = T
        d = self
        self.x_prompt = d.din("x_prompt", [SEQ, D])
        self.w_in = d.din("w_in", [DEPTH, D, D_IN])
        self.w_out = d.din("w_out", [DEPTH, D, D])
        self.norm_g = d.din("norm_g", [DEPTH, D])
        self.gla_w_gate = d.din("gla_w_gate", [DEPTH, 16, 128])
        self.gla_b_gate = d.din("gla_b_gate", [DEPTH, 128])
        self.gla_norm_g = d.din("gla_norm_g", [DEPTH, 64])
        self.nsa_q_norm_g = d.din("nsa_q_norm_g", [DEPTH, 64])
        self.nsa_k_norm_g = d.din("nsa_k_norm_g", [DEPTH, 3, 64])
        self.nsa_cmp_pos = d.din("nsa_cmp_pos", [DEPTH, 2, 32, 64])
        self.nsa_cmp_w = d.din("nsa_cmp_w", [DEPTH, 2, 32, 64, 64])
        self.ml_conv_w = d.din("ml_conv_w", [DEPTH, 4, 512])
        self.ml_conv_b = d.din("ml_conv_b", [DEPTH, 512])
        self.ml_gate_b = d.din("ml_gate_b", [DEPTH, 8])
        self.ml_norm_g = d.din("ml_norm_g", [DEPTH, 64])
        self.c_rope = d.din("c_rope", [SEQ, 16])
        self.c_ovl = d.din("c_ovl", [256, 64])
        self.c_selb = d.din("c_selb", [32, 128, 64])
        self.y_prompt = d.dout("y_prompt", [SEQ, D])
        self.p_rows = d.dout("p_rows", [DEPTH, SEQ, 512])
        self.p_win = d.dout("p_win", [DEPTH, 512, 256])
        self.p_gla = d.dout("p_gla", [DEPTH, 128, 64])
        self.p_c = d.dout("p_c", [DEPTH, 4, 64, 64])
        self.p_n = d.dout("p_n", [DEPTH, 4, 64])
        self.p_m = d.dout("p_m", [DEPTH, 4])
        self.p_conv = d.dout("p_conv", [DEPTH, 3, 512])
        NPOOL = self.NPOOL
        self.x_sample = d.din("x_sample", [4, 8, D])
        self.cache = [d.din("cache_nsa_kv%d" % l_, [NPOOL * 128 * 2, 256]) for l_ in range(DEPTH)]
        self.st_win = d.din("state_nsa_win", [DEPTH, 4, 512, 256])
        self.st_gla = d.din("state_gla", [DEPTH, 4, 128, 64])
        self.st_c = d.din("state_mlstm_C", [DEPTH, 4, 4, 64, 64])
        self.st_n = d.din("state_mlstm_n", [DEPTH, 4, 4, 64])
        self.st_m = d.din("state_mlstm_m", [DEPTH, 4, 4])
        self.st_conv = d.din("state_mlstm_conv", [DEPTH, 4, 3, 512])
        self.page_table = d.din("page_table", [4, 64], I32)
        self.c_rope_d = d.din("c_rope_d", [8, 16])
        self.c_ovl_d = d.din("c_ovl_d", [512, 130])
        self.c_selb_d = d.din("c_selb_d", [8, 136])
        self.y_sample = d.dout("y_sample", [4, 8, D])
        self.s_rows = d.dout("s_rows", [DEPTH, 4, 8, 512])
        self.s_win = d.dout("s_win", [DEPTH, 4, 512, 256])
        self.s_gla = d.dout("s_gla", [DEPTH, 4, 128, 64])
        self.s_c = d.dout("s_c", [DEPTH, 4, 4, 64, 64])
        self.s_n = d.dout("s_n", [DEPTH, 4, 4, 64])
        self.s_m = d.dout("s_m", [DEPTH, 4, 4])
        self.s_conv = d.dout("s_conv", [DEPTH, 4, 3, 512])
        self.yd1 = d.dint("yd1_scratch", [4, 8, D])
        self.y1 = d.dint("y1_scratch", [SEQ, D])
        self.out_tok = V(None, Buf())

    def setup_consts(self):
        em = self.em
        sb = self.sb
        c = self.c = {}
        stage = self.stage = sb("w_stage", [128, 1024])

        c["ident_f"] = sb("ident_f", [128, 128])
        em.memset(c["ident_f"], 0.0)
        em.asel(c["ident_f"], c["ident_f"], [[-1, 128]], ALU.not_equal, 1.0, 0, 1)
        c["ident_b"] = sb("ident_b", [128, 128], BF16)
        em.cp(c["ident_b"], c["ident_f"])
        c["U"] = sb("U", [128, 128])
        em.memset(c["U"], 1.0)
        em.asel(c["U"], c["U"], [[1, 128]], ALU.is_ge, 0.0, 0, -1)
        c["ones"] = sb("ones", [128, 128])
        em.memset(c["ones"], 1.0)
        c["negm"] = sb("negm", [128, 128])
        em.memset(c["negm"], 0.0)
        em.asel(c["negm"], c["negm"], [[-1, 128]], ALU.is_ge, NEG, 0, 1)
        tmpf = stage[:, 0:128]
        tmpf2 = stage[:, 128:256]
        em.memset(tmpf, 0.0)
        em.asel(tmpf, tmpf, [[1, 128]], ALU.is_ge, NEG, 0, -1)
        c["causal4"] = sb("causal4", [128, 4, 128], BF16)
        for h in range(4):
            em.cp(c["causal4"][:, h, :], tmpf)
        em.memset(tmpf2, 0.0)
        em.asel(tmpf2, tmpf2, [[-1, 128]], ALU.is_ge, NEG, -1, 1)
        c["anti4"] = sb("anti4", [128, 4, 128], BF16)
        for h in range(4):
            em.cp(c["anti4"][:, h, :], tmpf2)
        c["hmask"] = sb("hmask", [128, 4])
        em.memset(c["hmask"], 1.0)
        em.asel(c["hmask"], c["hmask"], [[-32, 4]], ALU.is_ge, 0.0, 0, 1)
        em.asel(c["hmask"], c["hmask"], [[32, 4]], ALU.is_ge, 0.0, 31, -1)
        c["pmask"] = sb("pmask", [128, 2])
        em.tt(c["pmask"][:, 0:1], c["hmask"][:, 0:1], c["hmask"][:, 2:3], ALU.add)
        em.tt(c["pmask"][:, 1:2], c["hmask"][:, 1:2], c["hmask"][:, 3:4], ALU.add)
        c["pmask2"] = sb("pmask2", [128, 2])
        em.tt(c["pmask2"][:, 0:1], c["hmask"][:, 0:1], c["hmask"][:, 1:2], ALU.add)
        em.tt(c["pmask2"][:, 1:2], c["hmask"][:, 2:3], c["hmask"][:, 3:4], ALU.add)
        c["ident4"] = sb("ident4", [128, 4, 128], BF16)
        for h in range(4):
            em.cp(c["ident4"][:, h, :], c["ident_f"])
        ii = self.iota_i32 = sb("iota_i32", [128, 128], I32)
        em.op("pool", lambda e: e.iota(ii.ap, pattern=[[1, 128]], base=0, channel_multiplier=0), writes=[ii])
        c["iota_i"] = sb("iota_i", [128, 128])
        em.cp(c["iota_i"], ii)
        pc_ = sb("pcol_i32", [128, 1], I32)
        em.op("pool", lambda e: e.iota(pc_.ap, pattern=[[0, 1]], base=31, channel_multiplier=16), writes=[pc_])
        c["pcol"] = sb("pcol", [128, 1])
        em.cp(c["pcol"], pc_)
        c["ovl"] = sb("ovl", [128, 2, 64])
        em.dma("sp", c["ovl"], self.c_ovl.rearrange("(t p) j -> p t j", p=128), serial=True)
        c["ovl_b"] = sb("ovl_b", [128, 2, 64], BF16)
        em.cp(c["ovl_b"], c["ovl"])
        c["rope"] = sb("rope", [128, 16])
        self.pA = self.ps("pA", [128, 512])
        self.pB = self.ps("pB", [128, 512])
        self.pT0 = self.ps("pT0", [128, 1024], BF16)
        self.pT1 = self.ps("pT1", [128, 1024], BF16)
        self.pS0 = self.ps("pS0", [128, 512])
        self.pS1 = self.ps("pS1", [128, 512])
        self.pO0 = self.ps("pO0", [128, 512])
        self.pO1 = self.ps("pO1", [128, 512])
        self._pab = 0
        self._pt = 0
        self._psx = 0
        self._po = 0
        w = self.w = {}
        w["win"] = sb("w_in_b", [128, 8, D_IN], BF16)
        w["wout"] = sb("w_out_b", [128, 8, D], BF16)
        w["stage"] = self.stage
        w["normg"] = sb("normg_bc", [128, D])
        w["wgate"] = sb("wgate_b", [16, 128], BF16)
        w["bgate"] = sb("bgate_bc", [128, 128])
        w["glag"] = sb("glag_bc", [128, 64])
        w["qg"] = sb("qg_bc", [128, 64])
        w["kg"] = sb("kg_bc", [128, 3, 64])
        w["mlg"] = sb("mlg_bc", [128, 64])
        w["gateb"] = sb("gateb_bc", [128, 8])
        w["convw"] = sb("convw_col", [128, 4, 4])
        w["convb"] = sb("convb_col", [128, 4])
        w["cmpw"] = sb("cmpw_blk", [128, 2, 32, 128], BF16)
        w["cmpb_tm"] = sb("cmpb_tm", [8, 2, 128])

    def nextAB(self):
        self._pab ^= 1
        return self.pA if self._pab else self.pB

    def nextT(self):
        self._pt ^= 1
        return self.pT0 if self._pt else self.pT1

    def nextS(self):
        self._psx ^= 1
        return self.pS0 if self._psx else self.pS1

    def nextO(self):
        self._po ^= 1
        return self.pO0 if self._po else self.pO1

    def load_layer_weights(self, L):
        em, w, c = self.em, self.w, self.c
        em.redirect_pool = False
        win_v = self.w_in[L].rearrange("(k p) n -> k p n", p=128)
        for k in range(8):
            for c0 in range(0, D_IN, 1024):
                cw = min(1024, D_IN - c0)
                em.dma("sp" if k % 2 == 0 else "act", w["stage"][:, 0:cw], win_v[k, :, c0:c0 + cw], lane="wst", serial=True)
                em.cp(w["win"][:, k, c0:c0 + cw], w["stage"][:, 0:cw], eng=("dve" if k % 2 == 0 else "pool"))
        wout_v = self.w_out[L].rearrange("(k p) n -> k p n", p=128)
        for k in range(8):
            em.dma("sp" if k % 2 == 0 else "act", w["stage"][:, 0:D], wout_v[k], lane="wst", serial=True)
            em.cp(w["wout"][:, k, :], w["stage"][:, 0:D], eng=("dve" if k % 2 == 0 else "pool"))
        em.dma("sp", w["normg"], self.norm_g[L:L + 1, :].to_broadcast([128, D]), serial=True)
        em.dma("sp", w["bgate"], self.gla_b_gate[L:L + 1, :].to_broadcast([128, 128]), serial=True)
        em.dma("sp", w["glag"], self.gla_norm_g[L:L + 1, :].to_broadcast([128, 64]), serial=True)
        em.dma("sp", w["qg"], self.nsa_q_norm_g[L:L + 1, :].to_broadcast([128, 64]), serial=True)
        em.dma("sp", w["kg"], self.nsa_k_norm_g[L:L + 1].to_broadcast([128, 3, 64]), serial=True)
        em.dma("sp", w["mlg"], self.ml_norm_g[L:L + 1, :].to_broadcast([128, 64]), serial=True)
        em.dma("sp", w["gateb"], self.ml_gate_b[L:L + 1, :].to_broadcast([128, 8]), serial=True)
        em.dma("sp", w["stage"][0:16, 0:128], self.gla_w_gate[L], serial=True)
        em.cp(w["wgate"], w["stage"][0:16, 0:128])
        with self.nc.allow_non_contiguous_dma("tiny per-feature column loads"):
            for tap in range(4):
                em.dma("sp", w["convw"][:, :, tap], self.ml_conv_w[L, tap].rearrange("(c p) -> p c", p=128), serial=True)
            em.dma("sp", w["convb"], self.ml_conv_b[L].rearrange("(c p) -> p c", p=128), serial=True)
        em.memset(w["cmpw"], 0.0)
        for kv in range(2):
            for g in range(2):
                for sh in range(2):
                    stg = w["stage"][g * 64:(g + 1) * 64, 0:1024].re("p (s e) -> p s e", e=64)
                    em.dma("sp", stg, self.nsa_cmp_w[L, kv, sh * 16:(sh + 1) * 16].rearrange("s d e -> d s e"), serial=True)
                    em.cp(w["cmpw"][g * 64:(g + 1) * 64, kv, sh * 16:(sh + 1) * 16, g * 64:(g + 1) * 64], stg)
        with self.nc.allow_non_contiguous_dma("tiny transposed pos-emb load"):
            for g in range(2):
                for kk in range(2):
                    em.dma("sp", w["stage"][g * 64:(g + 1) * 64, kk * 32:(kk + 1) * 32],
                           self.nsa_cmp_pos[L, kk].rearrange("s d -> d s"), serial=True)
        for r in range(8):
            em.cp(w["pe_rep"][:, :, :, r], w["stage"][:, 0:64].re("p (k s) -> p k s", k=2))
        for kv in range(2):
            pp = self.nextAB()
            for s in range(32):
                em.mm(pp[0:8, 0:128], w["pe_rep"][:, kv, s, :], w["cmpw"][:, kv, s, :], start=(s == 0), stop=(s == 31))
            em.cp(w["cmpb_tm"][:, kv, :], pp[0:8, 0:128])

    def prompt_layer(self, L):
        em, w, c, sb = self.em, self.w, self.c, self.sb
        NT = self.NT
        P = 128
        last = (L == self.LAYERS - 1)
        b = self.pb
        em.memset(b["S"], 0.0)
        em.memset(b["S_b"], 0.0)
        em.memset(b["Ct"], 0.0)
        em.memset(b["Ct_b"], 0.0)
        em.memset(b["Mrep"], 0.0)
        em.memset(b["qkT"], 0.0)
        em.memset(b["rawT"], 0.0)
        em.memset(b["KcT"], 0.0)
        em.memset(b["VcT"], 0.0)
        em.memset(b["first_prev"], 0.0)
        if L > 0:
            if self.NDEC > 0:
                em.memset(self.dec_allk, 0.0)
                em.memset(self.dec_allv, 0.0)
                em.memset(b["Vslc"][:, :, :, 64:65].wb(self.dec_allv.bs), 1.0)
            else:
                for k_ in ("KslcT", "Vslc"):
                    vv = b[k_].wb(list(b[k_].bs) + self.hb[k_])
                    em.memset(vv, 0.0)
                em.memset(b["Vslc"][:, :, :, 64:65].wb(list(b["Vslc"].bs) + self.hb["Vslc"]), 1.0)
        if not hasattr(self, "y1_v"):
            self.y1_v = [V(self.y1[t_ * 128:(t_ + 1) * 128, :], Buf()) for t_ in range(32)]
        x_src = self.x_prompt if L == 0 else self.y1_v
        y_dst = self.y_prompt if last else self.y1_v
        for t in range(NT):
            self.tile_step(L, t, x_src, y_dst)
        self.write_states(L)

    def alloc_prompt_bufs(self):
        sb = self.sb
        b = self.pb = {}
        b["x"] = sb("x_t", [128, D])
        b["xn"] = sb("xn_b", [128, D], BF16)
        self.w["pe_rep"] = b["xn"][:, 0:512].re("p (k s r) -> p k s r", k=2, s=32)
        b["xnT"] = sb("xnT", [128, 8, 128], BF16)
        b["u"] = sb("u_t", [128, D_IN])
        b["st1"] = sb("stat1", [128, 64])
        b["S"] = sb("gla_S", [128, 64])
        b["S_b"] = sb("gla_S_b", [128, 64], BF16)
        b["ga_b"] = sb("ga_b", [128, 16], BF16)
        b["gaT"] = sb("gaT", [16, 128], BF16)
        b["g1"] = sb("gla_g1", [128, 128])
        b["loga"] = sb("gla_loga", [128, 128])
        b["eb"] = sb("gla_eb", [128, 128])
        b["enb"] = sb("gla_enb", [128, 128])
        b["ekb"] = sb("gla_ekb", [128, 128])
        b["edec"] = sb("gla_edec", [128, 1])
        b["qt_"] = sb("gla_qt", [128, 128], BF16)
        b["kt_"] = sb("gla_kt", [128, 128], BF16)
        b["kh_"] = sb("gla_kh", [128, 128], BF16)
        b["qT_"] = sb("gla_qT", [128, 4, 128], BF16)
        b["kT_"] = sb("gla_kT", [128, 128], BF16)
        b["gv"] = sb("gla_v", [128, 256], BF16)
        b["At"] = sb("gla_At", [128, 4, 128], BF16)
        b["dSc"] = sb("gla_dSc", [128, 64])
        b["go"] = sb("gla_o", [128, 4, 64])
        b["mixed"] = sb("mixed", [128, D], BF16)
        b["mixT"] = b["xnT"]
        b["nrm_sq"] = sb("nrm_sq", [128, 512])
        b["nrm_ss"] = sb("nrm_ss", [128, 8])
        b["silu"] = sb("silu_t", [128, 512])
        b["qkT"] = sb("ml_qkT", [128, 4, 3 + 128])
        b["conv"] = b["silu"].re("p (c i) -> p c i", c=4)
        b["qkc"] = sb("ml_qkc", [128, 4, 128], BF16)
        b["qm"] = sb("ml_qm", [128, 4, 128], BF16)
        b["k_tm"] = sb("ml_k_tm", [128, 256], BF16)
        b["vext"] = sb("ml_vext", [128, 4, 65], BF16)
        b["vw"] = sb("ml_vw", [128, 4, 65], BF16)
        b["gates"] = sb("ml_gates", [128, 8])
        b["fi"] = sb("ml_fi", [128, 4])
        b["F"] = sb("ml_F", [128, 4])
        b["gj"] = sb("ml_g", [128, 4])
        b["diag"] = None
        b["tmp"] = sb("ml_tmp", [128, 4, 128])
        b["dSm"] = b["tmp"].re("p c i -> p (c i)")[:, 0:256].re("p (h e) -> p h e", h=4)
        b["Dm"] = b["tmp"]
        b["sij"] = sb("ml_sij", [128, 4, 128], BF16)
        b["sijT"] = sb("ml_sijT", [128, 4, 128], BF16)
        b["mx"] = sb("ml_mx", [128, 4])
        b["nmx"] = sb("ml_nmx", [128, 4])
        b["m"] = sb("ml_m", [128, 8])
        b["wint"] = sb("ml_wint", [128, 4])
        b["Mrep"] = sb("ml_Mrep", [128, 4])
        b["lastrep"] = sb("ml_lastrep", [128, 8])
        b["decay"] = sb("ml_decay", [128, 4])
        b["wj"] = sb("ml_wj", [128, 4])
        b["Ct"] = sb("ml_Ct", [128, 4, 65])
        b["Ct_b"] = sb("ml_Ct_b", [128, 4, 65], BF16)
        b["num"] = sb("ml_num", [128, 4, 65])
        b["den"] = sb("ml_den", [128, 4])
        b["den2"] = sb("ml_den2", [128, 4])
        b["hh"] = sb("ml_h", [128, 4, 64])
        b["sel_last"] = sb("sel_last", [128, 128])
        em = self.em
        em.memset(b["sel_last"], 0.0)
        em.asel(b["sel_last"], b["sel_last"], [[0, 128]], ALU.not_equal, 1.0, -127, 1)
        b["sel_last8"] = self.sb("sel_last8", [8, 128])
        em.memset(b["sel_last8"], 0.0)
        em.asel(b["sel_last8"], b["sel_last8"], [[0, 128]], ALU.not_equal, 1.0, -7, 1)
        b["qn"] = b["silu"].re("p (h e) -> p h e", h=8)
        b["qr"] = sb("nsa_qr", [128, 8, 64], BF16)
        b["qn_b"] = sb("nsa_qn_b", [128, 8, 64], BF16)
        b["rt1"] = sb("rope_t1", [128, 8, 8])
        b["rt2"] = sb("rope_t2", [128, 8, 8])
        b["rows"] = sb("nsa_rows", [128, 4, 128])
        b["win"] = sb("nsa_win", [128, 2, 128])
        b["kk_b"] = sb("nsa_kk_b", [128, 4, 128], BF16)
        b["QTr"] = sb("nsa_QTr", [128, 2, 4, 128], BF16)
        b["QTn"] = sb("nsa_QTn", [128, 2, 4, 128], BF16)
        b["KslcT"] = sb("KslcT", [128, SEQ], BF16)
        b["KwinT"] = sb("KwinT", [128, 5 * 128], BF16)
        b["Vslc"] = sb("Vslc", [128, 32, 2, 65], BF16)
        b["Vwin"] = sb("Vwin", [128, 5, 2, 65], BF16)
        b["rawT"] = sb("rawT", [128, 2, 16 + 128], BF16)
        b["KcT"] = sb("KcT", [128, 256], BF16)
        b["VcT"] = sb("VcT", [128, 256], BF16)
        b["Vc"] = sb("Vc", [128, 2, 2, 129], BF16)
        b["kc_tm"] = sb("kc_tm", [8, 2, 128])
        b["kc_n"] = sb("kc_n", [8, 2, 64], BF16)
        b["vc_b"] = sb("vc_b", [8, 128], BF16)
        b["first_prev"] = sb("first_prev", [8, 1])
        b["cmpmask"] = sb("cmpmask", [128, 128])
        b["cmpmask4"] = sb("cmpmask4", [128, 4, 128], BF16)
        b["PT"] = sb("PT", [128, 4, 128], BF16)
        b["PT2"] = sb("PT2", [128, 4, 128], BF16)
        b["ocmp"] = b["tmp"].re("p c i -> p (c i)").re("p (h e) -> p h e", h=8)
        b["rden"] = sb("rden", [128, 8])
        b["imp"] = sb("imp", [128, 2, 64])
        b["impt"] = sb("impt", [128, 4, 64])
        b["selb"] = sb("selb", [128, 64])
        b["score"] = sb("score", [128, 2, 64])
        b["score2"] = sb("score2", [128, 64])
        b["max8"] = sb("max8", [128, 8])
        b["selbias"] = sb("selbias", [128, 2, 64], BF16)
        b["gate"] = sb("nsa_gate", [128, 24])
        b["selx"] = sb("selx", [128, 2, 2, 64], BF16)
        b["ob"] = sb("nsa_ob", [128, 8, 64])
        b["diag"] = b["ob"].re("p h e -> p (h e)").re("p (c i) -> p c i", c=4)
        self.hb = {k: [Buf() for _ in range(33)] for k in ("KslcT", "KwinT", "Vslc", "Vwin")}
        def allb(k):
            return b[k].wb(list(b[k].bs) + self.hb[k])
        em.memset(allb("Vslc"), 0.0)
        em.memset(allb("Vwin"), 0.0)
        em.memset(allb("Vslc")[:, :, :, 64:65], 1.0)
        em.memset(allb("Vwin")[:, :, :, 64:65], 1.0)
        em.memset(allb("KslcT"), 0.0)
        em.memset(allb("KwinT"), 0.0)
        em.memset(b["Vc"], 0.0)
        em.memset(b["Vc"][:, :, :, 64:65], 1.0)
        for g in range(2):
            em.cp(b["Vc"][:, :, g, 65:129], self.c["ovl_b"])
        em.memset(b["vext"][:, :, 64:65], 1.0)
        em.memset(b["mixed"], 0.0)

    def rms_heads(self, out, in_, H, gain_bc, P=128, extra_scale=1.0):
        em, b = self.em, self.pb
        sq = b["nrm_sq"][0:P, 0:H * 64].re("p (h e) -> p h e", e=64)
        ss = b["nrm_ss"][0:P, 0:H]
        em.tt(sq, in_, in_, ALU.mult)
        em.red(ss, sq)
        em.ts(ss, ss, 1.0 / 64, ALU.mult, EPS, ALU.add)
        em.act(ss, ss, AF.Sqrt)
        em.recip(ss, ss)
        em.tt(sq, in_, ss.un(2).bc([P, H, 64]), ALU.mult)
        if extra_scale != 1.0:
            em.stt(out, sq, extra_scale, gain_bc.un(1).bc([P, H, 64]), ALU.mult, ALU.mult)
        else:
            em.tt(out, sq, gain_bc.un(1).bc([P, H, 64]), ALU.mult)

    def rope(self, out, in_, H, cs, P=128):
        em, b = self.em, self.pb
        cos = cs[:, 0:8].un(1).bc([P, H, 8])
        sin = cs[:, 8:16].un(1).bc([P, H, 8])
        t1 = b["rt1"][0:P, 0:H, :]
        t2 = b["rt2"][0:P, 0:H, :]
        x1 = in_[:, :, 0:8]
        x2 = in_[:, :, 8:16]
        em.cp(out[:, :, 16:64], in_[:, :, 16:64], eng="pool")
        em.tt(t1, x1, cos, ALU.mult)
        em.tt(t2, x2, sin, ALU.mult)
        em.tt(out[:, :, 0:8], t1, t2, ALU.subtract)
        em.tt(t1, x2, cos, ALU.mult)
        em.tt(t2, x1, sin, ALU.mult)
        em.tt(out[:, :, 8:16], t1, t2, ALU.add)

    def out_dma(self, q, dst, src):
        self.em.dma(q, dst, src, lane="out_" + q)
        pass

    def tile_step(self, L, t, x_src, y_dst):
        em, w, c, b = self.em, self.w, self.c, self.pb
        em.redirect_pool = True
        P = 128
        r0 = t * 128
        last_tile = (t == self.NT - 1)
        x, u = b["x"], b["u"]
        em.dma("sp", x, x_src[r0:r0 + 128, :] if not isinstance(x_src, list) else x_src[t], lane="x")
        ss = b["st1"][:, 0:1]
        em.act(u[:, 0:D], x, AF.Square, accum=ss)
        em.ts(ss, ss, 1.0 / D, ALU.mult, EPS, ALU.add)
        em.act(ss, ss, AF.Sqrt)
        em.recip(ss, ss)
        em.stt(b["xn"], x, ss, w["normg"], ALU.mult, ALU.mult)
        for k in range(8):
            pt = self.nextT()
            em.tr(pt[:, 0:128], b["xn"][:, k * 128:(k + 1) * 128], c["ident_b"])
            em.cp(b["xnT"][:, k, :], pt[:, 0:128], eng=("act" if k % 2 else "dve"))
        chunks = [(0, 512), (512, 512), (1024, 512), (1536, 512), (2048, 512), (2560, 40), (3112, 512), (3624, 264)]
        for ci, (c0, cw) in enumerate(chunks):
            pp = self.nextAB()
            for k in range(8):
                em.mm(pp[:, 0:cw], b["xnT"][:, k, :], w["win"][:, k, c0:c0 + cw], start=(k == 0), stop=(k == 7))
            em.cp(u[:, c0:c0 + cw], pp[:, 0:cw], eng=("act" if ci % 2 else "dve"))
        em.cp(b["qkT"][:, :, 0:3], b["qkT"][:, :, 128:131], eng="pool")
        for cch in range(4):
            pp = self.nextAB()
            for k in range(8):
                em.mm(pp[:, 0:128], w["win"][:, k, O_MQK + cch * 128:O_MQK + (cch + 1) * 128], b["xnT"][:, k, :],
                      start=(k == 0), stop=(k == 7))
            em.cp(b["qkT"][:, cch, 3:131], pp[:, 0:128], eng=("act" if cch % 2 else "dve"))
        import os as _os
        if not _os.environ.get("SKIP_GLA"):
            self.gla_tile(L, P)
        if not _os.environ.get("SKIP_ML"):
            self.mlstm_tile(L, P)
        if not _os.environ.get("SKIP_NSA"):
            self.nsa_tile(L, t)
        for k in range(8):
            pt = self.nextT()
            em.tr(pt[:, 0:128], b["mixed"][:, k * 128:(k + 1) * 128], c["ident_b"])
            em.cp(b["mixT"][:, k, :], pt[:, 0:128], eng=("act" if k % 2 else "dve"))
        for half in range(2):
            pp = self.nextAB()
            for k in range(8):
                em.mm(pp[:, 0:512], b["mixT"][:, k, :], w["wout"][:, k, half * 512:(half + 1) * 512],
                      start=(k == 0), stop=(k == 7))
            em.tt(x[:, half * 512:(half + 1) * 512], pp[:, 0:512], x[:, half * 512:(half + 1) * 512], ALU.add)
        self.out_dma("sp", y_dst[r0:r0 + 128, :] if not isinstance(y_dst, list) else y_dst[t], x)
        if last_tile:
            with self.nc.allow_non_contiguous_dma("tiny conv state"):
                for r in range(3):
                    self.out_dma("sp", self.p_conv[L, r].rearrange("(c p) -> p c", p=128), b["qkT"][:, :, 128 + r])

    def gla_tile(self, L, P):
        em, w, c, b = self.em, self.w, self.c, self.pb
        u = b["u"]
        em.cp(b["ga_b"][0:P], u[0:P, O_GA:O_GA + 16])
        pt = self.nextT()
        em.tr(pt[0:16, 0:P], b["ga_b"][0:P], c["ident_b"][0:P, 0:P])
        em.cp(b["gaT"][:, 0:P], pt[0:16, 0:P])
        pp = self.nextAB()
        em.mm(pp[0:P, 0:128], b["gaT"][:, 0:P], w["wgate"])
        g1 = b["g1"][0:P]
        em.tt(g1, pp[0:P, 0:128], w["bgate"][0:P], ALU.add)
        em.act(g1, g1, AF.Exp, scale=-1.0)
        em.act(g1, g1, AF.Ln, bias=1.0)
        loga = b["loga"][0:P]
        em.ts(loga, g1, -1.0 / 16.0, ALU.mult)
        pp = self.nextAB()
        em.mm(pp[0:P, 0:128], c["U"][0:P, 0:P], loga)
        em.mm(pp[0:P, 128:256], c["ones"][0:P, 0:P], loga)
        em.mm(pp[:, 256:257], loga, c["ones"][0:P, 0:1])
        em.act(b["eb"][0:P], pp[0:P, 0:128], AF.Exp)
        em.act(b["enb"][0:P], pp[0:P, 0:128], AF.Exp, scale=-1.0)
        em.cp(g1, pp[0:P, 0:128], eng="act")
        em.tt(b["ekb"][0:P], pp[0:P, 128:256], g1, ALU.subtract)
        em.act(b["ekb"][0:P], b["ekb"][0:P], AF.Exp)
        em.act(b["edec"], pp[:, 256:257], AF.Exp)
        em.stt(b["qt_"][0:P], u[0:P, O_GQ:O_GQ + 128], 32 ** -0.5, b["eb"][0:P], ALU.mult, ALU.mult)
        em.tt(b["kt_"][0:P], u[0:P, O_GK:O_GK + 128], b["enb"][0:P], ALU.mult)
        em.tt(b["kh_"][0:P], u[0:P, O_GK:O_GK + 128], b["ekb"][0:P], ALU.mult)
        em.cp(b["gv"][0:P], u[0:P, O_GV:O_GV + 256], eng="pool")
        pt = self.nextT()
        em.tr(pt[:, 0:P], b["qt_"][0:P], c["ident_b"][0:P, 0:P])
        em.tr(pt[:, 128:128 + P], b["kt_"][0:P], c["ident_b"][0:P, 0:P])
        for h in range(4):
            em.ts(b["qT_"][:, h, 0:P], pt[:, 0:P], c["hmask"][:, h:h + 1], ALU.mult)
        em.cp(b["kT_"][:, 0:P], pt[:, 128:128 + P])
        pa = self.nextAB()
        pav = pa[0:P, 0:4 * P].re("p (h i) -> p h i", h=4)
        for h in range(4):
            em.mm(pav[:, h, :], b["kT_"][:, 0:P], b["qT_"][:, h, 0:P])
        At = b["At"][0:P, :, 0:P]
        em.tt(At, pav, c["U"][0:P, 0:P].un(1).bc([P, 4, P]), ALU.mult)
        po = self.nextAB()
        pov = po[0:P, 0:256].re("p (h e) -> p h e", h=4)
        for h in range(4):
            em.mm(pov[:, h, :], At[:, h, :], b["gv"][0:P, h * 64:(h + 1) * 64], start=True, stop=False)
            em.mm(pov[:, h, :], b["qT_"][:, h, 0:P], b["S_b"], start=False, stop=True)
        go = b["go"][0:P]
        em.cp(go, pov)
        pd = self.nextAB()
        em.mm(pd[:, 0:256], b["kh_"][0:P], b["gv"][0:P])
        em.tt(b["dSm"], pd[:, 0:256].re("p (h e) -> p h e", h=4), c["hmask"].un(2).bc([128, 4, 64]), ALU.mult)
        em.red(b["dSc"], b["dSm"].re("p h e -> p e h"))
        em.stt(b["S"], b["S"], b["edec"], b["dSc"], ALU.mult, ALU.add)
        em.cp(b["S_b"], b["S"], eng="act")
        mixv = b["mixed"][0:P, 0:256].re("p (h e) -> p h e", h=4)
        sl = b["silu"][0:P, 0:256]
        em.act(sl, u[0:P, O_GZ:O_GZ + 256], AF.Silu)
        self.rms_heads(go, go, 4, w["glag"][0:P], P)
        em.tt(mixv, go, sl.re("p (h e) -> p h e", h=4), ALU.mult)

    def mlstm_tile(self, L, P):
        em, w, c, b = self.em, self.w, self.c, self.pb
        u = b["u"]
        for cch in range(4):
            cv = b["conv"][:, cch, 0:P]
            em.ts(cv, b["qkT"][:, cch, 0:P], w["convw"][:, cch, 0:1], ALU.mult)
            for tap in range(1, 4):
                em.stt(cv, b["qkT"][:, cch, tap:tap + P], w["convw"][:, cch, tap:tap + 1], cv, ALU.mult, ALU.add)
            em.act(b["qkc"][:, cch, 0:P], cv, AF.Silu, bias=w["convb"][:, cch:cch + 1])
        for h in range(4):
            em.ts(b["qm"][:, h, 0:P], b["qkc"][:, h // 2, 0:P], c["pmask2"][:, h % 2:h % 2 + 1], ALU.mult)
        pt = self.nextT()
        for j in range(2):
            em.tr(pt[0:P, j * 128:(j + 1) * 128], b["qkc"][:, 2 + j, 0:P], c["ident_b"])
        em.ts(b["k_tm"][0:P], pt[0:P, 0:256], 0.125, ALU.mult)
        em.cp(b["vext"][0:P, :, 0:64], u[0:P, O_MV:O_MV + 256].re("p (h e) -> p h e", h=4), eng="pool")
        gt = b["gates"][0:P]
        em.tt(gt, u[0:P, O_MIF:O_MIF + 8], w["gateb"][0:P], ALU.add)
        fi = b["fi"][0:P]
        em.act(fi, gt[:, 4:8], AF.Exp, scale=-1.0)
        em.act(fi, fi, AF.Ln, bias=1.0)
        em.ts(fi, fi, -1.0, ALU.mult)
        pp = self.nextAB()
        em.mm(pp[0:P, 0:4], c["U"][0:P, 0:P], fi)
        Fm = b["m"][0:P]
        em.cp(Fm[:, 0:4], pp[0:P, 0:4])
        gj = b["gj"][0:P]
        em.tt(gj, gt[:, 0:4], Fm[:, 0:4], ALU.subtract)
        dg = b["diag"][0:P, :, 0:P]
        em.tt(dg, c["ident_f"][0:P, 0:P].un(1).bc([P, 4, P]), gj.un(2).bc([P, 4, P]), ALU.mult)
        pg = self.nextAB()
        pgv = pg[0:P, 0:4 * P].re("p (h j) -> p h j", h=4)
        em.mm(pgv, c["ones"][0:P, 0:P], dg)
        tmp = b["tmp"][0:P, :, 0:P]
        em.tt(tmp, pgv, c["negm"][0:P, 0:P].un(1).bc([P, 4, P]), ALU.add)
        mx = b["mx"][0:P]
        em.red(mx, tmp, op=ALU.max)
        em.tt(mx, mx, b["Mrep"][0:P], ALU.max)
        em.tt(Fm[:, 4:8], Fm[:, 0:4], mx, ALU.add)
        nmx = b["nmx"][0:P]
        em.ts(nmx, mx, -1.0, ALU.mult)
        Dm = b["Dm"][0:P, :, 0:P]
        for h in range(4):
            em.act(Dm[:, h, :], tmp[:, h, :], AF.Exp, bias=nmx[:, h:h + 1])
        wint = b["wint"][0:P]
        em.tt(wint, b["Mrep"][0:P], mx, ALU.subtract)
        em.act(wint, wint, AF.Exp)
        psc = self.nextS()
        pscv = psc[0:P, 0:4 * P].re("p (h j) -> p h j", h=4)
        for h in range(4):
            em.mm(pscv[:, h, :], b["qm"][:, h, 0:P], b["qkc"][:, 2 + h // 2, 0:P])
        sij = b["sij"][0:P, :, 0:P]
        em.stt(sij, pscv, 0.125, Dm, ALU.mult, ALU.mult)
        pt = self.nextT()
        ptv = pt[0:P, 0:4 * P].re("p (h i) -> p h i", h=4)
        for h in range(4):
            em.tr(ptv[:, h, :], sij[:, h, :], c["ident_b"][0:P, 0:P])
        sijT = b["sijT"][0:P, :, 0:P]
        em.cp(sijT, ptv)
        pn = self.nextO()
        pnv = pn[0:P, 0:260].re("p (h e) -> p h e", h=4)
        pi_ = self.nextO()
        piv = pi_[0:P, 0:260].re("p (h e) -> p h e", h=4)
        for h in range(4):
            em.mm(pnv[:, h, :], sijT[:, h, :], b["vext"][0:P, h, :])
            em.mm(piv[:, h, :], b["qm"][:, h, 0:P], b["Ct_b"][:, h, :])
        num = b["num"][0:P]
        em.tt(num, piv, wint.un(2).bc([P, 4, 65]), ALU.mult)
        em.tt(num, num, pnv, ALU.add)
        den = b["den"][0:P]
        em.stt(den, num[:, :, 64], -1.0, num[:, :, 64], ALU.mult, ALU.max)
        den2 = b["den2"][0:P]
        em.act(den2, Fm[:, 4:8], AF.Exp, scale=-1.0)
        em.tt(den, den, den2, ALU.max)
        em.recip(den, den)
        hh = b["hh"][0:P]
        em.tt(hh, num[:, :, 0:64], den.un(2).bc([P, 4, 64]), ALU.mult)
        pl = self.nextAB()
        em.mm(pl[0:128, 0:8], (b["sel_last"] if P == 128 else b["sel_last8"])[0:P, :], Fm)
        lr = b["lastrep"]
        em.cp(lr, pl[:, 0:8])
        dec = b["decay"]
        em.tt(dec, lr[:, 0:4], b["Mrep"], ALU.add)
        em.tt(dec, dec, lr[:, 4:8], ALU.subtract)
        em.act(dec, dec, AF.Exp)
        wj = b["wj"][0:P]
        em.tt(wj, gj, lr[0:P, 0:4], ALU.add)
        em.tt(wj, wj, lr[0:P, 4:8], ALU.subtract)
        em.act(wj, wj, AF.Exp)
        em.tt(b["vw"][0:P], b["vext"][0:P], wj.un(2).bc([P, 4, 65]), ALU.mult)
        pc = self.nextAB()
        pcv = pc[:, 0:260].re("p (h e) -> p h e", h=4)
        for h in range(4):
            em.mm(pcv[:, h, :], b["k_tm"][0:P, (h // 2) * 128:(h // 2 + 1) * 128], b["vw"][0:P, h, :])
        em.tt(b["Ct"], b["Ct"], dec.un(2).bc([128, 4, 65]), ALU.mult)
        em.tt(b["Ct"], b["Ct"], pcv, ALU.add)
        em.cp(b["Ct_b"], b["Ct"], eng="act")
        em.cp(b["Mrep"], lr[:, 4:8])
        self.rms_heads(hh, hh, 4, w["mlg"][0:P], P)
        sl = b["silu"][0:P, 0:256]
        em.act(sl, u[0:P, O_MO:O_MO + 256], AF.Sigmoid)
        em.tt(hh, hh, sl.re("p (h e) -> p h e", h=4), ALU.mult)
        em.act(sl, u[0:P, O_MZ:O_MZ + 256], AF.Silu)
        em.tt(b["mixed"][0:P, 768:1024].re("p (h e) -> p h e", h=4), hh, sl.re("p (h e) -> p h e", h=4), ALU.mult)

    def attn_block(self, KT, QT, extra, Vr, PTb, acc, first):
        em = self.em
        ps_ = self.nextS()
        nk = KT.ap.shape[-1]
        psv = ps_[0:nk, 0:512]
        em.mm(psv, KT, QT.re("p h i -> p (h i)"), start=True, stop=(len(extra) == 0))
        for ei, (lt, rh) in enumerate(extra):
            em.mm(psv, lt, rh, start=False, stop=(ei == len(extra) - 1))
        em.act(PTb[0:nk].re("p h i -> p (h i)"), psv, AF.Exp, scale=ATT_SCALE)
        po = self.nextO()
        po_v = po[:, 0:260].re("p (h e) -> p h e", h=4)
        for hh in range(4):
            em.mm(po_v[:, hh, :], PTb[0:nk, hh, :], Vr, start=True, stop=True)
        if first:
            em.cp(acc, po_v)
        else:
            em.tt(acc, acc, po_v, ALU.add)

    def nsa_tile(self, L, t):
        import os as _os
        em, w, c, b, hb = self.em, self.w, self.c, self.pb, self.hb
        u = b["u"]
        P = 128
        r0 = t * 128
        cs = c["rope"]
        em.dma("sp", cs, self.c_rope[r0:r0 + 128, :], lane="cst")
        qn = b["qn"]
        self.rms_heads(qn, u[:, O_NQ:O_NQ + 512].re("p (h e) -> p h e", h=8), 8, w["qg"])
        em.cp(b["qn_b"].re("p (hh g) e -> p g hh e", g=2), qn.re("p (g hh) e -> p g hh e", g=2), eng="pool")
        self.rope(b["ocmp"], qn, 8, cs)
        em.cp(b["qr"].re("p (hh g) e -> p g hh e", g=2), b["ocmp"].re("p (g hh) e -> p g hh e", g=2))
        kv = u[:, O_NKV:O_NKV + 768].re("p (s g e) -> p s g e", s=6, g=2)
        rows = b["rows"].re("p s (g e) -> p s g e", g=2)
        winr = b["win"].re("p s (g e) -> p s g e", g=2)
        em.cp(rows[:, 0], kv[:, 0], eng="pool")
        em.cp(rows[:, 1], kv[:, 1], eng="pool")
        em.cp(rows[:, 3], kv[:, 3], eng="pool")
        em.cp(winr[:, 1], kv[:, 5], eng="pool")
        ktmp = b["ocmp"][:, 0:2, :]
        self.rms_heads(ktmp, kv[:, 2], 2, w["kg"][:, 1, :])
        self.rope(rows[:, 2], ktmp, 2, cs)
        self.rms_heads(ktmp, kv[:, 4], 2, w["kg"][:, 2, :])
        self.rope(winr[:, 0], ktmp, 2, cs)
        self.out_dma("act", self.p_rows[L, r0:r0 + 128, :], b["rows"].re("p s f -> p (s f)"))
        if r0 >= SEQ - 512:
            w0 = r0 - (SEQ - 512)
            self.out_dma("act", self.p_win[L, w0:w0 + 128, :], b["win"].re("p s f -> p (s f)"))
        kkb = b["kk_b"]
        em.cp(kkb[:, 0:2, :], b["rows"][:, 0:2, :])
        em.cp(kkb[:, 2, :], b["rows"][:, 2, :])
        em.cp(kkb[:, 3, :], b["win"][:, 0, :])
        em.cp(b["Vslc"][:, t, :, 0:64].wb(hb["Vslc"][t]), rows[:, 3], eng="pool")
        em.cp(b["Vwin"][:, t % 5, :, 0:64].wb(hb["Vwin"][t % 5]), winr[:, 1], eng="pool")
        pt = self.nextT()
        for j in range(4):
            em.tr(pt[:, j * 128:(j + 1) * 128], kkb[:, j, :], c["ident_b"])
        em.cp(b["rawT"][:, :, 0:16], b["rawT"][:, :, 128:144], eng="pool")
        em.cp(b["rawT"][:, :, 16:144], pt[:, 0:256].re("p (k i) -> p k i", k=2))
        em.cp(b["KslcT"][:, r0:r0 + 128].wb(hb["KslcT"][t]), pt[:, 256:384], eng="act")
        em.cp(b["KwinT"][:, (t % 5) * 128:(t % 5 + 1) * 128].wb(hb["KwinT"][t % 5]), pt[:, 384:512], eng="act")
        pt = self.nextT()
        qr4 = b["qr"].re("p (hh g) e -> p hh (g e)", g=2)
        qn4 = b["qn_b"].re("p (hh g) e -> p hh (g e)", g=2)
        for hh in range(4):
            em.tr(pt[:, hh * 128:(hh + 1) * 128], qr4[:, hh, :], c["ident_b"])
            em.tr(pt[:, 512 + hh * 128:512 + (hh + 1) * 128], qn4[:, hh, :], c["ident_b"])
        for g in range(2):
            em.ts(b["QTr"][:, g].re("p h i -> p (h i)"), pt[:, 0:512], c["pmask2"][:, g:g + 1], ALU.mult)
            em.ts(b["QTn"][:, g].re("p h i -> p (h i)"), pt[:, 512:1024], c["pmask2"][:, g:g + 1], ALU.mult)
        for kvi in range(2):
            pp = self.nextAB()
            for s in range(32):
                em.mm(pp[0:8, 0:128], b["rawT"][:, kvi, s:s + 113:16], w["cmpw"][:, kvi, s, :], start=(s == 0), stop=(s == 31))
            em.tt(b["kc_tm"][:, kvi, :], pp[0:8, 0:128], w["cmpb_tm"][:, kvi, :], ALU.add)
        self.rms_heads(b["kc_n"], b["kc_tm"][:, 0, :].re("p (g e) -> p g e", g=2), 2, w["kg"][0:8, 0, :], P=8)
        em.cp(b["vc_b"], b["kc_tm"][:, 1, :])
        pt = self.nextT()
        em.tr(pt[:, 0:8], b["kc_n"].re("p g e -> p (g e)"), c["ident_b"][0:8, 0:8])
        em.tr(pt[:, 8:16], b["vc_b"], c["ident_b"][0:8, 0:8])
        n0 = 8 * t - 1
        if t == 0:
            em.cp(b["KcT"][:, 0:7], pt[:, 1:8])
            em.cp(b["VcT"][:, 0:7], pt[:, 9:16])
        else:
            em.cp(b["KcT"][:, n0:n0 + 8], pt[:, 0:8])
            em.cp(b["VcT"][:, n0:n0 + 8], pt[:, 8:16])
        nvis = 8 * t + 7
        ntl = 1 if nvis <= 128 else 2
        fr = ntl - 1
        for rt in sorted({fr} | ({0} if t == 16 else set())):
            pt = self.nextT()
            em.tr(pt[:, 0:128], b["VcT"][:, rt * 128:(rt + 1) * 128], c["ident_b"])
            em.cp(b["Vc"][:, rt, :, 0:64], pt[:, 0:128].re("p (g e) -> p g e", g=2))
        for nt_ in range(ntl):
            if 16 * (nt_ * 128 + 127) + 31 <= 128 * t:
                continue
            em.ts(b["cmpmask"], c["iota_i"], c["pcol"], ALU.subtract, float(2048 * nt_ - 128 * t), ALU.is_lt)
            em.ts(b["cmpmask"], b["cmpmask"], NEG, ALU.mult)
            for hh in range(2):
                em.cp(b["cmpmask4"][:, nt_ * 2 + hh, :], b["cmpmask"], eng=("dve" if hh % 2 else "pool"))
        em.act(b["gate"], u[:, O_NG:O_NG + 24], AF.Sigmoid)
        for g in range(2):
            for half in range(2):
                pov = b["nrm_sq"][:, 0:258].re("p (h e) -> p h e", h=2)
                for nt_ in range(ntl):
                    ps_ = self.nextS()
                    psv = ps_[:, 0:256]
                    qsl = b["QTn"][:, g, half * 2:half * 2 + 2, :].re("p h i -> p (h i)")
                    masked = not (16 * (nt_ * 128 + 127) + 31 <= 128 * t)
                    em.mm(psv, b["KcT"][:, nt_ * 128:(nt_ + 1) * 128], qsl, start=True, stop=not masked)
                    if masked:
                        em.mm(psv, c["ident_b"], b["cmpmask4"][:, nt_ * 2:nt_ * 2 + 2, :].re("p h i -> p (h i)"), start=False, stop=True)
                    PTb = b["PT"] if nt_ == 0 else b["PT2"]
                    em.act(PTb[:, 0:2, :].re("p h i -> p (h i)"), psv, AF.Exp, scale=ATT_SCALE)
                    po = self.nextO()
                    pcv = po[:, 0:258].re("p (h e) -> p h e", h=2)
                    for h2 in range(2):
                        em.mm(pcv[:, h2, :], PTb[:, h2, :], b["Vc"][:, nt_, g, :], start=True, stop=True)
                    if nt_ == 0:
                        em.cp(pov, pcv)
                    else:
                        em.tt(pov, pov, pcv, ALU.add)
                rd = b["rden"][:, 0:2]
                em.ts(rd, pov[:, :, 64], 1e-30, ALU.max)
                em.recip(rd, rd)
                h0 = g * 4 + half * 2
                em.tt(b["ocmp"][:, 0:2, :], pov[:, :, 0:64], rd.un(2).bc([128, 2, 64]), ALU.mult)
                em.tt(b["ob"][:, h0:h0 + 2, :], b["ocmp"][:, 0:2, :], b["gate"][:, h0:h0 + 2].un(2).bc([128, 2, 64]), ALU.mult)
                em.tt(b["impt"][:, half * 2:half * 2 + 2, :], pov[:, :, 65:129], rd.un(2).bc([128, 2, 64]), ALU.mult)
            em.red(b["imp"][:, g, :], b["impt"].re("p h j -> p j h"))
        em.dma("sp", b["selb"], self.c_selb[t], lane="cst2")
        for g in range(2):
            sc = b["score"][:, g, :]
            em.tt(sc, b["imp"][:, g, :], b["selb"], ALU.add)
            em.op("dve", lambda e: e.max(out=b["max8"].ap, in_=sc.ap), reads=[sc], writes=[b["max8"]])
            em.op("dve", lambda e: e.match_replace(out=b["score2"].ap, in_to_replace=b["max8"].ap, in_values=sc.ap,
                                                   imm_value=-3.0e38), reads=[sc, b["max8"]], writes=[b["score2"]])
            em.op("dve", lambda e: e.max(out=b["max8"].ap, in_=b["score2"].ap), reads=[b["score2"]], writes=[b["max8"]])
            em.ts(b["score2"], sc, b["max8"][:, 7:8], ALU.is_ge, -1.0, ALU.add)
            em.ts(b["selbias"][:, g, :], b["score2"], -NEG, ALU.mult)
        for g in range(2):
            QT = b["QTr"][:, g]
            pov = b["num"]
            for kt in range(t + 1):
                sx = b["selx"][:, kt % 2]
                em.cp(sx, b["selbias"][:, g, 2 * kt:2 * kt + 2].un(2).bc([128, 2, 64]), eng=("pool" if kt % 2 else "dve"))
                extra = [(sx.re("p a b -> p (a b)"), c["ident4"].re("p h i -> p (h i)"))]
                if kt == t:
                    extra.append((c["ident_b"], c["causal4"].re("p h i -> p (h i)")))
                PTb = b["PT"] if kt % 2 == 0 else b["PT2"]
                self.attn_block(b["KslcT"][:, kt * 128:(kt + 1) * 128].wb(hb["KslcT"][kt]), QT, extra,
                                b["Vslc"][:, kt, g, :].wb(hb["Vslc"][kt]), PTb, pov, kt == 0)
            rd = b["rden"][:, 0:4]
            em.recip(rd, pov[:, :, 64])
            em.tt(b["ocmp"][:, 0:4, :], pov[:, :, 0:64], rd.un(2).bc([128, 4, 64]), ALU.mult)
            em.tt(b["ocmp"][:, 0:4, :], b["ocmp"][:, 0:4, :], b["gate"][:, 8 + g * 4:12 + g * 4].un(2).bc([128, 4, 64]), ALU.mult)
            if not _os.environ.get("NSA_NO_SLC"):
                em.tt(b["ob"][:, g * 4:(g + 1) * 4, :], b["ob"][:, g * 4:(g + 1) * 4, :], b["ocmp"][:, 0:4, :], ALU.add)
            pov = b["num"]
            k0 = max(0, t - 4)
            for kt in range(k0, t + 1):
                extra = []
                if kt == t:
                    extra.append((c["ident_b"], c["causal4"].re("p h i -> p (h i)")))
                elif kt == t - 4:
                    extra.append((c["ident_b"], c["anti4"].re("p h i -> p (h i)")))
                PTb = b["PT"] if kt % 2 == 0 else b["PT2"]
                self.attn_block(b["KwinT"][:, (kt % 5) * 128:(kt % 5 + 1) * 128].wb(hb["KwinT"][kt % 5]), QT, extra,
                                b["Vwin"][:, kt % 5, g, :].wb(hb["Vwin"][kt % 5]), PTb, pov, kt == k0)
            rd = b["rden"][:, 4:8]
            em.recip(rd, pov[:, :, 64])
            em.tt(b["ocmp"][:, 4:8, :], pov[:, :, 0:64], rd.un(2).bc([128, 4, 64]), ALU.mult)
            em.tt(b["ocmp"][:, 4:8, :], b["ocmp"][:, 4:8, :], b["gate"][:, 16 + g * 4:20 + g * 4].un(2).bc([128, 4, 64]), ALU.mult)
            if not _os.environ.get("NSA_NO_WIN"):
                em.tt(b["ob"][:, g * 4:(g + 1) * 4, :], b["ob"][:, g * 4:(g + 1) * 4, :], b["ocmp"][:, 4:8, :], ALU.add)
        ob = b["ob"]
        sl = b["silu"]
        em.act(sl, u[:, O_NZ:O_NZ + 512], AF.Silu)
        em.tt(b["mixed"][:, 256:768], ob.re("p h e -> p (h e)"), sl, ALU.mult)

    def write_states(self, L):
        em, c, b = self.em, self.c, self.pb
        self.out_dma("sp", self.p_gla[L], b["S"])
        cc = b["tmp"][0:64, :, 0:64]
        for h in range(4):
            pp = self.nextAB()
            o_ = (h % 2) * 64
            em.tr(pp[0:64, 0:64], b["Ct"][o_:o_ + 64, h, 0:64], c["ident_f"][o_:o_ + 64, o_:o_ + 64])
            em.cp(cc[:, h, :], pp[0:64, 0:64])
        self.out_dma("sp", self.p_c[L].rearrange("h e d -> e h d"), cc)
        with self.nc.allow_non_contiguous_dma("tiny state vectors"):
            for h in range(4):
                o_ = (h % 2) * 64
                self.out_dma("sp", self.p_n[L, h].rearrange("(d o) -> d o", o=1), b["Ct"][o_:o_ + 64, h, 64:65])
        self.out_dma("sp", self.p_m[L:L + 1, :], b["Mrep"][0:1, :])


    def alloc_dec_bufs(self):
        em, c, b, sb = self.em, self.c, self.pb, self.sb
        d = self.db = {}
        kb_ = b["KslcT"]
        o = [0]

        def carve(n):
            v = kb_[:, o[0]:o[0] + n]
            o[0] += n
            return v
        self.dbufs = {}
        kbufs, vbufs = [], []

        def own(name, v, n=1, lst=None):
            bl = [Buf() for _ in range(n)]
            self.dbufs[name] = bl
            lst.extend(bl)
            return v.wb(bl)
        d["rawT"] = own("rawT", carve(2 * 528).re("p (k i) -> p k i", k=2), 1, kbufs)
        d["KcT"] = own("KcT", carve(512), 1, kbufs)
        d["VcT"] = own("VcT", carve(512), 1, kbufs)
        d["PTall"] = own("PTall", carve(128).re("p (n q) -> p n q", n=4), 1, kbufs)
        d["KTpg"] = own("KTpg", carve(256).re("p (a k) -> p a k", a=2), 2, kbufs)
        d["pgb"] = own("pgb", carve(512).re("p (a f) -> p a f", a=2), 2, kbufs)
        d["PTpg"] = own("PTpg", carve(128).re("p (a q) -> p a q", a=2), 2, kbufs)
        d["Vpg"] = own("Vpg", carve(2 * 2 * 65).re("p (a g e) -> p a g e", a=2, g=2), 2, kbufs)
        assert o[0] <= 4096
        vb_ = b["Vslc"].re("p t g e -> p (t g e)")
        d["Vc"] = own("Vc", vb_[:, 0:4 * 2 * 194].re("p (n g e) -> p n g e", n=4, g=2), 1, vbufs)
        d["stg"] = b["silu"].re("p (a f) -> p a f", a=2)
        d["kc32"] = b["num"].re("p h e -> p (h e)")[0:32, 0:256].re("p (k f) -> p k f", k=2)
        stage = self.stage
        d["cmpb32"] = stage[0:32, 0:256].re("p (k f) -> p k f", k=2)
        d["ptf"] = stage[:, 256:320]
        d["OT"] = stage[0:65, 320:384].re("p (g q) -> p g q", g=2)
        d["imp"] = stage[0:8, 384:656].re("p (g j) -> p g j", g=2)
        d["impt"] = stage[0:8, 656:916].re("p (g j) -> p g j", g=2)
        d["pcolf"] = stage[:, 916:917]
        uh = b["u"][0:8, 2600:3112]
        d["selb"] = uh[:, 0:136]
        d["score"] = uh[:, 136:408].re("p (g j) -> p g j", g=2)
        d["score2"] = b["go"].re("p h e -> p (h e)")[0:8, 0:136]
        d["rope"] = sb("rope_d", [8, 16])
        d["idx"] = self.iota_i32.re("p (a k) -> p a k", a=2)
        d["pti"] = sb("pt_i", [128, 64], I32)
        o2 = [4 * 2 * 194]

        def carve2(n):
            v = vb_[:, o2[0]:o2[0] + n]
            o2[0] += n
            return v
        d["ovl"] = own("ovl", carve2(520).re("p (n j) -> p n j", n=4), 1, vbufs)
        d["selbias"] = own("selbias", carve2(272)[0:8].re("p (g j) -> p g j", g=2), 1, vbufs)
        d["sx"] = own("sx", carve2(512)[0:8].re("p (a g b c) -> p a g b c", a=2, g=2, b=2), 2, vbufs)
        d["Vnew"] = own("Vnew", carve2(260)[0:8].re("p (a g e) -> p a g e", a=2, g=2), 1, vbufs)
        d["anti8"] = own("anti8", carve2(128)[0:8], 1, vbufs)
        d["QTr"] = own("QTr", carve2(64).re("p (g h i) -> p g h i", g=2, h=4), 1, vbufs)
        d["QTn"] = own("QTn", carve2(64).re("p (g h i) -> p g h i", g=2, h=4), 1, vbufs)
        d["KTnew"] = own("KTnew", carve2(16).re("p (j i) -> p j i", j=2), 1, vbufs)
        d["id4_8"] = own("id4_8", carve2(32)[0:8].re("p (h i) -> p h i", h=4), 1, vbufs)
        d["causal8"] = own("causal8", carve2(32)[0:8].re("p (h i) -> p h i", h=4), 1, vbufs)
        assert o2[0] <= 4160
        self.dec_allk = b["KslcT"].wb(list(b["KslcT"].bs) + self.hb["KslcT"] + kbufs)
        self.dec_allv = b["Vslc"].wb(list(b["Vslc"].bs) + self.hb["Vslc"] + vbufs)
        self.pc2 = sb("pcol2_i32", [128, 1], I32)
        em.op("pool", lambda e: e.iota(self.pc2.ap, pattern=[[0, 1]], base=0, channel_multiplier=1), writes=[self.pc2])
        em.dma("sp", d["rope"], self.c_rope_d, serial=True)

    def dec_consts(self):
        em, c, b, d = self.em, self.c, self.pb, self.db
        stage = self.stage
        ov = b["x"][:, 0:520].re("p (n j) -> p n j", n=4)
        em.dma("sp", ov, self.c_ovl_d.rearrange("(n p) j -> p n j", p=128), serial=True)
        em.cp(d["ovl"], ov)
        em.dma("sp", d["selb"], self.c_selb_d, serial=True)
        for h in range(4):
            em.cp(d["id4_8"][:, h, :], c["ident_f"][0:8, 0:8])
            em.cp(d["causal8"][:, h, :], c["causal4"][0:8, 0, 0:8])
        a8 = b["x"][0:8, 600:728]
        em.memset(a8, 0.0)
        em.asel(a8, a8, [[1, 128]], ALU.is_gt, NEG, 0, -1)
        em.cp(d["anti8"], a8)
        em.cp(d["pcolf"], self.pc2)
        em.memset(d["Vpg"][:, :, :, 64:65], 1.0)

    def dv(self, name, a):
        return self.db[name][:, a].wb(self.dbufs[name][a])

    def dec_layer(self, L):
        em, w, c, b = self.em, self.w, self.c, self.pb
        if L == 0:
            self.alloc_dec_bufs()
        d = self.db
        em.memset(self.dec_allk, 0.0)
        em.memset(self.dec_allv, 0.0)
        self.dec_consts()
        for kv in range(2):
            pp = self.nextAB()
            em.mm(pp[0:32, 0:128], c["ones"][0:1, 0:32], w["cmpb_tm"][0:1, kv, :])
            em.cp(d["cmpb32"][:, kv, :], pp[0:32, 0:128])
        for g in range(2):
            em.cp(d["Vc"][:, :, g, 64:194], d["ovl"])
        if not hasattr(self, "yd1_v"):
            self.yd1_v = [V(self.yd1[s_], Buf()) for s_ in range(4)]
        x_src = self.x_sample if L == 0 else self.yd1_v
        y_dst = self.y_sample if L == self.LAYERS - 1 else self.yd1_v
        for s in range(self.NDEC):
            self.dec_seq(L, s, x_src, y_dst)

    def dec_seq(self, L, s, x_src, y_dst):
        em, w, c, b, d = self.em, self.w, self.c, self.pb, self.db
        em.redirect_pool = True
        P = 8
        em.rotate("g0")
        em.rotate("g1")
        em.rotate("w0")
        em.rotate("w1")
        x, u = b["x"], b["u"]
        em.dma("sp", b["S"], self.st_gla[L, s], serial=True)
        em.cp(b["S_b"], b["S"], eng="act")
        em.memset(b["Ct"], 0.0)
        cst = b["tmp"][0:64, :, :].re("p c i -> p (c i)")[:, 0:256].re("p (h d) -> p h d", h=4)
        em.dma("sp", cst, self.st_c[L, s].rearrange("h e d -> e h d"), serial=True)
        for j in range(2):
            pp = self.nextAB()
            em.tr(pp[:, 0:64], cst[:, 2 * j:2 * j + 2, :].re("p h d -> p (h d)"), c["ident_f"][0:64, 0:64])
            em.cp(b["Ct"][0:64, 2 * j, 0:64], pp[0:64, 0:64])
            em.cp(b["Ct"][64:128, 2 * j + 1, 0:64], pp[64:128, 0:64])
        with self.nc.allow_non_contiguous_dma("tiny state vectors"):
            for h in range(4):
                o_ = (h % 2) * 64
                em.dma("sp", b["Ct"][o_:o_ + 64, h, 64:65], self.st_n[L, s, h].rearrange("(d o) -> d o", o=1), serial=True)
            for r in range(3):
                em.dma("sp", b["qkT"][:, :, r], self.st_conv[L, s, r].rearrange("(c p) -> p c", p=128), serial=True)
        em.cp(b["Ct_b"], b["Ct"], eng="act")
        em.dma("sp", b["Mrep"], self.st_m[L, s:s + 1, :].to_broadcast([128, 4]), serial=True)
        em.dma("sp", x[0:P], x_src[s], lane="x")
        ss = b["st1"][0:P, 0:1]
        em.act(u[0:P, 0:D], x[0:P], AF.Square, accum=ss)
        em.ts(ss, ss, 1.0 / D, ALU.mult, EPS, ALU.add)
        em.act(ss, ss, AF.Sqrt)
        em.recip(ss, ss)
        em.stt(b["xn"][0:P], x[0:P], ss, w["normg"][0:P], ALU.mult, ALU.mult)
        for k in range(8):
            pt = self.nextT()
            em.tr(pt[:, 0:P], b["xn"][0:P, k * 128:(k + 1) * 128], c["ident_b"][0:P, 0:P])
            em.cp(b["xnT"][:, k, 0:P], pt[:, 0:P], eng=("act" if k % 2 else "dve"))
        chunks = [(0, 512), (512, 512), (1024, 512), (1536, 512), (2048, 512), (2560, 40), (3112, 512), (3624, 264)]
        for ci, (c0, cw) in enumerate(chunks):
            pp = self.nextAB()
            for k in range(8):
                em.mm(pp[0:P, 0:cw], b["xnT"][:, k, 0:P], w["win"][:, k, c0:c0 + cw], start=(k == 0), stop=(k == 7))
            em.cp(u[0:P, c0:c0 + cw], pp[0:P, 0:cw], eng=("act" if ci % 2 else "dve"))
        for cch in range(4):
            pp = self.nextAB()
            for k in range(8):
                em.mm(pp[:, 0:P], w["win"][:, k, O_MQK + cch * 128:O_MQK + (cch + 1) * 128], b["xnT"][:, k, 0:P],
                      start=(k == 0), stop=(k == 7))
            em.cp(b["qkT"][:, cch, 3:3 + P], pp[:, 0:P], eng=("act" if cch % 2 else "dve"))
        import os as _os
        self.gla_tile(L, P)
        self.mlstm_tile(L, P)
        if not _os.environ.get("SKIP_NSAD"):
            self.nsa_dec(L, s)
        for k in range(8):
            pt = self.nextT()
            em.tr(pt[:, 0:P], b["mixed"][0:P, k * 128:(k + 1) * 128], c["ident_b"][0:P, 0:P])
            em.cp(b["mixT"][:, k, 0:P], pt[:, 0:P], eng=("act" if k % 2 else "dve"))
        for half in range(2):
            pp = self.nextAB()
            for k in range(8):
                em.mm(pp[0:P, 0:512], b["mixT"][:, k, 0:P], w["wout"][:, k, half * 512:(half + 1) * 512],
                      start=(k == 0), stop=(k == 7))
            em.tt(x[0:P, half * 512:(half + 1) * 512], pp[0:P, 0:512], x[0:P, half * 512:(half + 1) * 512], ALU.add)
        self.out_dma("sp", y_dst[s], x[0:P])
        with self.nc.allow_non_contiguous_dma("tiny conv state"):
            for r in range(3):
                self.out_dma("sp", self.s_conv[L, s, r].rearrange("(c p) -> p c", p=128), b["qkT"][:, :, P + r])
        self.out_dma("sp", self.s_gla[L, s], b["S"])
        cc = b["tmp"][0:64, :, 0:64]
        for h in range(4):
            pp = self.nextAB()
            o_ = (h % 2) * 64
            em.tr(pp[0:64, 0:64], b["Ct"][o_:o_ + 64, h, 0:64], c["ident_f"][o_:o_ + 64, o_:o_ + 64])
            em.cp(cc[:, h, :], pp[0:64, 0:64])
        self.out_dma("sp", self.s_c[L, s].rearrange("h e d -> e h d"), cc)
        with self.nc.allow_non_contiguous_dma("tiny state vectors"):
            for h in range(4):
                o_ = (h % 2) * 64
                self.out_dma("sp", self.s_n[L, s, h].rearrange("(d o) -> d o", o=1), b["Ct"][o_:o_ + 64, h, 64:65])
        self.out_dma("sp", self.s_m[L, s:s + 1, :], b["Mrep"][0:1, :])

    def dec_finish_branch(self, g, gate_off):
        em, c, b, d = self.em, self.c, self.pb, self.db
        po = self.pO0 if g == 0 else self.pO1
        em.cp(d["OT"][:, g, :], po[0:65, 0:32])
        pp = self.nextAB()
        ppv = pp[0:8, 0:260].re("p (h e) -> p h e", h=4)
        for hh in range(4):
            em.tr(ppv[:, hh, :], d["OT"][:, g, hh * 8:(hh + 1) * 8], c["ident_f"][0:65, 0:65])
        rd = b["rden"][0:8, 0:4]
        em.recip(rd, ppv[:, :, 64])
        t_ = b["ocmp"][0:8, 0:4, :]
        em.tt(t_, ppv[:, :, 0:64], rd.un(2).bc([8, 4, 64]), ALU.mult)
        em.tt(t_, t_, b["gate"][0:8, gate_off + g * 4:gate_off + g * 4 + 4].un(2).bc([8, 4, 64]), ALU.mult)
        import os as _os
        if not _os.environ.get("NSA_NO_SLC" if gate_off == 8 else "NSA_NO_WIN"):
            em.tt(b["ob"][0:8, g * 4:(g + 1) * 4, :], b["ob"][0:8, g * 4:(g + 1) * 4, :], t_, ALU.add)

    def nsa_dec(self, L, s):
        em, w, c, b, d = self.em, self.w, self.c, self.pb, self.db
        u = b["u"]
        P = 8
        cs = d["rope"]
        em.dma("act", d["pti"], self.page_table[s:s + 1, :].to_broadcast([128, 64]), serial=True)
        em.cp(d["ptf"], d["pti"])
        em.ts(d["ptf"], d["ptf"], 128.0, ALU.mult, d["pcolf"], ALU.add)
        em.ts(d["ptf"], d["ptf"], 2.0, ALU.mult)
        em.cp(d["idx"][:, 0, :], d["ptf"])
        em.ts(d["ptf"], d["ptf"], 1.0, ALU.add)
        em.cp(d["idx"][:, 1, :], d["ptf"])
        qn = b["qn"][0:P]
        self.rms_heads(qn, u[0:P, O_NQ:O_NQ + 512].re("p (h e) -> p h e", h=8), 8, w["qg"][0:P], P=P)
        em.cp(b["qn_b"][0:P].re("p (hh g) e -> p g hh e", g=2), qn.re("p (g hh) e -> p g hh e", g=2), eng="pool")
        self.rope(b["ocmp"][0:P], qn, 8, cs, P=P)
        em.cp(b["qr"][0:P].re("p (hh g) e -> p g hh e", g=2), b["ocmp"][0:P].re("p (g hh) e -> p g hh e", g=2))
        kv = u[0:P, O_NKV:O_NKV + 768].re("p (s g e) -> p s g e", s=6, g=2)
        rows = b["rows"][0:P].re("p s (g e) -> p s g e", g=2)
        winr = b["win"][0:P].re("p s (g e) -> p s g e", g=2)
        em.cp(rows[:, 0], kv[:, 0], eng="pool")
        em.cp(rows[:, 1], kv[:, 1], eng="pool")
        em.cp(rows[:, 3], kv[:, 3], eng="pool")
        em.cp(winr[:, 1], kv[:, 5], eng="pool")
        ktmp = b["ocmp"][0:P, 0:2, :]
        self.rms_heads(ktmp, kv[:, 2], 2, w["kg"][0:P, 1, :], P=P)
        self.rope(rows[:, 2], ktmp, 2, cs, P=P)
        self.rms_heads(ktmp, kv[:, 4], 2, w["kg"][0:P, 2, :], P=P)
        self.rope(winr[:, 0], ktmp, 2, cs, P=P)
        self.out_dma("act", self.s_rows[L, s], b["rows"][0:P].re("p s f -> p (s f)"))
        self.out_dma("act", self.s_win[L, s, 0:504, :], self.st_win[L, s, 8:512, :])
        self.out_dma("act", self.s_win[L, s, 504:512, :], b["win"][0:P].re("p s f -> p (s f)"))
        kkb = b["kk_b"]
        em.cp(kkb[0:P, 0, :], b["rows"][0:P, 2, :])
        em.cp(kkb[0:P, 1, :], b["win"][0:P, 0, :])
        pt = self.nextT()
        for j in range(2):
            em.tr(pt[:, j * 8:(j + 1) * 8], kkb[0:P, j, :], c["ident_b"][0:P, 0:P])
        em.cp(d["KTnew"], pt[:, 0:16].re("p (j i) -> p j i", j=2))
        em.memset(d["Vnew"][:, :, :, 64:65], 1.0)
        em.cp(d["Vnew"][:, 0, :, 0:64], rows[:, 3])
        em.cp(d["Vnew"][:, 1, :, 0:64], winr[:, 1])
        pt = self.nextT()
        qr4 = b["qr"][0:P].re("p (hh g) e -> p hh (g e)", g=2)
        qn4 = b["qn_b"][0:P].re("p (hh g) e -> p hh (g e)", g=2)
        for hh in range(4):
            em.tr(pt[:, hh * 8:(hh + 1) * 8], qr4[:, hh, :], c["ident_b"][0:P, 0:P])
            em.tr(pt[:, 32 + hh * 8:32 + (hh + 1) * 8], qn4[:, hh, :], c["ident_b"][0:P, 0:P])
        for g in range(2):
            em.ts(d["QTr"][:, g].re("p h i -> p (h i)"), pt[:, 0:32], c["pmask2"][:, g:g + 1], ALU.mult)
            em.ts(d["QTn"][:, g].re("p h i -> p (h i)"), pt[:, 32:64], c["pmask2"][:, g:g + 1], ALU.mult)
        em.act(b["gate"][0:P], u[0:P, O_NG:O_NG + 24], AF.Sigmoid)
        cacheL = self.cache[L]
        em.memset(d["rawT"], 0.0)
        em.memset(d["KcT"], 0.0)
        em.memset(d["VcT"], 0.0)
        for bq in range(16):
            em.cp(d["rawT"][:, :, 0:16], d["rawT"][:, :, 512:528], eng="pool")
            for pg in range(4):
                kt = bq * 4 + pg
                a = kt % 2
                stg = d["stg"][:, a, :]
                self.gather(stg, cacheL, d["idx"][:, 0, kt:kt + 1], lane="g%d" % a)
                pgb = self.dv("pgb", a)
                em.cp(pgb, stg, eng=("act" if a else "dve"))
                pt = self.nextT()
                for j in range(2):
                    em.tr(pt[:, j * 128:(j + 1) * 128], pgb[:, j * 128:(j + 1) * 128], c["ident_b"])
                em.cp(d["rawT"][:, :, 16 + pg * 128:16 + (pg + 1) * 128], pt[:, 0:256].re("p (k i) -> p k i", k=2),
                      eng=("dve" if a else "act"))
            for kvi in range(2):
                pp = self.nextAB()
                for s_ in range(32):
                    em.mm(pp[0:32, 0:128], d["rawT"][:, kvi, s_:s_ + 497:16], w["cmpw"][:, kvi, s_, :],
                          start=(s_ == 0), stop=(s_ == 31))
                em.tt(d["kc32"][:, kvi, :], pp[0:32, 0:128], d["cmpb32"][:, kvi, :], ALU.add)
            kcn = b["kk_b"][0:32, 2, :].re("p (g e) -> p g e", g=2)
            self.rms_heads(kcn, d["kc32"][:, 0, :].re("p (g e) -> p g e", g=2), 2, w["kg"][0:32, 0, :], P=32)
            vcb = b["kk_b"][0:32, 3, :]
            em.cp(vcb, d["kc32"][:, 1, :])
            pt = self.nextT()
            em.tr(pt[:, 0:32], kcn.re("p g e -> p (g e)"), c["ident_b"][0:32, 0:32])
            em.tr(pt[:, 32:64], vcb, c["ident_b"][0:32, 0:32])
            n0 = 32 * bq - 1
            if bq == 0:
                em.cp(d["KcT"][:, 0:31], pt[:, 1:32])
                em.cp(d["VcT"][:, 0:31], pt[:, 33:64])
            else:
                em.cp(d["KcT"][:, n0:n0 + 32], pt[:, 0:32])
                em.cp(d["VcT"][:, n0:n0 + 32], pt[:, 32:64])
        for nt_ in range(4):
            pt = self.nextT()
            em.tr(pt[:, 0:128], d["VcT"][:, nt_ * 128:(nt_ + 1) * 128], c["ident_b"])
            em.cp(d["Vc"][:, nt_, :, 0:64], pt[:, 0:128].re("p (g e) -> p g e", g=2))
        for g in range(2):
            ps_ = self.nextS()
            for nt_ in range(4):
                em.mm(ps_[:, nt_ * 32:(nt_ + 1) * 32], d["KcT"][:, nt_ * 128:(nt_ + 1) * 128],
                      d["QTn"][:, g].re("p h i -> p (h i)"))
            em.act(d["PTall"].re("p n q -> p (n q)"), ps_[:, 0:128], AF.Exp, scale=ATT_SCALE)
            for half in range(2):
                po = self.nextAB()
                pov = po[0:8, 0:388].re("p (h e) -> p h e", h=2)
                for h2 in range(2):
                    hh = half * 2 + h2
                    for nt_ in range(4):
                        em.mm(pov[:, h2, :], d["PTall"][:, nt_, hh * 8:(hh + 1) * 8], d["Vc"][:, nt_, g, :],
                              start=(nt_ == 0), stop=(nt_ == 3))
                rd = b["rden"][0:8, 0:2]
                em.ts(rd, pov[:, :, 64], 1e-30, ALU.max)
                em.recip(rd, rd)
                h0 = g * 4 + half * 2
                em.tt(b["ocmp"][0:8, 0:2, :], pov[:, :, 0:64], rd.un(2).bc([8, 2, 64]), ALU.mult)
                em.tt(b["ob"][0:8, h0:h0 + 2, :], b["ocmp"][0:8, 0:2, :], b["gate"][0:8, h0:h0 + 2].un(2).bc([8, 2, 64]), ALU.mult)
                em.tt(d["impt"][:, :, 0:129], pov[:, :, 65:194], rd.un(2).bc([8, 2, 129]), ALU.mult)
                if half == 0:
                    em.tt(d["imp"][:, g, 0:129], d["impt"][:, 0, 0:129], d["impt"][:, 1, 0:129], ALU.add)
                else:
                    em.tt(d["imp"][:, g, 0:129], d["imp"][:, g, 0:129], d["impt"][:, 0, 0:129], ALU.add)
                    em.tt(d["imp"][:, g, 0:129], d["imp"][:, g, 0:129], d["impt"][:, 1, 0:129], ALU.add)
        em.memset(d["imp"][:, :, 129:136], 0.0)
        for g in range(2):
            sc = d["score"][:, g, :]
            em.tt(sc, d["imp"][:, g, :], d["selb"], ALU.add)
            mx8 = b["max8"][0:8]
            em.op("dve", lambda e: e.max(out=mx8.ap, in_=sc.ap), reads=[sc], writes=[mx8])
            em.op("dve", lambda e: e.match_replace(out=d["score2"].ap, in_to_replace=mx8.ap, in_values=sc.ap,
                                                   imm_value=-3.0e38), reads=[sc, mx8], writes=[d["score2"]])
            em.op("dve", lambda e: e.max(out=mx8.ap, in_=d["score2"].ap), reads=[d["score2"]], writes=[mx8])
            em.ts(d["score2"], sc, mx8[:, 7:8], ALU.is_ge, -1.0, ALU.add)
            em.ts(d["selbias"][:, g, :], d["score2"], -NEG, ALU.mult)
        for kt in range(65):
            a = kt % 2
            newt = (kt == 64)
            if not newt:
                stg = d["stg"][:, a, :]
                self.gather(stg, cacheL, d["idx"][:, 1, kt:kt + 1], lane="g%d" % a)
                pgb = self.dv("pgb", a)[:, 0:128]
                em.cp(pgb, stg[:, 0:128], eng=("act" if a else "dve"))
                em.cp(self.dv("Vpg", a)[:, :, 0:64], stg[:, 128:256].re("p (g e) -> p g e", g=2), eng="pool")
                pt = self.nextT()
                em.tr(pt[:, 0:128], pgb, c["ident_b"])
                KT = self.dv("KTpg", a)
                em.cp(KT, pt[:, 0:128], eng=("dve" if a else "act"))
                nk = 128
            else:
                KT = d["KTnew"][:, 0, :]
                nk = 8
            ps_ = self.nextS()
            sx = self.dv("sx", a)
            for g in range(2):
                if not newt:
                    em.cp(sx[:, g], d["selbias"][:, g, 2 * kt:2 * kt + 2].un(2).bc([8, 2, 64]), eng="pool")
                    lt = sx[:, g].re("p a b -> p (a b)")
                else:
                    em.cp(sx[:, g, 0, 0:8], d["selbias"][:, g, 128:129].bc([8, 8]), eng="pool")
                    lt = sx[:, g, 0, 0:8]
                psv = ps_[0:nk, g * 32:(g + 1) * 32]
                em.mm(psv, KT, d["QTr"][:, g].re("p h i -> p (h i)"), start=True, stop=False)
                em.mm(psv, lt, d["id4_8"].re("p h i -> p (h i)"), start=False, stop=not newt)
                if newt:
                    em.mm(psv, c["ident_b"][0:8, 0:8], d["causal8"].re("p h i -> p (h i)"), start=False, stop=True)
            PT = self.dv("PTpg", a)[0:nk]
            em.act(PT, ps_[0:nk, 0:64], AF.Exp, scale=ATT_SCALE)
            for g in range(2):
                po = self.pO0 if g == 0 else self.pO1
                Vr = self.dv("Vpg", a)[:, g, :] if not newt else d["Vnew"][:, 0, g, :]
                em.mm(po[0:65, 0:32], Vr, PT[:, g * 32:(g + 1) * 32], start=(kt == 0), stop=newt)
        for g in range(2):
            self.dec_finish_branch(g, 8)
        for kt in range(5):
            a = kt % 2
            newt = (kt == 4)
            if not newt:
                stg = d["stg"][:, a, :]
                em.dma("sp", stg, self.st_win[L, s, kt * 128:(kt + 1) * 128, :], lane="w%d" % a)
                pgb = self.dv("pgb", a)[:, 0:128]
                em.cp(pgb, stg[:, 0:128], eng=("act" if a else "dve"))
                em.cp(self.dv("Vpg", a)[:, :, 0:64], stg[:, 128:256].re("p (g e) -> p g e", g=2), eng="pool")
                pt = self.nextT()
                em.tr(pt[:, 0:128], pgb, c["ident_b"])
                KT = self.dv("KTpg", a)
                em.cp(KT, pt[:, 0:128], eng=("dve" if a else "act"))
                nk = 128
            else:
                KT = d["KTnew"][:, 1, :]
                nk = 8
            ps_ = self.nextS()
            for g in range(2):
                psv = ps_[0:nk, g * 32:(g + 1) * 32]
                extra = (kt == 0) or newt
                em.mm(psv, KT, d["QTr"][:, g].re("p h i -> p (h i)"), start=True, stop=not extra)
                if kt == 0:
                    em.mm(psv, d["anti8"], d["id4_8"].re("p h i -> p (h i)"), start=False, stop=True)
                if newt:
                    em.mm(psv, c["ident_b"][0:8, 0:8], d["causal8"].re("p h i -> p (h i)"), start=False, stop=True)
            PT = self.dv("PTpg", a)[0:nk]
            em.act(PT, ps_[0:nk, 0:64], AF.Exp, scale=ATT_SCALE)
            for g in range(2):
                po = self.pO0 if g == 0 else self.pO1
                Vr = self.dv("Vpg", a)[:, g, :] if not newt else d["Vnew"][:, 1, g, :]
                em.mm(po[0:65, 0:32], Vr, PT[:, g * 32:(g + 1) * 32], start=(kt == 0), stop=newt)
        for g in range(2):
            self.dec_finish_branch(g, 16)
        sl = b["silu"][0:P]
        em.act(sl, u[0:P, O_NZ:O_NZ + 512], AF.Silu)
        em.tt(b["mixed"][0:P, 256:768], b["ob"][0:P].re("p h e -> p (h e)"), sl, ALU.mult)

    def gather(self, out_v, src_rows, idx_v, lane=None):
        em = self.em
        if em.n_ins >= em.limit:
            return
        em._deps("pool", [idx_v], [out_v])
        ins = self.nc.gpsimd.indirect_dma_start(out=out_v.ap, out_offset=None, in_=src_rows[:, :],
                                                in_offset=bass.IndirectOffsetOnAxis(ap=idx_v.ap, axis=0))
        k = em.lane_key("pool", lane)
        em.cnt[k] += 16
        ins.then_inc(em.sem[k], 16)
        em._mark((k, em.cnt[k]), [idx_v], [out_v])
        em.n_ins += 1


def _host_consts():
    half = 8
    inv = np.exp(-math.log(500000.0) * np.arange(half, dtype=np.float32) * 2.0 / 16).astype(np.float32)
    pos = np.arange(SEQ, dtype=np.float32)
    ang = (pos[:, None] * inv[None, :]).astype(np.float32)
    rope = np.concatenate([np.cos(ang), np.sin(ang)], axis=1).astype(np.float32)
    i = np.arange(256)[:, None]
    j = np.arange(64)[None, :]
    s = i - 4 * j + 1
    cnt = np.minimum(np.minimum(s + 1, 4 + 2 - 1 - s), 2)
    ovl = np.maximum(cnt, 0).astype(np.float32)
    ovl[255] = 0.0
    selb = np.zeros((32, 128, 64), np.float32)
    for t in range(32):
        p = t * 128 + np.arange(128)
        cur = (p // 64)[:, None]
        blk = np.arange(64)[None, :]
        valid = blk * 64 <= p[:, None]
        forced = (blk == 0) | (blk == cur) | (blk == cur - 1)
        bias = np.where(forced, 1.0e9 + 1.0e6 * blk, np.where(valid, 0.0, -1.0e9 - 1.0e6 * blk))
        selb[t] = bias
    return rope, ovl, selb


def _host_consts_dec():
    half = 8
    inv = np.exp(-math.log(500000.0) * np.arange(half, dtype=np.float32) * 2.0 / 16).astype(np.float32)
    pos = (8192 + np.arange(8)).astype(np.float32)
    ang = (pos[:, None] * inv[None, :]).astype(np.float32)
    rope = np.concatenate([np.cos(ang), np.sin(ang)], axis=1).astype(np.float32)
    i = np.arange(512)[:, None]
    j = np.arange(129)[None, :]
    s = i - 4 * j + 1
    cnt = np.minimum(np.minimum(s + 1, 4 + 2 - 1 - s), 2)
    ovl = np.maximum(cnt, 0).astype(np.float32)
    ovl[511] = 0.0
    valid = np.ones((512, 1), np.float32)
    valid[511] = 0.0
    ovl_d = np.concatenate([valid, ovl], axis=1).astype(np.float32)
    selb = np.zeros((8, 136), np.float32)
    p = 8192 + np.arange(8)
    cur = (p // 64)[:, None]
    blk = np.arange(136)[None, :]
    valid_b = (blk * 64 <= p[:, None]) & (blk < 129)
    forced = ((blk == 0) | (blk == cur) | (blk == cur - 1)) & (blk < 129)
    selb[:] = np.where(forced, 1.0e9 + 1.0e6 * blk, np.where(valid_b, 0.0, -1.0e9 - 1.0e6 * blk))
    return rope, ovl_d, selb


_CACHE = {}


def _get_nc(nt, ndec, layers):
    key = (nt, ndec, layers)
    if key not in _CACHE:
        kb = KB(nt, ndec, layers)
        _CACHE[key] = kb.build()
    return _CACHE[key]


def kernel(x_prompt, x_sample, cache_nsa_kv, state_nsa_win, state_gla, state_mlstm_C, state_mlstm_n,
           state_mlstm_m, state_mlstm_conv, page_table, norm_g, w_in, w_out, gla_w_gate, gla_b_gate,
           gla_norm_g, nsa_q_norm_g, nsa_k_norm_g, nsa_cmp_pos, nsa_cmp_w, ml_conv_w, ml_conv_b,
           ml_gate_b, ml_norm_g, _nt=32, _cores=8, _layers=2, _ndec=4):
    f = lambda a: np.ascontiguousarray(np.asarray(a, dtype=np.float32))
    rope, ovl, selb = _host_consts()
    rope_d, ovl_d, selb_d = _host_consts_dec()
    nc = _get_nc(_nt, _ndec, _layers)
    shared = {
        "w_in": f(w_in), "w_out": f(w_out), "norm_g": f(norm_g), "gla_w_gate": f(gla_w_gate),
        "gla_b_gate": f(gla_b_gate), "gla_norm_g": f(gla_norm_g), "nsa_q_norm_g": f(nsa_q_norm_g),
        "nsa_k_norm_g": f(nsa_k_norm_g), "nsa_cmp_pos": f(nsa_cmp_pos), "nsa_cmp_w": f(nsa_cmp_w),
        "ml_conv_w": f(ml_conv_w), "ml_conv_b": f(ml_conv_b), "ml_gate_b": f(ml_gate_b).reshape(DEPTH, 8),
        "ml_norm_g": f(ml_norm_g), "c_rope": rope, "c_ovl": ovl, "c_selb": selb,
        "c_rope_d": rope_d, "c_ovl_d": ovl_d, "c_selb_d": selb_d,
    }
    cache = f(cache_nsa_kv)
    for l_ in range(DEPTH):
        shared["cache_nsa_kv%d" % l_] = cache[l_].reshape(-1, 256)
    pt_ = np.ascontiguousarray(np.asarray(page_table, dtype=np.int32))
    in_maps = []
    for cidx in range(_cores):
        m = dict(shared)
        m["x_prompt"] = f(x_prompt[cidx % 4])
        sl = slice(4 * cidx, 4 * cidx + 4)
        m["x_sample"] = f(x_sample[sl])
        m["state_nsa_win"] = f(state_nsa_win[:, sl]).reshape(DEPTH, 4, 512, 256)
        m["state_gla"] = f(state_gla[:, sl]).reshape(DEPTH, 4, 128, 64)
        m["state_mlstm_C"] = f(state_mlstm_C[:, sl])
        m["state_mlstm_n"] = f(state_mlstm_n[:, sl])
        m["state_mlstm_m"] = f(state_mlstm_m[:, sl])
        m["state_mlstm_conv"] = f(state_mlstm_conv[:, sl])
        m["page_table"] = np.ascontiguousarray(pt_[sl])
        in_maps.append(m)
    res = run_bass_kernel_spmd(nc, in_maps, core_ids=list(range(_cores)))
    R = res.results
    nb = min(4, _cores)
    y_prompt = np.stack([R[i]["y_prompt"] for i in range(nb)])
    p_rows = np.stack([R[i]["p_rows"] for i in range(nb)], axis=1).reshape(DEPTH, nb, SEQ, 4, 2, 64)
    p_win = np.stack([R[i]["p_win"] for i in range(nb)], axis=1).reshape(DEPTH, nb, 512, 2, 2, 64)
    p_gla = np.stack([R[i]["p_gla"] for i in range(nb)], axis=1).reshape(DEPTH, nb, 4, 32, 64)
    p_c = np.stack([R[i]["p_c"] for i in range(nb)], axis=1)
    p_n = np.stack([R[i]["p_n"] for i in range(nb)], axis=1)
    p_m = np.stack([R[i]["p_m"] for i in range(nb)], axis=1)
    p_conv = np.stack([R[i]["p_conv"] for i in range(nb)], axis=1)
    cat = lambda k, ax: np.concatenate([R[i][k] for i in range(_cores)], axis=ax)
    nd = 4 * _cores
    y_sample = cat("y_sample", 0)
    s_rows = cat("s_rows", 1).reshape(DEPTH, nd, 8, 4, 2, 64)
    s_win = cat("s_win", 1).reshape(DEPTH, nd, 512, 2, 2, 64)
    s_gla = cat("s_gla", 1).reshape(DEPTH, nd, 4, 32, 64)
    s_c = cat("s_c", 1)
    s_n = cat("s_n", 1)
    s_m = cat("s_m", 1)
    s_conv = cat("s_conv", 1)
    return (y_prompt, y_sample, p_rows, s_rows, p_win, s_win, p_gla, s_gla, p_c, s_c, p_n, s_n, p_m, s_m,
            p_conv, s_conv)
```

```python
import math
import numpy as np
from contextlib import ExitStack
import concourse.bass as bass
import concourse.mybir as mybir
from concourse.bass_utils import run_bass_kernel_spmd

F32 = mybir.dt.float32
BF16 = mybir.dt.bfloat16
I32 = mybir.dt.int32
AF = mybir.ActivationFunctionType
ALU = mybir.AluOpType
AX = mybir.AxisListType

D = 1024
SEQ = 4096
DEPTH = 2
NEG = -30000.0
EPS = 1e-6
O_GQ, O_GK, O_GV, O_GA, O_GZ = 0, 128, 256, 512, 528
O_NQ, O_NKV, O_NG, O_NZ = 784, 1296, 2064, 2088
O_MQK, O_MV, O_MIF, O_MO, O_MZ = 2600, 3112, 3368, 3376, 3632
D_IN = 3888
ATT_SCALE = 0.125


class Buf:
    __slots__ = ("w", "rd", "excl")

    def __init__(self, excl=False):
        self.w = None
        self.rd = []
        self.excl = excl


class V:
    __slots__ = ("ap", "bs")

    def __init__(self, ap, bs):
        self.ap = ap
        self.bs = bs if isinstance(bs, (list, tuple)) else [bs]

    def __getitem__(self, key):
        return V(self.ap[key], self.bs)

    def re(self, pattern_, **kw):
        return V(self.ap.rearrange(pattern_, **kw), self.bs)

    def bc(self, shape):
        return V(self.ap.to_broadcast(list(shape)), self.bs)

    def un(self, axis):
        return V(self.ap.unsqueeze(axis), self.bs)

    def wb(self, b):
        return V(self.ap, b)


class Em:
    def __init__(self, nc, st):
        self.nc = nc
        self.st = st
        self.E = {"pe": nc.tensor, "act": nc.scalar, "dve": nc.vector, "pool": nc.gpsimd, "sp": nc.sync}
        self.sem = {}
        self.cnt = {}
        for k in list(self.E) + ["d_sp", "d_act", "d_pool"]:
            self.sem[k] = st.enter_context(nc.semaphore("s_" + k))
            self.cnt[k] = 0
        self.seen = {k: {} for k in self.E}
        self.dkey = {"sp": "d_sp", "act": "d_act", "pool": "d_pool"}
        self.serbuf = {}
        self.redirect_pool = False
        self.nrot = 0
        self.n_ins = 0
        import os as _os
        self.limit = int(_os.environ.get("OP_LIMIT", "100000000"))

    def _deps(self, eng, reads, writes):
        deps = {}
        for v in reads:
            for b in v.bs:
                if b.w is not None:
                    deps[b.w[0]] = max(deps.get(b.w[0], 0), b.w[1])
                if b.excl:
                    for r in b.rd:
                        if r[0] != eng:
                            deps[r[0]] = max(deps.get(r[0], 0), r[1])
        for v in writes:
            for b in v.bs:
                if b.w is not None:
                    deps[b.w[0]] = max(deps.get(b.w[0], 0), b.w[1])
                for r in b.rd:
                    deps[r[0]] = max(deps.get(r[0], 0), r[1])
        for sk, val in deps.items():
            if sk == eng and eng == "pe":
                continue
            if self.seen[eng].get(sk, 0) < val:
                self.E[eng].wait_ge(self.sem[sk], val)
                self.seen[eng][sk] = val

    def _mark(self, tok, reads, writes):
        for v in reads:
            for b in v.bs:
                b.rd.append(tok)
                if len(b.rd) > 64:
                    mx = {}
                    for r in b.rd:
                        mx[r[0]] = max(mx.get(r[0], 0), r[1])
                    b.rd = list(mx.items())
        for v in writes:
            for b in v.bs:
                b.w = tok
                b.rd = []

    def rotate(self, q):
        self.nrot += 1
        k = "d_%s#%d" % (q, self.nrot)
        self.sem[k] = self.st.enter_context(self.nc.semaphore("s_" + k.replace("#", "_")))
        self.cnt[k] = 0
        self.dkey[q] = k

    def op(self, eng, build, reads=(), writes=(), generic=False):
        if generic and eng == "pool" and self.redirect_pool:
            eng = "dve"
        if self.n_ins >= self.limit:
            return None
        self._deps(eng, reads, writes)
        ins = build(self.E[eng])
        self.cnt[eng] += 1
        ins.then_inc(self.sem[eng], 1)
        self._mark((eng, self.cnt[eng]), reads, writes)
        self.n_ins += 1
        return ins

    def lane_key(self, q, lane):
        if lane is None:
            return self.dkey[q]
        if lane not in self.dkey:
            self.rotate(lane)
        return self.dkey[lane]

    def dma(self, q, out, in_, lane=None, serial=False, **kw):
        reads = [in_] if isinstance(in_, V) else []
        writes = [out] if isinstance(out, V) else []
        if serial:
            lane = lane or ("ser_" + q)
            if lane not in self.serbuf:
                self.serbuf[lane] = V(None, Buf())
            writes = writes + [self.serbuf[lane]]
        if self.n_ins >= self.limit:
            return None
        self._deps(q, reads, writes)
        o = out.ap if isinstance(out, V) else out
        i = in_.ap if isinstance(in_, V) else in_
        ins = self.E[q].dma_start(out=o, in_=i, **kw)
        k = self.lane_key(q, lane)
        self.cnt[k] += 16
        ins.then_inc(self.sem[k], 16)
        self._mark((k, self.cnt[k]), reads, writes)
        self.n_ins += 1
        return ins

    def mm(self, out, lhsT, rhs, start=True, stop=True):
        return self.op("pe", lambda e: e.matmul(out.ap, lhsT=lhsT.ap, rhs=rhs.ap, start=start, stop=stop),
                       reads=[lhsT, rhs], writes=[out])

    def tr(self, out, in_, ident):
        return self.op("pe", lambda e: e.transpose(out=out.ap, in_=in_.ap, identity=ident.ap),
                       reads=[in_, ident], writes=[out])

    def act(self, out, in_, func, bias=None, scale=1.0, accum=None, eng="act"):
        reads = [in_]
        kw = {}
        if bias is not None:
            if isinstance(bias, V):
                reads.append(bias)
                kw["bias"] = bias.ap
            else:
                kw["bias"] = bias
        if isinstance(scale, V):
            reads.append(scale)
            kw["scale"] = scale.ap
        else:
            kw["scale"] = scale
        writes = [out]
        if accum is not None:
            kw["accum_out"] = accum.ap
            writes.append(accum)
        return self.op(eng, lambda e: e.activation(out=out.ap, in_=in_.ap, func=func, **kw), reads=reads, writes=writes)

    def tt(self, out, a, b, op, eng="dve"):
        return self.op(eng, lambda e: e.tensor_tensor(out=out.ap, in0=a.ap, in1=b.ap, op=op), reads=[a, b], writes=[out],
                       generic=True)

    def ts(self, out, a, s1, op0, s2=None, op1=None, eng="dve"):
        reads = [a]
        s1a = s1.ap if isinstance(s1, V) else s1
        s2a = s2.ap if isinstance(s2, V) else s2
        if isinstance(s1, V):
            reads.append(s1)
        if isinstance(s2, V):
            reads.append(s2)
        if op1 is None:
            return self.op(eng, lambda e: e.tensor_scalar(out=out.ap, in0=a.ap, scalar1=s1a, scalar2=None, op0=op0),
                           reads=reads, writes=[out])
        return self.op(eng, lambda e: e.tensor_scalar(out=out.ap, in0=a.ap, scalar1=s1a, scalar2=s2a, op0=op0, op1=op1),
                       reads=reads, writes=[out])

    def stt(self, out, a, s, b, op0, op1, eng="dve"):
        reads = [a, b]
        sa = s.ap if isinstance(s, V) else s
        if isinstance(s, V):
            reads.append(s)
        return self.op(eng, lambda e: e.scalar_tensor_tensor(out=out.ap, in0=a.ap, scalar=sa, in1=b.ap, op0=op0, op1=op1),
                       reads=reads, writes=[out])

    def cp(self, out, in_, eng="dve"):
        if eng == "act":
            return self.op("act", lambda e: e.copy(out=out.ap, in_=in_.ap), reads=[in_], writes=[out])
        return self.op(eng, lambda e: e.tensor_copy(out=out.ap, in_=in_.ap), reads=[in_], writes=[out], generic=True)

    def red(self, out, in_, op=ALU.add, eng="dve"):
        return self.op(eng, lambda e: e.tensor_reduce(out=out.ap, in_=in_.ap, axis=AX.X, op=op), reads=[in_], writes=[out])

    def memset(self, out, val, eng="pool"):
        return self.op(eng, lambda e: e.memset(out.ap, val), writes=[out], generic=True)

    def asel(self, out, in_, pattern, cmp, fill, base, cm):
        return self.op("pool", lambda e: e.affine_select(out=out.ap, in_=in_.ap, pattern=pattern, compare_op=cmp,
                                                         fill=fill, base=base, channel_multiplier=cm),
                       reads=[in_], writes=[out])

    def recip(self, out, in_):
        return self.op("dve", lambda e: e.reciprocal(out=out.ap, in_=in_.ap), reads=[in_], writes=[out])


class KB:
    def __init__(self, nt_prompt=32, n_dec=4, layers=2, npool=2560):
        self.NPOOL = npool
        self.NT = nt_prompt
        self.NDEC = n_dec
        self.LAYERS = layers
        self.nc = bass.Bass("TRN2", target_bir_lowering=False)
        self.st = ExitStack()
        self.out_views = []

    def sb(self, name, shape, dt=F32):
        t = self.st.enter_context(self.nc.sbuf_tensor(name, list(shape), dt))
        return V(t[:], Buf())

    def ps(self, name, shape, dt=F32):
        t = self.st.enter_context(self.nc.psum_tensor(name, list(shape), dt))
        return V(t[:], Buf(excl=True))

    def din(self, name, shape, dt=F32):
        return self.nc.dram_tensor(name, list(shape), dt, kind="ExternalInput").ap()

    def dout(self, name, shape, dt=F32):
        return self.nc.dram_tensor(name, list(shape), dt, kind="ExternalOutput").ap()

    def dint(self, name, shape, dt=F32):
        return self.nc.dram_tensor(name, list(shape), dt, kind="Internal").ap()

    def build(self):
        nc = self.nc
        NT = self.NT
        with self.st:
            em = self.em = Em(nc, self.st)
            self.declare_io()
            self.setup_consts()
            self.alloc_prompt_bufs()
            for layer in range(self.LAYERS):
                self.load_layer_weights(layer)
                if NT > 0:
                    self.prompt_layer(layer)
                if self.NDEC > 0:
                    self.dec_layer(layer)
            em._deps("sp", [self.out_tok], [])
            for k in list(em.cnt):
                if k.startswith("d_") and em.cnt[k] > 0:
                    nc.sync.wait_ge(em.sem[k], em.cnt[k])
        return nc

    def declare_io(self):
        T = self.NT * 128
        self.T = T
        d = self
        self.x_prompt = d.din("x_prompt", [SEQ, D])
        self.w_in = d.din("w_in", [DEPTH, D, D_IN])
        self.w_out = d.din("w_out", [DEPTH, D, D])
        self.norm_g = d.din("norm_g", [DEPTH, D])
        self.gla_w_gate = d.din("gla_w_gate", [DEPTH, 16, 128])
        self.gla_b_gate = d.din("gla_b_gate", [DEPTH, 128])
        self.gla_norm_g = d.din("gla_norm_g", [DEPTH, 64])
        self.nsa_q_norm_g = d.din("nsa_q_norm_g", [DEPTH, 64])
        self.nsa_k_norm_g = d.din("nsa_k_norm_g", [DEPTH, 3, 64])
        self.nsa_cmp_pos = d.din("nsa_cmp_pos", [DEPTH, 2, 32, 64])
        self.nsa_cmp_w = d.din("nsa_cmp_w", [DEPTH, 2, 32, 64, 64])
        self.ml_conv_w = d.din("ml_conv_w", [DEPTH, 4, 512])
        self.ml_conv_b = d.din("ml_conv_b", [DEPTH, 512])
        self.ml_gate_b = d.din("ml_gate_b", [DEPTH, 8])
        self.ml_norm_g = d.din("ml_norm_g", [DEPTH, 64])
        self.c_rope = d.din("c_rope", [SEQ, 16])
        self.c_ovl = d.din("c_ovl", [256, 64])
        self.c_selb = d.din("c_selb", [32, 128, 64])
        self.y_prompt = d.dout("y_prompt", [SEQ, D])
        self.p_rows = d.dout("p_rows", [DEPTH, SEQ, 512])
        self.p_win = d.dout("p_win", [DEPTH, 512, 256])
        self.p_gla = d.dout("p_gla", [DEPTH, 128, 64])
        self.p_c = d.dout("p_c", [DEPTH, 4, 64, 64])
        self.p_n = d.dout("p_n", [DEPTH, 4, 64])
        self.p_m = d.dout("p_m", [DEPTH, 4])
        self.p_conv = d.dout("p_conv", [DEPTH, 3, 512])
        NPOOL = self.NPOOL
        self.x_sample = d.din("x_sample", [4, 8, D])
        self.cache = [d.din("cache_nsa_kv%d" % l_, [NPOOL * 128 * 2, 256]) for l_ in range(DEPTH)]
        self.st_win = d.din("state_nsa_win", [DEPTH, 4, 512, 256])
        self.st_gla = d.din("state_gla", [DEPTH, 4, 128, 64])
        self.st_c = d.din("state_mlstm_C", [DEPTH, 4, 4, 64, 64])
        self.st_n = d.din("state_mlstm_n", [DEPTH, 4, 4, 64])
        self.st_m = d.din("state_mlstm_m", [DEPTH, 4, 4])
        self.st_conv = d.din("state_mlstm_conv", [DEPTH, 4, 3, 512])
        self.page_table = d.din("page_table", [4, 64], I32)
        self.c_rope_d = d.din("c_rope_d", [8, 16])
        self.c_ovl_d = d.din("c_ovl_d", [512, 130])
        self.c_selb_d = d.din("c_selb_d", [8, 136])
        self.y_sample = d.dout("y_sample", [4, 8, D])
        self.s_rows = d.dout("s_rows", [DEPTH, 4, 8, 512])
        self.s_win = d.dout("s_win", [DEPTH, 4, 512, 256])
        self.s_gla = d.dout("s_gla", [DEPTH, 4, 128, 64])
        self.s_c = d.dout("s_c", [DEPTH, 4, 4, 64, 64])
        self.s_n = d.dout("s_n", [DEPTH, 4, 4, 64])
        self.s_m = d.dout("s_m", [DEPTH, 4, 4])
        self.s_conv = d.dout("s_conv", [DEPTH, 4, 3, 512])
        self.yd1 = d.dint("yd1_scratch", [4, 8, D])
        self.y1 = d.dint("y1_scratch", [SEQ, D])
        self.out_tok = V(None, Buf())

    def setup_consts(self):
        em = self.em
        sb = self.sb
        c = self.c = {}
        stage = self.stage = sb("w_stage", [128, 1024])

        c["ident_f"] = sb("ident_f", [128, 128])
        em.memset(c["ident_f"], 0.0)
        em.asel(c["ident_f"], c["ident_f"], [[-1, 128]], ALU.not_equal, 1.0, 0, 1)
        c["ident_b"] = sb("ident_b", [128, 128], BF16)
        em.cp(c["ident_b"], c["ident_f"])
        c["U"] = sb("U", [128, 128])
        em.memset(c["U"], 1.0)
        em.asel(c["U"], c["U"], [[1, 128]], ALU.is_ge, 0.0, 0, -1)
        c["ones"] = sb("ones", [128, 128])
        em.memset(c["ones"], 1.0)
        c["negm"] = sb("negm", [128, 128])
        em.memset(c["negm"], 0.0)
        em.asel(c["negm"], c["negm"], [[-1, 128]], ALU.is_ge, NEG, 0, 1)
        tmpf = stage[:, 0:128]
        tmpf2 = stage[:, 128:256]
        em.memset(tmpf, 0.0)
        em.asel(tmpf, tmpf, [[1, 128]], ALU.is_ge, NEG, 0, -1)
        c["causal4"] = sb("causal4", [128, 4, 128], BF16)
        for h in range(4):
            em.cp(c["causal4"][:, h, :], tmpf)
        em.memset(tmpf2, 0.0)
        em.asel(tmpf2, tmpf2, [[-1, 128]], ALU.is_ge, NEG, -1, 1)
        c["anti4"] = sb("anti4", [128, 4, 128], BF16)
        for h in range(4):
            em.cp(c["anti4"][:, h, :], tmpf2)
        c["hmask"] = sb("hmask", [128, 4])
        em.memset(c["hmask"], 1.0)
        em.asel(c["hmask"], c["hmask"], [[-32, 4]], ALU.is_ge, 0.0, 0, 1)
        em.asel(c["hmask"], c["hmask"], [[32, 4]], ALU.is_ge, 0.0, 31, -1)
        c["pmask"] = sb("pmask", [128, 2])
        em.tt(c["pmask"][:, 0:1], c["hmask"][:, 0:1], c["hmask"][:, 2:3], ALU.add)
        em.tt(c["pmask"][:, 1:2], c["hmask"][:, 1:2], c["hmask"][:, 3:4], ALU.add)
        c["pmask2"] = sb("pmask2", [128, 2])
        em.tt(c["pmask2"][:, 0:1], c["hmask"][:, 0:1], c["hmask"][:, 1:2], ALU.add)
        em.tt(c["pmask2"][:, 1:2], c["hmask"][:, 2:3], c["hmask"][:, 3:4], ALU.add)
        c["ident4"] = sb("ident4", [128, 4, 128], BF16)
        for h in range(4):
            em.cp(c["ident4"][:, h, :], c["ident_f"])
        ii = self.iota_i32 = sb("iota_i32", [128, 128], I32)
        em.op("pool", lambda e: e.iota(ii.ap, pattern=[[1, 128]], base=0, channel_multiplier=0), writes=[ii])
        c["iota_i"] = sb("iota_i", [128, 128])
        em.cp(c["iota_i"], ii)
        pc_ = sb("pcol_i32", [128, 1], I32)
        em.op("pool", lambda e: e.iota(pc_.ap, pattern=[[0, 1]], base=31, channel_multiplier=16), writes=[pc_])
        c["pcol"] = sb("pcol", [128, 1])
        em.cp(c["pcol"], pc_)
        c["ovl"] = sb("ovl", [128, 2, 64])
        em.dma("sp", c["ovl"], self.c_ovl.rearrange("(t p) j -> p t j", p=128), serial=True)
        c["ovl_b"] = sb("ovl_b", [128, 2, 64], BF16)
        em.cp(c["ovl_b"], c["ovl"])
        c["rope"] = sb("rope", [128, 16])
        self.pA = self.ps("pA", [128, 512])
        self.pB = self.ps("pB", [128, 512])
        self.pT0 = self.ps("pT0", [128, 1024], BF16)
        self.pT1 = self.ps("pT1", [128, 1024], BF16)
        self.pS0 = self.ps("pS0", [128, 512])
        self.pS1 = self.ps("pS1", [128, 512])
        self.pO0 = self.ps("pO0", [128, 512])
        self.pO1 = self.ps("pO1", [128, 512])
        self._pab = 0
        self._pt = 0
        self._psx = 0
        self._po = 0
        w = self.w = {}
        w["win"] = sb("w_in_b", [128, 8, D_IN], BF16)
        w["wout"] = sb("w_out_b", [128, 8, D], BF16)
        w["stage"] = self.stage
        w["normg"] = sb("normg_bc", [128, D])
        w["wgate"] = sb("wgate_b", [16, 128], BF16)
        w["bgate"] = sb("bgate_bc", [128, 128])
        w["glag"] = sb("glag_bc", [128, 64])
        w["qg"] = sb("qg_bc", [128, 64])
        w["kg"] = sb("kg_bc", [128, 3, 64])
        w["mlg"] = sb("mlg_bc", [128, 64])
        w["gateb"] = sb("gateb_bc", [128, 8])
        w["convw"] = sb("convw_col", [128, 4, 4])
        w["convb"] = sb("convb_col", [128, 4])
        w["cmpw"] = sb("cmpw_blk", [128, 2, 32, 128], BF16)
        w["cmpb_tm"] = sb("cmpb_tm", [8, 2, 128])

    def nextAB(self):
        self._pab ^= 1
        return self.pA if self._pab else self.pB

    def nextT(self):
        self._pt ^= 1
        return self.pT0 if self._pt else self.pT1

    def nextS(self):
        self._psx ^= 1
        return self.pS0 if self._psx else self.pS1

    def nextO(self):
        self._po ^= 1
        return self.pO0 if self._po else self.pO1

    def load_layer_weights(self, L):
        em, w, c = self.em, self.w, self.c
        em.redirect_pool = False
        win_v = self.w_in[L].rearrange("(k p) n -> k p n", p=128)
        for k in range(8):
            for c0 in range(0, D_IN, 1024):
                cw = min(1024, D_IN - c0)
                em.dma("sp" if k % 2 == 0 else "act", w["stage"][:, 0:cw], win_v[k, :, c0:c0 + cw], lane="wst", serial=True)
                em.cp(w["win"][:, k, c0:c0 + cw], w["stage"][:, 0:cw], eng=("dve" if k % 2 == 0 else "pool"))
        wout_v = self.w_out[L].rearrange("(k p) n -> k p n", p=128)
        for k in range(8):
            em.dma("sp" if k % 2 == 0 else "act", w["stage"][:, 0:D], wout_v[k], lane="wst", serial=True)
            em.cp(w["wout"][:, k, :], w["stage"][:, 0:D], eng=("dve" if k % 2 == 0 else "pool"))
        em.dma("sp", w["normg"], self.norm_g[L:L + 1, :].to_broadcast([128, D]), serial=True)
        em.dma("sp", w["bgate"], self.gla_b_gate[L:L + 1, :].to_broadcast([128, 128]), serial=True)
        em.dma("sp", w["glag"], self.gla_norm_g[L:L + 1, :].to_broadcast([128, 64]), serial=True)
        em.dma("sp", w["qg"], self.nsa_q_norm_g[L:L + 1, :].to_broadcast([128, 64]), serial=True)
        em.dma("sp", w["kg"], self.nsa_k_norm_g[L:L + 1].to_broadcast([128, 3, 64]), serial=True)
        em.dma("sp", w["mlg"], self.ml_norm_g[L:L + 1, :].to_broadcast([128, 64]), serial=True)
        em.dma("sp", w["gateb"], self.ml_gate_b[L:L + 1, :].to_broadcast([128, 8]), serial=True)
        em.dma("sp", w["stage"][0:16, 0:128], self.gla_w_gate[L], serial=True)
        em.cp(w["wgate"], w["stage"][0:16, 0:128])
        with self.nc.allow_non_contiguous_dma("tiny per-feature column loads"):
            for tap in range(4):
                em.dma("sp", w["convw"][:, :, tap], self.ml_conv_w[L, tap].rearrange("(c p) -> p c", p=128), serial=True)
            em.dma("sp", w["convb"], self.ml_conv_b[L].rearrange("(c p) -> p c", p=128), serial=True)
        em.memset(w["cmpw"], 0.0)
        for kv in range(2):
            for g in range(2):
                for sh in range(2):
                    stg = w["stage"][g * 64:(g + 1) * 64, 0:1024].re("p (s e) -> p s e", e=64)
                    em.dma("sp", stg, self.nsa_cmp_w[L, kv, sh * 16:(sh + 1) * 16].rearrange("s d e -> d s e"), serial=True)
                    em.cp(w["cmpw"][g * 64:(g + 1) * 64, kv, sh * 16:(sh + 1) * 16, g * 64:(g + 1) * 64], stg)
        with self.nc.allow_non_contiguous_dma("tiny transposed pos-emb load"):
            for g in range(2):
                for kk in range(2):
                    em.dma("sp", w["stage"][g * 64:(g + 1) * 64, kk * 32:(kk + 1) * 32],
                           self.nsa_cmp_pos[L, kk].rearrange("s d -> d s"), serial=True)
        for r in range(8):
            em.cp(w["pe_rep"][:, :, :, r], w["stage"][:, 0:64].re("p (k s) -> p k s", k=2))
        for kv in range(2):
            pp = self.nextAB()
            for s in range(32):
                em.mm(pp[0:8, 0:128], w["pe_rep"][:, kv, s, :], w["cmpw"][:, kv, s, :], start=(s == 0), stop=(s == 31))
            em.cp(w["cmpb_tm"][:, kv, :], pp[0:8, 0:128])

    def prompt_layer(self, L):
        em, w, c, sb = self.em, self.w, self.c, self.sb
        NT = self.NT
        P = 128
        last = (L == self.LAYERS - 1)
        b = self.pb
        em.memset(b["S"], 0.0)
        em.memset(b["S_b"], 0.0)
        em.memset(b["Ct"], 0.0)
        em.memset(b["Ct_b"], 0.0)
        em.memset(b["Mrep"], 0.0)
        em.memset(b["qkT"], 0.0)
        em.memset(b["rawT"], 0.0)
        em.memset(b["KcT"], 0.0)
        em.memset(b["VcT"], 0.0)
        em.memset(b["first_prev"], 0.0)
        if L > 0:
            if self.NDEC > 0:
                em.memset(self.dec_allk, 0.0)
                em.memset(self.dec_allv, 0.0)
                em.memset(b["Vslc"][:, :, :, 64:65].wb(self.dec_allv.bs), 1.0)
            else:
                for k_ in ("KslcT", "Vslc"):
                    vv = b[k_].wb(list(b[k_].bs) + self.hb[k_])
                    em.memset(vv, 0.0)
                em.memset(b["Vslc"][:, :, :, 64:65].wb(list(b["Vslc"].bs) + self.hb["Vslc"]), 1.0)
        if not hasattr(self, "y1_v"):
            self.y1_v = [V(self.y1[t_ * 128:(t_ + 1) * 128, :], Buf()) for t_ in range(32)]
        x_src = self.x_prompt if L == 0 else self.y1_v
        y_dst = self.y_prompt if last else self.y1_v
        for t in range(NT):
            self.tile_step(L, t, x_src, y_dst)
        self.write_states(L)

    def alloc_prompt_bufs(self):
        sb = self.sb
        b = self.pb = {}
        b["x"] = sb("x_t", [128, D])
        b["xn"] = sb("xn_b", [128, D], BF16)
        self.w["pe_rep"] = b["xn"][:, 0:512].re("p (k s r) -> p k s r", k=2, s=32)
        b["xnT"] = sb("xnT", [128, 8, 128], BF16)
        b["u"] = sb("u_t", [128, D_IN])
        b["st1"] = sb("stat1", [128, 64])
        b["S"] = sb("gla_S", [128, 64])
        b["S_b"] = sb("gla_S_b", [128, 64], BF16)
        b["ga_b"] = sb("ga_b", [128, 16], BF16)
        b["gaT"] = sb("gaT", [16, 128], BF16)
        b["g1"] = sb("gla_g1", [128, 128])
        b["loga"] = sb("gla_loga", [128, 128])
        b["eb"] = sb("gla_eb", [128, 128])
        b["enb"] = sb("gla_enb", [128, 128])
        b["ekb"] = sb("gla_ekb", [128, 128])
        b["edec"] = sb("gla_edec", [128, 1])
        b["qt_"] = sb("gla_qt", [128, 128], BF16)
        b["kt_"] = sb("gla_kt", [128, 128], BF16)
        b["kh_"] = sb("gla_kh", [128, 128], BF16)
        b["qT_"] = sb("gla_qT", [128, 4, 128], BF16)
        b["kT_"] = sb("gla_kT", [128, 128], BF16)
        b["gv"] = sb("gla_v", [128, 256], BF16)
        b["At"] = sb("gla_At", [128, 4, 128], BF16)
        b["dSc"] = sb("gla_dSc", [128, 64])
        b["go"] = sb("gla_o", [128, 4, 64])
        b["mixed"] = sb("mixed", [128, D], BF16)
        b["mixT"] = b["xnT"]
        b["nrm_sq"] = sb("nrm_sq", [128, 512])
        b["nrm_ss"] = sb("nrm_ss", [128, 8])
        b["silu"] = sb("silu_t", [128, 512])
        b["qkT"] = sb("ml_qkT", [128, 4, 3 + 128])
        b["conv"] = b["silu"].re("p (c i) -> p c i", c=4)
        b["qkc"] = sb("ml_qkc", [128, 4, 128], BF16)
        b["qm"] = sb("ml_qm", [128, 4, 128], BF16)
        b["k_tm"] = sb("ml_k_tm", [128, 256], BF16)
        b["vext"] = sb("ml_vext", [128, 4, 65], BF16)
        b["vw"] = sb("ml_vw", [128, 4, 65], BF16)
        b["gates"] = sb("ml_gates", [128, 8])
        b["fi"] = sb("ml_fi", [128, 4])
        b["F"] = sb("ml_F", [128, 4])
        b["gj"] = sb("ml_g", [128, 4])
        b["diag"] = None
        b["tmp"] = sb("ml_tmp", [128, 4, 128])
        b["dSm"] = b["tmp"].re("p c i -> p (c i)")[:, 0:256].re("p (h e) -> p h e", h=4)
        b["Dm"] = b["tmp"]
        b["sij"] = sb("ml_sij", [128, 4, 128], BF16)
        b["sijT"] = sb("ml_sijT", [128, 4, 128], BF16)
        b["mx"] = sb("ml_mx", [128, 4])
        b["nmx"] = sb("ml_nmx", [128, 4])
        b["m"] = sb("ml_m", [128, 8])
        b["wint"] = sb("ml_wint", [128, 4])
        b["Mrep"] = sb("ml_Mrep", [128, 4])
        b["lastrep"] = sb("ml_lastrep", [128, 8])
        b["decay"] = sb("ml_decay", [128, 4])
        b["wj"] = sb("ml_wj", [128, 4])
        b["Ct"] = sb("ml_Ct", [128, 4, 65])
        b["Ct_b"] = sb("ml_Ct_b", [128, 4, 65], BF16)
        b["num"] = sb("ml_num", [128, 4, 65])
        b["den"] = sb("ml_den", [128, 4])
        b["den2"] = sb("ml_den2", [128, 4])
        b["hh"] = sb("ml_h", [128, 4, 64])
        b["sel_last"] = sb("sel_last", [128, 128])
        em = self.em
        em.memset(b["sel_last"], 0.0)
        em.asel(b["sel_last"], b["sel_last"], [[0, 128]], ALU.not_equal, 1.0, -127, 1)
        b["sel_last8"] = self.sb("sel_last8", [8, 128])
        em.memset(b["sel_last8"], 0.0)
        em.asel(b["sel_last8"], b["sel_last8"], [[0, 128]], ALU.not_equal, 1.0, -7, 1)
        b["qn"] = b["silu"].re("p (h e) -> p h e", h=8)
        b["qr"] = sb("nsa_qr", [128, 8, 64], BF16)
        b["qn_b"] = sb("nsa_qn_b", [128, 8, 64], BF16)
        b["rt1"] = sb("rope_t1", [128, 8, 8])
        b["rt2"] = sb("rope_t2", [128, 8, 8])
        b["rows"] = sb("nsa_rows", [128, 4, 128])
        b["win"] = sb("nsa_win", [128, 2, 128])
        b["kk_b"] = sb("nsa_kk_b", [128, 4, 128], BF16)
        b["QTr"] = sb("nsa_QTr", [128, 2, 4, 128], BF16)
        b["QTn"] = sb("nsa_QTn", [128, 2, 4, 128], BF16)
        b["KslcT"] = sb("KslcT", [128, SEQ], BF16)
        b["KwinT"] = sb("KwinT", [128, 5 * 128], BF16)
        b["Vslc"] = sb("Vslc", [128, 32, 2, 65], BF16)
        b["Vwin"] = sb("Vwin", [128, 5, 2, 65], BF16)
        b["rawT"] = sb("rawT", [128, 2, 16 + 128], BF16)
        b["KcT"] = sb("KcT", [128, 256], BF16)
        b["VcT"] = sb("VcT", [128, 256], BF16)
        b["Vc"] = sb("Vc", [128, 2, 2, 129], BF16)
        b["kc_tm"] = sb("kc_tm", [8, 2, 128])
        b["kc_n"] = sb("kc_n", [8, 2, 64], BF16)
        b["vc_b"] = sb("vc_b", [8, 128], BF16)
        b["first_prev"] = sb("first_prev", [8, 1])
        b["cmpmask"] = sb("cmpmask", [128, 128])
        b["cmpmask4"] = sb("cmpmask4", [128, 4, 128], BF16)
        b["PT"] = sb("PT", [128, 4, 128], BF16)
        b["PT2"] = sb("PT2", [128, 4, 128], BF16)
        b["ocmp"] = b["tmp"].re("p c i -> p (c i)").re("p (h e) -> p h e", h=8)
        b["rden"] = sb("rden", [128, 8])
        b["imp"] = sb("imp", [128, 2, 64])
        b["impt"] = sb("impt", [128, 4, 64])
        b["selb"] = sb("selb", [128, 64])
        b["score"] = sb("score", [128, 2, 64])
        b["score2"] = sb("score2", [128, 64])
        b["max8"] = sb("max8", [128, 8])
        b["selbias"] = sb("selbias", [128, 2, 64], BF16)
        b["gate"] = sb("nsa_gate", [128, 24])
        b["selx"] = sb("selx", [128, 2, 2, 64], BF16)
        b["ob"] = sb("nsa_ob", [128, 8, 64])
        b["diag"] = b["ob"].re("p h e -> p (h e)").re("p (c i) -> p c i", c=4)
        self.hb = {k: [Buf() for _ in range(33)] for k in ("KslcT", "KwinT", "Vslc", "Vwin")}
        def allb(k):
            return b[k].wb(list(b[k].bs) + self.hb[k])
        em.memset(allb("Vslc"), 0.0)
        em.memset(allb("Vwin"), 0.0)
        em.memset(allb("Vslc")[:, :, :, 64:65], 1.0)
        em.memset(allb("Vwin")[:, :, :, 64:65], 1.0)
        em.memset(allb("KslcT"), 0.0)
        em.memset(allb("KwinT"), 0.0)
        em.memset(b["Vc"], 0.0)
        em.memset(b["Vc"][:, :, :, 64:65], 1.0)
        for g in range(2):
            em.cp(b["Vc"][:, :, g, 65:129], self.c["ovl_b"])
        em.memset(b["vext"][:, :, 64:65], 1.0)
        em.memset(b["mixed"], 0.0)

    def rms_heads(self, out, in_, H, gain_bc, P=128, extra_scale=1.0):
        em, b = self.em, self.pb
        sq = b["nrm_sq"][0:P, 0:H * 64].re("p (h e) -> p h e", e=64)
        ss = b["nrm_ss"][0:P, 0:H]
        em.tt(sq, in_, in_, ALU.mult)
        em.red(ss, sq)
        em.ts(ss, ss, 1.0 / 64, ALU.mult, EPS, ALU.add)
        em.act(ss, ss, AF.Sqrt)
        em.recip(ss, ss)
        em.tt(sq, in_, ss.un(2).bc([P, H, 64]), ALU.mult)
        if extra_scale != 1.0:
            em.stt(out, sq, extra_scale, gain_bc.un(1).bc([P, H, 64]), ALU.mult, ALU.mult)
        else:
            em.tt(out, sq, gain_bc.un(1).bc([P, H, 64]), ALU.mult)

    def rope(self, out, in_, H, cs, P=128):
        em, b = self.em, self.pb
        cos = cs[:, 0:8].un(1).bc([P, H, 8])
        sin = cs[:, 8:16].un(1).bc([P, H, 8])
        t1 = b["rt1"][0:P, 0:H, :]
        t2 = b["rt2"][0:P, 0:H, :]
        x1 = in_[:, :, 0:8]
        x2 = in_[:, :, 8:16]
        em.cp(out[:, :, 16:64], in_[:, :, 16:64], eng="pool")
        em.tt(t1, x1, cos, ALU.mult)
        em.tt(t2, x2, sin, ALU.mult)
        em.tt(out[:, :, 0:8], t1, t2, ALU.subtract)
        em.tt(t1, x2, cos, ALU.mult)
        em.tt(t2, x1, sin, ALU.mult)
        em.tt(out[:, :, 8:16], t1, t2, ALU.add)

    def out_dma(self, q, dst, src):
        self.em.dma(q, dst, src, lane="out_" + q)
        pass

    def tile_step(self, L, t, x_src, y_dst):
        em, w, c, b = self.em, self.w, self.c, self.pb
        em.redirect_pool = True
        P = 128
        r0 = t * 128
        last_tile = (t == self.NT - 1)
        x, u = b["x"], b["u"]
        em.dma("sp", x, x_src[r0:r0 + 128, :] if not isinstance(x_src, list) else x_src[t], lane="x")
        ss = b["st1"][:, 0:1]
        em.act(u[:, 0:D], x, AF.Square, accum=ss)
        em.ts(ss, ss, 1.0 / D, ALU.mult, EPS, ALU.add)
        em.act(ss, ss, AF.Sqrt)
        em.recip(ss, ss)
        em.stt(b["xn"], x, ss, w["normg"], ALU.mult, ALU.mult)
        for hf in range(2):
            pt = self.nextT()
            for k4 in range(4):
                k = hf * 4 + k4
                em.tr(pt[:, k4 * 128:(k4 + 1) * 128], b["xn"][:, k * 128:(k + 1) * 128], c["ident_b"])
            em.cp(b["xnT"][:, hf * 4:hf * 4 + 4, :].re("p k i -> p (k i)"), pt[:, 0:512], eng=("act" if hf else "dve"))
        chunks = [(0, 512), (512, 512), (1024, 512), (1536, 512), (2048, 512), (2560, 40), (3112, 512), (3624, 264)]
        for ci, (c0, cw) in enumerate(chunks):
            pp = self.nextAB()
            for k in range(8):
                em.mm(pp[:, 0:cw], b["xnT"][:, k, :], w["win"][:, k, c0:c0 + cw], start=(k == 0), stop=(k == 7))
            em.cp(u[:, c0:c0 + cw], pp[:, 0:cw], eng=("act" if ci % 2 else "dve"))
        em.cp(b["qkT"][:, :, 0:3], b["qkT"][:, :, 128:131], eng="pool")
        for cch in range(4):
            pp = self.nextAB()
            for k in range(8):
                em.mm(pp[:, 0:128], w["win"][:, k, O_MQK + cch * 128:O_MQK + (cch + 1) * 128], b["xnT"][:, k, :],
                      start=(k == 0), stop=(k == 7))
            em.cp(b["qkT"][:, cch, 3:131], pp[:, 0:128], eng=("act" if cch % 2 else "dve"))
        import os as _os
        if not _os.environ.get("SKIP_GLA"):
            self.gla_tile(L, P)
        if not _os.environ.get("SKIP_ML"):
            self.mlstm_tile(L, P)
        if not _os.environ.get("SKIP_NSA"):
            self.nsa_tile(L, t)
        for hf in range(2):
            pt = self.nextT()
            for k4 in range(4):
                k = hf * 4 + k4
                em.tr(pt[:, k4 * 128:(k4 + 1) * 128], b["mixed"][:, k * 128:(k + 1) * 128], c["ident_b"])
            em.cp(b["mixT"][:, hf * 4:hf * 4 + 4, :].re("p k i -> p (k i)"), pt[:, 0:512], eng=("act" if hf else "dve"))
        for half in range(2):
            pp = self.nextAB()
            for k in range(8):
                em.mm(pp[:, 0:512], b["mixT"][:, k, :], w["wout"][:, k, half * 512:(half + 1) * 512],
                      start=(k == 0), stop=(k == 7))
            em.tt(x[:, half * 512:(half + 1) * 512], pp[:, 0:512], x[:, half * 512:(half + 1) * 512], ALU.add)
        self.out_dma("sp", y_dst[r0:r0 + 128, :] if not isinstance(y_dst, list) else y_dst[t], x)
        if last_tile:
            with self.nc.allow_non_contiguous_dma("tiny conv state"):
                for r in range(3):
                    self.out_dma("sp", self.p_conv[L, r].rearrange("(c p) -> p c", p=128), b["qkT"][:, :, 128 + r])

    def gla_tile(self, L, P):
        em, w, c, b = self.em, self.w, self.c, self.pb
        u = b["u"]
        em.cp(b["ga_b"][0:P], u[0:P, O_GA:O_GA + 16])
        pt = self.nextT()
        em.tr(pt[0:16, 0:P], b["ga_b"][0:P], c["ident_b"][0:P, 0:P])
        em.cp(b["gaT"][:, 0:P], pt[0:16, 0:P])
        pp = self.nextAB()
        em.mm(pp[0:P, 0:128], b["gaT"][:, 0:P], w["wgate"])
        g1 = b["g1"][0:P]
        em.tt(g1, pp[0:P, 0:128], w["bgate"][0:P], ALU.add)
        em.act(g1, g1, AF.Exp, scale=-1.0)
        em.act(g1, g1, AF.Ln, bias=1.0)
        loga = b["loga"][0:P]
        em.ts(loga, g1, -1.0 / 16.0, ALU.mult)
        pp = self.nextAB()
        em.mm(pp[0:P, 0:128], c["U"][0:P, 0:P], loga)
        em.mm(pp[0:P, 128:256], c["ones"][0:P, 0:P], loga)
        em.mm(pp[:, 256:257], loga, c["ones"][0:P, 0:1])
        em.act(b["eb"][0:P], pp[0:P, 0:128], AF.Exp)
        em.act(b["enb"][0:P], pp[0:P, 0:128], AF.Exp, scale=-1.0)
        em.cp(g1, pp[0:P, 0:128], eng="act")
        em.tt(b["ekb"][0:P], pp[0:P, 128:256], g1, ALU.subtract)
        em.act(b["ekb"][0:P], b["ekb"][0:P], AF.Exp)
        em.act(b["edec"], pp[:, 256:257], AF.Exp)
        em.stt(b["qt_"][0:P], u[0:P, O_GQ:O_GQ + 128], 32 ** -0.5, b["eb"][0:P], ALU.mult, ALU.mult)
        em.tt(b["kt_"][0:P], u[0:P, O_GK:O_GK + 128], b["enb"][0:P], ALU.mult)
        em.tt(b["kh_"][0:P], u[0:P, O_GK:O_GK + 128], b["ekb"][0:P], ALU.mult)
        em.cp(b["gv"][0:P], u[0:P, O_GV:O_GV + 256], eng="pool")
        pt = self.nextT()
        em.tr(pt[:, 0:P], b["qt_"][0:P], c["ident_b"][0:P, 0:P])
        em.tr(pt[:, 128:128 + P], b["kt_"][0:P], c["ident_b"][0:P, 0:P])
        for h in range(4):
            em.ts(b["qT_"][:, h, 0:P], pt[:, 0:P], c["hmask"][:, h:h + 1], ALU.mult)
        em.cp(b["kT_"][:, 0:P], pt[:, 128:128 + P])
        pa = self.nextAB()
        pav = pa[0:P, 0:4 * P].re("p (h i) -> p h i", h=4)
        for h in range(4):
            em.mm(pav[:, h, :], b["kT_"][:, 0:P], b["qT_"][:, h, 0:P])
        At = b["At"][0:P, :, 0:P]
        em.tt(At, pav, c["U"][0:P, 0:P].un(1).bc([P, 4, P]), ALU.mult)
        po = self.nextAB()
        pov = po[0:P, 0:256].re("p (h e) -> p h e", h=4)
        for h in range(4):
            em.mm(pov[:, h, :], At[:, h, :], b["gv"][0:P, h * 64:(h + 1) * 64], start=True, stop=False)
            em.mm(pov[:, h, :], b["qT_"][:, h, 0:P], b["S_b"], start=False, stop=True)
        go = b["go"][0:P]
        em.cp(go, pov)
        pd = self.nextAB()
        em.mm(pd[:, 0:256], b["kh_"][0:P], b["gv"][0:P])
        em.tt(b["dSm"], pd[:, 0:256].re("p (h e) -> p h e", h=4), c["hmask"].un(2).bc([128, 4, 64]), ALU.mult)
        em.red(b["dSc"], b["dSm"].re("p h e -> p e h"))
        em.stt(b["S"], b["S"], b["edec"], b["dSc"], ALU.mult, ALU.add)
        em.cp(b["S_b"], b["S"], eng="act")
        mixv = b["mixed"][0:P, 0:256].re("p (h e) -> p h e", h=4)
        sl = b["silu"][0:P, 0:256]
        em.act(sl, u[0:P, O_GZ:O_GZ + 256], AF.Silu)
        self.rms_heads(go, go, 4, w["glag"][0:P], P)
        em.tt(mixv, go, sl.re("p (h e) -> p h e", h=4), ALU.mult)

    def mlstm_tile(self, L, P):
        em, w, c, b = self.em, self.w, self.c, self.pb
        u = b["u"]
        for cch in range(4):
            cv = b["conv"][:, cch, 0:P]
            em.ts(cv, b["qkT"][:, cch, 0:P], w["convw"][:, cch, 0:1], ALU.mult)
            for tap in range(1, 4):
                em.stt(cv, b["qkT"][:, cch, tap:tap + P], w["convw"][:, cch, tap:tap + 1], cv, ALU.mult, ALU.add)
            em.act(b["qkc"][:, cch, 0:P], cv, AF.Silu, bias=w["convb"][:, cch:cch + 1])
        for h in range(4):
            em.ts(b["qm"][:, h, 0:P], b["qkc"][:, h // 2, 0:P], c["pmask2"][:, h % 2:h % 2 + 1], ALU.mult)
        pt = self.nextT()
        for j in range(2):
            em.tr(pt[0:P, j * 128:(j + 1) * 128], b["qkc"][:, 2 + j, 0:P], c["ident_b"])
        em.ts(b["k_tm"][0:P], pt[0:P, 0:256], 0.125, ALU.mult)
        em.cp(b["vext"][0:P, :, 0:64], u[0:P, O_MV:O_MV + 256].re("p (h e) -> p h e", h=4), eng="pool")
        gt = b["gates"][0:P]
        em.tt(gt, u[0:P, O_MIF:O_MIF + 8], w["gateb"][0:P], ALU.add)
        fi = b["fi"][0:P]
        em.act(fi, gt[:, 4:8], AF.Exp, scale=-1.0)
        em.act(fi, fi, AF.Ln, bias=1.0)
        em.ts(fi, fi, -1.0, ALU.mult)
        pp = self.nextAB()
        em.mm(pp[0:P, 0:4], c["U"][0:P, 0:P], fi)
        Fm = b["m"][0:P]
        em.cp(Fm[:, 0:4], pp[0:P, 0:4])
        gj = b["gj"][0:P]
        em.tt(gj, gt[:, 0:4], Fm[:, 0:4], ALU.subtract)
        dg = b["diag"][0:P, :, 0:P]
        em.tt(dg, c["ident_f"][0:P, 0:P].un(1).bc([P, 4, P]), gj.un(2).bc([P, 4, P]), ALU.mult)
        pg = self.nextAB()
        pgv = pg[0:P, 0:4 * P].re("p (h j) -> p h j", h=4)
        em.mm(pgv, c["ones"][0:P, 0:P], dg)
        tmp = b["tmp"][0:P, :, 0:P]
        em.tt(tmp, pgv, c["negm"][0:P, 0:P].un(1).bc([P, 4, P]), ALU.add)
        mx = b["mx"][0:P]
        em.red(mx, tmp, op=ALU.max)
        em.tt(mx, mx, b["Mrep"][0:P], ALU.max)
        em.tt(Fm[:, 4:8], Fm[:, 0:4], mx, ALU.add)
        nmx = b["nmx"][0:P]
        em.ts(nmx, mx, -1.0, ALU.mult)
        Dm = b["Dm"][0:P, :, 0:P]
        for h in range(4):
            em.act(Dm[:, h, :], tmp[:, h, :], AF.Exp, bias=nmx[:, h:h + 1])
        wint = b["wint"][0:P]
        em.tt(wint, b["Mrep"][0:P], mx, ALU.subtract)
        em.act(wint, wint, AF.Exp)
        psc = self.nextS()
        pscv = psc[0:P, 0:4 * P].re("p (h j) -> p h j", h=4)
        for h in range(4):
            em.mm(pscv[:, h, :], b["qm"][:, h, 0:P], b["qkc"][:, 2 + h // 2, 0:P])
        sij = b["sij"][0:P, :, 0:P]
        em.stt(sij, pscv, 0.125, Dm, ALU.mult, ALU.mult)
        pt = self.nextT()
        ptv = pt[0:P, 0:4 * P].re("p (h i) -> p h i", h=4)
        for h in range(4):
            em.tr(ptv[:, h, :], sij[:, h, :], c["ident_b"][0:P, 0:P])
        sijT = b["sijT"][0:P, :, 0:P]
        em.cp(sijT, ptv)
        pn = self.nextO()
        pnv = pn[0:P, 0:260].re("p (h e) -> p h e", h=4)
        pi_ = self.nextO()
        piv = pi_[0:P, 0:260].re("p (h e) -> p h e", h=4)
        for h in range(4):
            em.mm(pnv[:, h, :], sijT[:, h, :], b["vext"][0:P, h, :])
            em.mm(piv[:, h, :], b["qm"][:, h, 0:P], b["Ct_b"][:, h, :])
        num = b["num"][0:P]
        em.tt(num, piv, wint.un(2).bc([P, 4, 65]), ALU.mult)
        em.tt(num, num, pnv, ALU.add)
        den = b["den"][0:P]
        em.stt(den, num[:, :, 64], -1.0, num[:, :, 64], ALU.mult, ALU.max)
        den2 = b["den2"][0:P]
        em.act(den2, Fm[:, 4:8], AF.Exp, scale=-1.0)
        em.tt(den, den, den2, ALU.max)
        em.recip(den, den)
        hh = b["hh"][0:P]
        em.tt(hh, num[:, :, 0:64], den.un(2).bc([P, 4, 64]), ALU.mult)
        pl = self.nextAB()
        em.mm(pl[0:128, 0:8], (b["sel_last"] if P == 128 else b["sel_last8"])[0:P, :], Fm)
        lr = b["lastrep"]
        em.cp(lr, pl[:, 0:8])
        dec = b["decay"]
        em.tt(dec, lr[:, 0:4], b["Mrep"], ALU.add)
        em.tt(dec, dec, lr[:, 4:8], ALU.subtract)
        em.act(dec, dec, AF.Exp)
        wj = b["wj"][0:P]
        em.tt(wj, gj, lr[0:P, 0:4], ALU.add)
        em.tt(wj, wj, lr[0:P, 4:8], ALU.subtract)
        em.act(wj, wj, AF.Exp)
        em.tt(b["vw"][0:P], b["vext"][0:P], wj.un(2).bc([P, 4, 65]), ALU.mult)
        pc = self.nextAB()
        pcv = pc[:, 0:260].re("p (h e) -> p h e", h=4)
        for h in range(4):
            em.mm(pcv[:, h, :], b["k_tm"][0:P, (h // 2) * 128:(h // 2 + 1) * 128], b["vw"][0:P, h, :])
        em.tt(b["Ct"], b["Ct"], dec.un(2).bc([128, 4, 65]), ALU.mult)
        em.tt(b["Ct"], b["Ct"], pcv, ALU.add)
        em.cp(b["Ct_b"], b["Ct"], eng="act")
        em.cp(b["Mrep"], lr[:, 4:8])
        self.rms_heads(hh, hh, 4, w["mlg"][0:P], P)
        sl = b["silu"][0:P, 0:256]
        em.act(sl, u[0:P, O_MO:O_MO + 256], AF.Sigmoid)
        em.tt(hh, hh, sl.re("p (h e) -> p h e", h=4), ALU.mult)
        em.act(sl, u[0:P, O_MZ:O_MZ + 256], AF.Silu)
        em.tt(b["mixed"][0:P, 768:1024].re("p (h e) -> p h e", h=4), hh, sl.re("p (h e) -> p h e", h=4), ALU.mult)

    def attn_block(self, KT, QT, extra, Vr, PTb, acc, first):
        em = self.em
        ps_ = self.nextS()
        nk = KT.ap.shape[-1]
        psv = ps_[0:nk, 0:512]
        em.mm(psv, KT, QT.re("p h i -> p (h i)"), start=True, stop=(len(extra) == 0))
        for ei, (lt, rh) in enumerate(extra):
            em.mm(psv, lt, rh, start=False, stop=(ei == len(extra) - 1))
        em.act(PTb[0:nk].re("p h i -> p (h i)"), psv, AF.Exp, scale=ATT_SCALE)
        po = self.nextO()
        po_v = po[:, 0:260].re("p (h e) -> p h e", h=4)
        for hh in range(4):
            em.mm(po_v[:, hh, :], PTb[0:nk, hh, :], Vr, start=True, stop=True)
        if first:
            em.cp(acc, po_v)
        else:
            em.tt(acc, acc, po_v, ALU.add)

    def nsa_tile(self, L, t):
        import os as _os
        em, w, c, b, hb = self.em, self.w, self.c, self.pb, self.hb
        u = b["u"]
        P = 128
        r0 = t * 128
        cs = c["rope"]
        em.dma("sp", cs, self.c_rope[r0:r0 + 128, :], lane="cst")
        qn = b["qn"]
        self.rms_heads(qn, u[:, O_NQ:O_NQ + 512].re("p (h e) -> p h e", h=8), 8, w["qg"])
        em.cp(b["qn_b"].re("p (hh g) e -> p g hh e", g=2), qn.re("p (g hh) e -> p g hh e", g=2), eng="pool")
        self.rope(b["ocmp"], qn, 8, cs)
        em.cp(b["qr"].re("p (hh g) e -> p g hh e", g=2), b["ocmp"].re("p (g hh) e -> p g hh e", g=2))
        kv = u[:, O_NKV:O_NKV + 768].re("p (s g e) -> p s g e", s=6, g=2)
        rows = b["rows"].re("p s (g e) -> p s g e", g=2)
        winr = b["win"].re("p s (g e) -> p s g e", g=2)
        em.cp(rows[:, 0], kv[:, 0], eng="pool")
        em.cp(rows[:, 1], kv[:, 1], eng="pool")
        em.cp(rows[:, 3], kv[:, 3], eng="pool")
        em.cp(winr[:, 1], kv[:, 5], eng="pool")
        ktmp = b["ocmp"][:, 0:2, :]
        self.rms_heads(ktmp, kv[:, 2], 2, w["kg"][:, 1, :])
        self.rope(rows[:, 2], ktmp, 2, cs)
        self.rms_heads(ktmp, kv[:, 4], 2, w["kg"][:, 2, :])
        self.rope(winr[:, 0], ktmp, 2, cs)
        self.out_dma("act", self.p_rows[L, r0:r0 + 128, :], b["rows"].re("p s f -> p (s f)"))
        if r0 >= SEQ - 512:
            w0 = r0 - (SEQ - 512)
            self.out_dma("act", self.p_win[L, w0:w0 + 128, :], b["win"].re("p s f -> p (s f)"))
        kkb = b["kk_b"]
        em.cp(kkb[:, 0:2, :], b["rows"][:, 0:2, :])
        em.cp(kkb[:, 2, :], b["rows"][:, 2, :])
        em.cp(kkb[:, 3, :], b["win"][:, 0, :])
        em.cp(b["Vslc"][:, t, :, 0:64].wb(hb["Vslc"][t]), rows[:, 3], eng="pool")
        em.cp(b["Vwin"][:, t % 5, :, 0:64].wb(hb["Vwin"][t % 5]), winr[:, 1], eng="pool")
        pt = self.nextT()
        for j in range(4):
            em.tr(pt[:, j * 128:(j + 1) * 128], kkb[:, j, :], c["ident_b"])
        em.cp(b["rawT"][:, :, 0:16], b["rawT"][:, :, 128:144], eng="pool")
        em.cp(b["rawT"][:, :, 16:144], pt[:, 0:256].re("p (k i) -> p k i", k=2))
        em.cp(b["KslcT"][:, r0:r0 + 128].wb(hb["KslcT"][t]), pt[:, 256:384], eng="act")
        em.cp(b["KwinT"][:, (t % 5) * 128:(t % 5 + 1) * 128].wb(hb["KwinT"][t % 5]), pt[:, 384:512], eng="act")
        pt = self.nextT()
        qr4 = b["qr"].re("p (hh g) e -> p hh (g e)", g=2)
        qn4 = b["qn_b"].re("p (hh g) e -> p hh (g e)", g=2)
        for hh in range(4):
            em.tr(pt[:, hh * 128:(hh + 1) * 128], qr4[:, hh, :], c["ident_b"])
            em.tr(pt[:, 512 + hh * 128:512 + (hh + 1) * 128], qn4[:, hh, :], c["ident_b"])
        for g in range(2):
            em.ts(b["QTr"][:, g].re("p h i -> p (h i)"), pt[:, 0:512], c["pmask2"][:, g:g + 1], ALU.mult)
            em.ts(b["QTn"][:, g].re("p h i -> p (h i)"), pt[:, 512:1024], c["pmask2"][:, g:g + 1], ALU.mult)
        for kvi in range(2):
            pp = self.nextAB()
            for s in range(32):
                em.mm(pp[0:8, 0:128], b["rawT"][:, kvi, s:s + 113:16], w["cmpw"][:, kvi, s, :], start=(s == 0), stop=(s == 31))
            em.tt(b["kc_tm"][:, kvi, :], pp[0:8, 0:128], w["cmpb_tm"][:, kvi, :], ALU.add)
        self.rms_heads(b["kc_n"], b["kc_tm"][:, 0, :].re("p (g e) -> p g e", g=2), 2, w["kg"][0:8, 0, :], P=8)
        em.cp(b["vc_b"], b["kc_tm"][:, 1, :])
        pt = self.nextT()
        em.tr(pt[:, 0:8], b["kc_n"].re("p g e -> p (g e)"), c["ident_b"][0:8, 0:8])
        em.tr(pt[:, 8:16], b["vc_b"], c["ident_b"][0:8, 0:8])
        n0 = 8 * t - 1
        if t == 0:
            em.cp(b["KcT"][:, 0:7], pt[:, 1:8])
            em.cp(b["VcT"][:, 0:7], pt[:, 9:16])
        else:
            em.cp(b["KcT"][:, n0:n0 + 8], pt[:, 0:8])
            em.cp(b["VcT"][:, n0:n0 + 8], pt[:, 8:16])
        nvis = 8 * t + 7
        ntl = 1 if nvis <= 128 else 2
        fr = ntl - 1
        for rt in sorted({fr} | ({0} if t == 16 else set())):
            pt = self.nextT()
            em.tr(pt[:, 0:128], b["VcT"][:, rt * 128:(rt + 1) * 128], c["ident_b"])
            em.cp(b["Vc"][:, rt, :, 0:64], pt[:, 0:128].re("p (g e) -> p g e", g=2))
        for nt_ in range(ntl):
            if 16 * (nt_ * 128 + 127) + 31 <= 128 * t:
                continue
            em.ts(b["cmpmask"], c["iota_i"], c["pcol"], ALU.subtract, float(2048 * nt_ - 128 * t), ALU.is_lt)
            em.ts(b["cmpmask"], b["cmpmask"], NEG, ALU.mult)
            for hh in range(2):
                em.cp(b["cmpmask4"][:, nt_ * 2 + hh, :], b["cmpmask"], eng=("dve" if hh % 2 else "pool"))
        em.act(b["gate"], u[:, O_NG:O_NG + 24], AF.Sigmoid)
        for g in range(2):
            for half in range(2):
                pov = b["nrm_sq"][:, 0:258].re("p (h e) -> p h e", h=2)
                for nt_ in range(ntl):
                    ps_ = self.nextS()
                    psv = ps_[:, 0:256]
                    qsl = b["QTn"][:, g, half * 2:half * 2 + 2, :].re("p h i -> p (h i)")
                    masked = not (16 * (nt_ * 128 + 127) + 31 <= 128 * t)
                    em.mm(psv, b["KcT"][:, nt_ * 128:(nt_ + 1) * 128], qsl, start=True, stop=not masked)
                    if masked:
                        em.mm(psv, c["ident_b"], b["cmpmask4"][:, nt_ * 2:nt_ * 2 + 2, :].re("p h i -> p (h i)"), start=False, stop=True)
                    PTb = b["PT"] if nt_ == 0 else b["PT2"]
                    em.act(PTb[:, 0:2, :].re("p h i -> p (h i)"), psv, AF.Exp, scale=ATT_SCALE)
                    po = self.nextO()
                    pcv = po[:, 0:258].re("p (h e) -> p h e", h=2)
                    for h2 in range(2):
                        em.mm(pcv[:, h2, :], PTb[:, h2, :], b["Vc"][:, nt_, g, :], start=True, stop=True)
                    if nt_ == 0:
                        em.cp(pov, pcv)
                    else:
                        em.tt(pov, pov, pcv, ALU.add)
                rd = b["rden"][:, 0:2]
                em.ts(rd, pov[:, :, 64], 1e-30, ALU.max)
                em.recip(rd, rd)
                h0 = g * 4 + half * 2
                em.tt(b["ocmp"][:, 0:2, :], pov[:, :, 0:64], rd.un(2).bc([128, 2, 64]), ALU.mult)
                em.tt(b["ob"][:, h0:h0 + 2, :], b["ocmp"][:, 0:2, :], b["gate"][:, h0:h0 + 2].un(2).bc([128, 2, 64]), ALU.mult)
                em.tt(b["impt"][:, half * 2:half * 2 + 2, :], pov[:, :, 65:129], rd.un(2).bc([128, 2, 64]), ALU.mult)
            em.red(b["imp"][:, g, :], b["impt"].re("p h j -> p j h"))
        em.dma("sp", b["selb"], self.c_selb[t], lane="cst2")
        for g in range(2):
            sc = b["score"][:, g, :]
            em.tt(sc, b["imp"][:, g, :], b["selb"], ALU.add)
            em.op("dve", lambda e: e.max(out=b["max8"].ap, in_=sc.ap), reads=[sc], writes=[b["max8"]])
            em.op("dve", lambda e: e.match_replace(out=b["score2"].ap, in_to_replace=b["max8"].ap, in_values=sc.ap,
                                                   imm_value=-3.0e38), reads=[sc, b["max8"]], writes=[b["score2"]])
            em.op("dve", lambda e: e.max(out=b["max8"].ap, in_=b["score2"].ap), reads=[b["score2"]], writes=[b["max8"]])
            em.ts(b["score2"], sc, b["max8"][:, 7:8], ALU.is_ge, -1.0, ALU.add)
            em.ts(b["selbias"][:, g, :], b["score2"], -NEG, ALU.mult)
        for g in range(2):
            QT = b["QTr"][:, g]
            pov = b["num"]
            for kt in range(t + 1):
                if kt == t:
                    extra = [(c["ident_b"], c["causal4"].re("p h i -> p (h i)"))]
                else:
                    sx = b["selx"][:, kt % 2]
                    em.cp(sx, b["selbias"][:, g, 2 * kt:2 * kt + 2].un(2).bc([128, 2, 64]), eng=("pool" if kt % 2 else "dve"))
                    extra = [(sx.re("p a b -> p (a b)"), c["ident4"].re("p h i -> p (h i)"))]
                PTb = b["PT"] if kt % 2 == 0 else b["PT2"]
                self.attn_block(b["KslcT"][:, kt * 128:(kt + 1) * 128].wb(hb["KslcT"][kt]), QT, extra,
                                b["Vslc"][:, kt, g, :].wb(hb["Vslc"][kt]), PTb, pov, kt == 0)
            rd = b["rden"][:, 0:4]
            em.recip(rd, pov[:, :, 64])
            em.tt(b["ocmp"][:, 0:4, :], pov[:, :, 0:64], rd.un(2).bc([128, 4, 64]), ALU.mult)
            em.tt(b["ocmp"][:, 0:4, :], b["ocmp"][:, 0:4, :], b["gate"][:, 8 + g * 4:12 + g * 4].un(2).bc([128, 4, 64]), ALU.mult)
            if not _os.environ.get("NSA_NO_SLC"):
                em.tt(b["ob"][:, g * 4:(g + 1) * 4, :], b["ob"][:, g * 4:(g + 1) * 4, :], b["ocmp"][:, 0:4, :], ALU.add)
            pov = b["num"]
            k0 = max(0, t - 4)
            for kt in range(k0, t + 1):
                extra = []
                if kt == t:
                    extra.append((c["ident_b"], c["causal4"].re("p h i -> p (h i)")))
                elif kt == t - 4:
                    extra.append((c["ident_b"], c["anti4"].re("p h i -> p (h i)")))
                PTb = b["PT"] if kt % 2 == 0 else b["PT2"]
                self.attn_block(b["KwinT"][:, (kt % 5) * 128:(kt % 5 + 1) * 128].wb(hb["KwinT"][kt % 5]), QT, extra,
                                b["Vwin"][:, kt % 5, g, :].wb(hb["Vwin"][kt % 5]), PTb, pov, kt == k0)
            rd = b["rden"][:, 4:8]
            em.recip(rd, pov[:, :, 64])
            em.tt(b["ocmp"][:, 4:8, :], pov[:, :, 0:64], rd.un(2).bc([128, 4, 64]), ALU.mult)
            em.tt(b["ocmp"][:, 4:8, :], b["ocmp"][:, 4:8, :], b["gate"][:, 16 + g * 4:20 + g * 4].un(2).bc([128, 4, 64]), ALU.mult)
            if not _os.environ.get("NSA_NO_WIN"):
                em.tt(b["ob"][:, g * 4:(g + 1) * 4, :], b["ob"][:, g * 4:(g + 1) * 4, :], b["ocmp"][:, 4:8, :], ALU.add)
        ob = b["ob"]
        sl = b["silu"]
        em.act(sl, u[:, O_NZ:O_NZ + 512], AF.Silu)
        em.tt(b["mixed"][:, 256:768], ob.re("p h e -> p (h e)"), sl, ALU.mult)

    def write_states(self, L):
        em, c, b = self.em, self.c, self.pb
        self.out_dma("sp", self.p_gla[L], b["S"])
        cc = b["tmp"][0:64, :, 0:64]
        for h in range(4):
            pp = self.nextAB()
            o_ = (h % 2) * 64
            em.tr(pp[0:64, 0:64], b["Ct"][o_:o_ + 64, h, 0:64], c["ident_f"][o_:o_ + 64, o_:o_ + 64])
            em.cp(cc[:, h, :], pp[0:64, 0:64])
        self.out_dma("sp", self.p_c[L].rearrange("h e d -> e h d"), cc)
        with self.nc.allow_non_contiguous_dma("tiny state vectors"):
            for h in range(4):
                o_ = (h % 2) * 64
                self.out_dma("sp", self.p_n[L, h].rearrange("(d o) -> d o", o=1), b["Ct"][o_:o_ + 64, h, 64:65])
        self.out_dma("sp", self.p_m[L:L + 1, :], b["Mrep"][0:1, :])


    def alloc_dec_bufs(self):
        em, c, b, sb = self.em, self.c, self.pb, self.sb
        d = self.db = {}
        kb_ = b["KslcT"]
        o = [0]

        def carve(n):
            v = kb_[:, o[0]:o[0] + n]
            o[0] += n
            return v
        self.dbufs = {}
        kbufs, vbufs = [], []

        def own(name, v, n=1, lst=None):
            bl = [Buf() for _ in range(n)]
            self.dbufs[name] = bl
            lst.extend(bl)
            return v.wb(bl)
        d["rawT"] = own("rawT", carve(2 * 528).re("p (k i) -> p k i", k=2), 1, kbufs)
        d["KcT"] = own("KcT", carve(512), 1, kbufs)
        d["VcT"] = own("VcT", carve(512), 1, kbufs)
        d["PTall"] = own("PTall", carve(128).re("p (n q) -> p n q", n=4), 1, kbufs)
        d["KTpg"] = own("KTpg", carve(256).re("p (a k) -> p a k", a=2), 2, kbufs)
        d["pgb"] = own("pgb", carve(512).re("p (a f) -> p a f", a=2), 2, kbufs)
        d["PTpg"] = own("PTpg", carve(128).re("p (a q) -> p a q", a=2), 2, kbufs)
        d["Vpg"] = own("Vpg", carve(2 * 2 * 65).re("p (a g e) -> p a g e", a=2, g=2), 2, kbufs)
        assert o[0] <= 4096
        vb_ = b["Vslc"].re("p t g e -> p (t g e)")
        d["Vc"] = own("Vc", vb_[:, 0:4 * 2 * 194].re("p (n g e) -> p n g e", n=4, g=2), 1, vbufs)
        d["stg"] = b["silu"].re("p (a f) -> p a f", a=2)
        d["kc32"] = b["num"].re("p h e -> p (h e)")[0:32, 0:256].re("p (k f) -> p k f", k=2)
        stage = self.stage
        d["cmpb32"] = stage[0:32, 0:256].re("p (k f) -> p k f", k=2)
        d["ptf"] = stage[:, 256:320]
        d["OT"] = stage[0:65, 320:384].re("p (g q) -> p g q", g=2)
        d["imp"] = stage[0:8, 384:656].re("p (g j) -> p g j", g=2)
        d["impt"] = stage[0:8, 656:916].re("p (g j) -> p g j", g=2)
        d["pcolf"] = stage[:, 916:917]
        uh = b["u"][0:8, 2600:3112]
        d["selb"] = uh[:, 0:136]
        d["score"] = uh[:, 136:408].re("p (g j) -> p g j", g=2)
        d["score2"] = b["go"].re("p h e -> p (h e)")[0:8, 0:136]
        d["rope"] = sb("rope_d", [8, 16])
        d["idx"] = self.iota_i32.re("p (a k) -> p a k", a=2)
        d["pti"] = sb("pt_i", [128, 64], I32)
        o2 = [4 * 2 * 194]

        def carve2(n):
            v = vb_[:, o2[0]:o2[0] + n]
            o2[0] += n
            return v
        d["ovl"] = own("ovl", carve2(520).re("p (n j) -> p n j", n=4), 1, vbufs)
        d["selbias"] = own("selbias", carve2(272)[0:8].re("p (g j) -> p g j", g=2), 1, vbufs)
        d["sx"] = own("sx", carve2(512)[0:8].re("p (a g b c) -> p a g b c", a=2, g=2, b=2), 2, vbufs)
        d["Vnew"] = own("Vnew", carve2(260)[0:8].re("p (a g e) -> p a g e", a=2, g=2), 1, vbufs)
        d["anti8"] = own("anti8", carve2(128)[0:8], 1, vbufs)
        d["QTr"] = own("QTr", carve2(64).re("p (g h i) -> p g h i", g=2, h=4), 1, vbufs)
        d["QTn"] = own("QTn", carve2(64).re("p (g h i) -> p g h i", g=2, h=4), 1, vbufs)
        d["KTnew"] = own("KTnew", carve2(16).re("p (j i) -> p j i", j=2), 1, vbufs)
        d["id4_8"] = own("id4_8", carve2(32)[0:8].re("p (h i) -> p h i", h=4), 1, vbufs)
        d["causal8"] = own("causal8", carve2(32)[0:8].re("p (h i) -> p h i", h=4), 1, vbufs)
        assert o2[0] <= 4160
        self.dec_allk = b["KslcT"].wb(list(b["KslcT"].bs) + self.hb["KslcT"] + kbufs)
        self.dec_allv = b["Vslc"].wb(list(b["Vslc"].bs) + self.hb["Vslc"] + vbufs)
        self.pc2 = sb("pcol2_i32", [128, 1], I32)
        em.op("pool", lambda e: e.iota(self.pc2.ap, pattern=[[0, 1]], base=0, channel_multiplier=1), writes=[self.pc2])
        em.dma("sp", d["rope"], self.c_rope_d, serial=True)

    def dec_consts(self):
        em, c, b, d = self.em, self.c, self.pb, self.db
        stage = self.stage
        ov = b["x"][:, 0:520].re("p (n j) -> p n j", n=4)
        em.dma("sp", ov, self.c_ovl_d.rearrange("(n p) j -> p n j", p=128), serial=True)
        em.cp(d["ovl"], ov)
        em.dma("sp", d["selb"], self.c_selb_d, serial=True)
        for h in range(4):
            em.cp(d["id4_8"][:, h, :], c["ident_f"][0:8, 0:8])
            em.cp(d["causal8"][:, h, :], c["causal4"][0:8, 0, 0:8])
        a8 = b["x"][0:8, 600:728]
        em.memset(a8, 0.0)
        em.asel(a8, a8, [[1, 128]], ALU.is_gt, NEG, 0, -1)
        em.cp(d["anti8"], a8)
        em.cp(d["pcolf"], self.pc2)
        em.memset(d["Vpg"][:, :, :, 64:65], 1.0)

    def dv(self, name, a):
        return self.db[name][:, a].wb(self.dbufs[name][a])

    def dec_layer(self, L):
        em, w, c, b = self.em, self.w, self.c, self.pb
        if L == 0:
            self.alloc_dec_bufs()
        d = self.db
        em.memset(self.dec_allk, 0.0)
        em.memset(self.dec_allv, 0.0)
        self.dec_consts()
        for kv in range(2):
            pp = self.nextAB()
            em.mm(pp[0:32, 0:128], c["ones"][0:1, 0:32], w["cmpb_tm"][0:1, kv, :])
            em.cp(d["cmpb32"][:, kv, :], pp[0:32, 0:128])
        for g in range(2):
            em.cp(d["Vc"][:, :, g, 64:194], d["ovl"])
        if not hasattr(self, "yd1_v"):
            self.yd1_v = [V(self.yd1[s_], Buf()) for s_ in range(4)]
        x_src = self.x_sample if L == 0 else self.yd1_v
        y_dst = self.y_sample if L == self.LAYERS - 1 else self.yd1_v
        for s in range(self.NDEC):
            self.dec_seq(L, s, x_src, y_dst)

    def dec_seq(self, L, s, x_src, y_dst):
        em, w, c, b, d = self.em, self.w, self.c, self.pb, self.db
        em.redirect_pool = True
        P = 8
        em.rotate("g0")
        em.rotate("g1")
        em.rotate("w0")
        em.rotate("w1")
        x, u = b["x"], b["u"]
        em.dma("sp", b["S"], self.st_gla[L, s], serial=True)
        em.cp(b["S_b"], b["S"], eng="act")
        em.memset(b["Ct"], 0.0)
        cst = b["tmp"][0:64, :, :].re("p c i -> p (c i)")[:, 0:256].re("p (h d) -> p h d", h=4)
        em.dma("sp", cst, self.st_c[L, s].rearrange("h e d -> e h d"), serial=True)
        for j in range(2):
            pp = self.nextAB()
            em.tr(pp[:, 0:64], cst[:, 2 * j:2 * j + 2, :].re("p h d -> p (h d)"), c["ident_f"][0:64, 0:64])
            em.cp(b["Ct"][0:64, 2 * j, 0:64], pp[0:64, 0:64])
            em.cp(b["Ct"][64:128, 2 * j + 1, 0:64], pp[64:128, 0:64])
        with self.nc.allow_non_contiguous_dma("tiny state vectors"):
            for h in range(4):
                o_ = (h % 2) * 64
                em.dma("sp", b["Ct"][o_:o_ + 64, h, 64:65], self.st_n[L, s, h].rearrange("(d o) -> d o", o=1), serial=True)
            for r in range(3):
                em.dma("sp", b["qkT"][:, :, r], self.st_conv[L, s, r].rearrange("(c p) -> p c", p=128), serial=True)
        em.cp(b["Ct_b"], b["Ct"], eng="act")
        em.dma("sp", b["Mrep"], self.st_m[L, s:s + 1, :].to_broadcast([128, 4]), serial=True)
        em.dma("sp", x[0:P], x_src[s], lane="x")
        ss = b["st1"][0:P, 0:1]
        em.act(u[0:P, 0:D], x[0:P], AF.Square, accum=ss)
        em.ts(ss, ss, 1.0 / D, ALU.mult, EPS, ALU.add)
        em.act(ss, ss, AF.Sqrt)
        em.recip(ss, ss)
        em.stt(b["xn"][0:P], x[0:P], ss, w["normg"][0:P], ALU.mult, ALU.mult)
        for k in range(8):
            pt = self.nextT()
            em.tr(pt[:, 0:P], b["xn"][0:P, k * 128:(k + 1) * 128], c["ident_b"][0:P, 0:P])
            em.cp(b["xnT"][:, k, 0:P], pt[:, 0:P], eng=("act" if k % 2 else "dve"))
        chunks = [(0, 512), (512, 512), (1024, 512), (1536, 512), (2048, 512), (2560, 40), (3112, 512), (3624, 264)]
        for ci, (c0, cw) in enumerate(chunks):
            pp = self.nextAB()
            for k in range(8):
                em.mm(pp[0:P, 0:cw], b["xnT"][:, k, 0:P], w["win"][:, k, c0:c0 + cw], start=(k == 0), stop=(k == 7))
            em.cp(u[0:P, c0:c0 + cw], pp[0:P, 0:cw], eng=("act" if ci % 2 else "dve"))
        for cch in range(4):
            pp = self.nextAB()
            for k in range(8):
                em.mm(pp[:, 0:P], w["win"][:, k, O_MQK + cch * 128:O_MQK + (cch + 1) * 128], b["xnT"][:, k, 0:P],
                      start=(k == 0), stop=(k == 7))
            em.cp(b["qkT"][:, cch, 3:3 + P], pp[:, 0:P], eng=("act" if cch % 2 else "dve"))
        import os as _os
        self.gla_tile(L, P)
        self.mlstm_tile(L, P)
        if not _os.environ.get("SKIP_NSAD"):
            self.nsa_dec(L, s)
        for k in range(8):
            pt = self.nextT()
            em.tr(pt[:, 0:P], b["mixed"][0:P, k * 128:(k + 1) * 128], c["ident_b"][0:P, 0:P])
            em.cp(b["mixT"][:, k, 0:P], pt[:, 0:P], eng=("act" if k % 2 else "dve"))
        for half in range(2):
            pp = self.nextAB()
            for k in range(8):
                em.mm(pp[0:P, 0:512], b["mixT"][:, k, 0:P], w["wout"][:, k, half * 512:(half + 1) * 512],
                      start=(k == 0), stop=(k == 7))
            em.tt(x[0:P, half * 512:(half + 1) * 512], pp[0:P, 0:512], x[0:P, half * 512:(half + 1) * 512], ALU.add)
        self.out_dma("sp", y_dst[s], x[0:P])
        with self.nc.allow_non_contiguous_dma("tiny conv state"):
            for r in range(3):
                self.out_dma("sp", self.s_conv[L, s, r].rearrange("(c p) -> p c", p=128), b["qkT"][:, :, P + r])
        self.out_dma("sp", self.s_gla[L, s], b["S"])
        cc = b["tmp"][0:64, :, 0:64]
        for h in range(4):
            pp = self.nextAB()
            o_ = (h % 2) * 64
            em.tr(pp[0:64, 0:64], b["Ct"][o_:o_ + 64, h, 0:64], c["ident_f"][o_:o_ + 64, o_:o_ + 64])
            em.cp(cc[:, h, :], pp[0:64, 0:64])
        self.out_dma("sp", self.s_c[L, s].rearrange("h e d -> e h d"), cc)
        with self.nc.allow_non_contiguous_dma("tiny state vectors"):
            for h in range(4):
                o_ = (h % 2) * 64
                self.out_dma("sp", self.s_n[L, s, h].rearrange("(d o) -> d o", o=1), b["Ct"][o_:o_ + 64, h, 64:65])
        self.out_dma("sp", self.s_m[L, s:s + 1, :], b["Mrep"][0:1, :])

    def dec_finish_branch(self, g, gate_off):
        em, c, b, d = self.em, self.c, self.pb, self.db
        po = self.pO0 if g == 0 else self.pO1
        em.cp(d["OT"][:, g, :], po[0:65, 0:32])
        pp = self.nextAB()
        ppv = pp[0:8, 0:260].re("p (h e) -> p h e", h=4)
        for hh in range(4):
            em.tr(ppv[:, hh, :], d["OT"][:, g, hh * 8:(hh + 1) * 8], c["ident_f"][0:65, 0:65])
        rd = b["rden"][0:8, 0:4]
        em.recip(rd, ppv[:, :, 64])
        t_ = b["ocmp"][0:8, 0:4, :]
        em.tt(t_, ppv[:, :, 0:64], rd.un(2).bc([8, 4, 64]), ALU.mult)
        em.tt(t_, t_, b["gate"][0:8, gate_off + g * 4:gate_off + g * 4 + 4].un(2).bc([8, 4, 64]), ALU.mult)
        import os as _os
        if not _os.environ.get("NSA_NO_SLC" if gate_off == 8 else "NSA_NO_WIN"):
            em.tt(b["ob"][0:8, g * 4:(g + 1) * 4, :], b["ob"][0:8, g * 4:(g + 1) * 4, :], t_, ALU.add)

    def nsa_dec(self, L, s):
        em, w, c, b, d = self.em, self.w, self.c, self.pb, self.db
        u = b["u"]
        P = 8
        cs = d["rope"]
        em.dma("act", d["pti"], self.page_table[s:s + 1, :].to_broadcast([128, 64]), serial=True)
        em.cp(d["ptf"], d["pti"])
        em.ts(d["ptf"], d["ptf"], 128.0, ALU.mult, d["pcolf"], ALU.add)
        em.ts(d["ptf"], d["ptf"], 2.0, ALU.mult)
        em.cp(d["idx"][:, 0, :], d["ptf"])
        em.ts(d["ptf"], d["ptf"], 1.0, ALU.add)
        em.cp(d["idx"][:, 1, :], d["ptf"])
        qn = b["qn"][0:P]
        self.rms_heads(qn, u[0:P, O_NQ:O_NQ + 512].re("p (h e) -> p h e", h=8), 8, w["qg"][0:P], P=P)
        em.cp(b["qn_b"][0:P].re("p (hh g) e -> p g hh e", g=2), qn.re("p (g hh) e -> p g hh e", g=2), eng="pool")
        self.rope(b["ocmp"][0:P], qn, 8, cs, P=P)
        em.cp(b["qr"][0:P].re("p (hh g) e -> p g hh e", g=2), b["ocmp"][0:P].re("p (g hh) e -> p g hh e", g=2))
        kv = u[0:P, O_NKV:O_NKV + 768].re("p (s g e) -> p s g e", s=6, g=2)
        rows = b["rows"][0:P].re("p s (g e) -> p s g e", g=2)
        winr = b["win"][0:P].re("p s (g e) -> p s g e", g=2)
        em.cp(rows[:, 0], kv[:, 0], eng="pool")
        em.cp(rows[:, 1], kv[:, 1], eng="pool")
        em.cp(rows[:, 3], kv[:, 3], eng="pool")
        em.cp(winr[:, 1], kv[:, 5], eng="pool")
        ktmp = b["ocmp"][0:P, 0:2, :]
        self.rms_heads(ktmp, kv[:, 2], 2, w["kg"][0:P, 1, :], P=P)
        self.rope(rows[:, 2], ktmp, 2, cs, P=P)
        self.rms_heads(ktmp, kv[:, 4], 2, w["kg"][0:P, 2, :], P=P)
        self.rope(winr[:, 0], ktmp, 2, cs, P=P)
        self.out_dma("act", self.s_rows[L, s], b["rows"][0:P].re("p s f -> p (s f)"))
        self.out_dma("act", self.s_win[L, s, 0:504, :], self.st_win[L, s, 8:512, :])
        self.out_dma("act", self.s_win[L, s, 504:512, :], b["win"][0:P].re("p s f -> p (s f)"))
        kkb = b["kk_b"]
        em.cp(kkb[0:P, 0, :], b["rows"][0:P, 2, :])
        em.cp(kkb[0:P, 1, :], b["win"][0:P, 0, :])
        pt = self.nextT()
        for j in range(2):
            em.tr(pt[:, j * 8:(j + 1) * 8], kkb[0:P, j, :], c["ident_b"][0:P, 0:P])
        em.cp(d["KTnew"], pt[:, 0:16].re("p (j i) -> p j i", j=2))
        em.memset(d["Vnew"][:, :, :, 64:65], 1.0)
        em.cp(d["Vnew"][:, 0, :, 0:64], rows[:, 3])
        em.cp(d["Vnew"][:, 1, :, 0:64], winr[:, 1])
        pt = self.nextT()
        qr4 = b["qr"][0:P].re("p (hh g) e -> p hh (g e)", g=2)
        qn4 = b["qn_b"][0:P].re("p (hh g) e -> p hh (g e)", g=2)
        for hh in range(4):
            em.tr(pt[:, hh * 8:(hh + 1) * 8], qr4[:, hh, :], c["ident_b"][0:P, 0:P])
            em.tr(pt[:, 32 + hh * 8:32 + (hh + 1) * 8], qn4[:, hh, :], c["ident_b"][0:P, 0:P])
        for g in range(2):
            em.ts(d["QTr"][:, g].re("p h i -> p (h i)"), pt[:, 0:32], c["pmask2"][:, g:g + 1], ALU.mult)
            em.ts(d["QTn"][:, g].re("p h i -> p (h i)"), pt[:, 32:64], c["pmask2"][:, g:g + 1], ALU.mult)
        em.act(b["gate"][0:P], u[0:P, O_NG:O_NG + 24], AF.Sigmoid)
        cacheL = self.cache[L]
        em.memset(d["rawT"], 0.0)
        em.memset(d["KcT"], 0.0)
        em.memset(d["VcT"], 0.0)
        for bq in range(16):
            em.cp(d["rawT"][:, :, 0:16], d["rawT"][:, :, 512:528], eng="pool")
            for pg in range(4):
                kt = bq * 4 + pg
                a = kt % 2
                stg = d["stg"][:, a, :]
                self.gather(stg, cacheL, d["idx"][:, 0, kt:kt + 1], lane="g%d" % a)
                pgb = self.dv("pgb", a)
                em.cp(pgb, stg, eng=("act" if a else "dve"))
                pt = self.nextT()
                for j in range(2):
                    em.tr(pt[:, j * 128:(j + 1) * 128], pgb[:, j * 128:(j + 1) * 128], c["ident_b"])
                em.cp(d["rawT"][:, :, 16 + pg * 128:16 + (pg + 1) * 128], pt[:, 0:256].re("p (k i) -> p k i", k=2),
                      eng=("dve" if a else "act"))
            for kvi in range(2):
                pp = self.nextAB()
                for s_ in range(32):
                    em.mm(pp[0:32, 0:128], d["rawT"][:, kvi, s_:s_ + 497:16], w["cmpw"][:, kvi, s_, :],
                          start=(s_ == 0), stop=(s_ == 31))
                em.tt(d["kc32"][:, kvi, :], pp[0:32, 0:128], d["cmpb32"][:, kvi, :], ALU.add)
            kcn = b["kk_b"][0:32, 2, :].re("p (g e) -> p g e", g=2)
            self.rms_heads(kcn, d["kc32"][:, 0, :].re("p (g e) -> p g e", g=2), 2, w["kg"][0:32, 0, :], P=32)
            vcb = b["kk_b"][0:32, 3, :]
            em.cp(vcb, d["kc32"][:, 1, :])
            pt = self.nextT()
            em.tr(pt[:, 0:32], kcn.re("p g e -> p (g e)"), c["ident_b"][0:32, 0:32])
            em.tr(pt[:, 32:64], vcb, c["ident_b"][0:32, 0:32])
            n0 = 32 * bq - 1
            if bq == 0:
                em.cp(d["KcT"][:, 0:31], pt[:, 1:32])
                em.cp(d["VcT"][:, 0:31], pt[:, 33:64])
            else:
                em.cp(d["KcT"][:, n0:n0 + 32], pt[:, 0:32])
                em.cp(d["VcT"][:, n0:n0 + 32], pt[:, 32:64])
        for nt_ in range(4):
            pt = self.nextT()
            em.tr(pt[:, 0:128], d["VcT"][:, nt_ * 128:(nt_ + 1) * 128], c["ident_b"])
            em.cp(d["Vc"][:, nt_, :, 0:64], pt[:, 0:128].re("p (g e) -> p g e", g=2))
        for g in range(2):
            ps_ = self.nextS()
            for nt_ in range(4):
                em.mm(ps_[:, nt_ * 32:(nt_ + 1) * 32], d["KcT"][:, nt_ * 128:(nt_ + 1) * 128],
                      d["QTn"][:, g].re("p h i -> p (h i)"))
            em.act(d["PTall"].re("p n q -> p (n q)"), ps_[:, 0:128], AF.Exp, scale=ATT_SCALE)
            for half in range(2):
                po = self.nextAB()
                pov = po[0:8, 0:388].re("p (h e) -> p h e", h=2)
                for h2 in range(2):
                    hh = half * 2 + h2
                    for nt_ in range(4):
                        em.mm(pov[:, h2, :], d["PTall"][:, nt_, hh * 8:(hh + 1) * 8], d["Vc"][:, nt_, g, :],
                              start=(nt_ == 0), stop=(nt_ == 3))
                rd = b["rden"][0:8, 0:2]
                em.ts(rd, pov[:, :, 64], 1e-30, ALU.max)
                em.recip(rd, rd)
                h0 = g * 4 + half * 2
                em.tt(b["ocmp"][0:8, 0:2, :], pov[:, :, 0:64], rd.un(2).bc([8, 2, 64]), ALU.mult)
                em.tt(b["ob"][0:8, h0:h0 + 2, :], b["ocmp"][0:8, 0:2, :], b["gate"][0:8, h0:h0 + 2].un(2).bc([8, 2, 64]), ALU.mult)
                em.tt(d["impt"][:, :, 0:129], pov[:, :, 65:194], rd.un(2).bc([8, 2, 129]), ALU.mult)
                if half == 0:
                    em.tt(d["imp"][:, g, 0:129], d["impt"][:, 0, 0:129], d["impt"][:, 1, 0:129], ALU.add)
                else:
                    em.tt(d["imp"][:, g, 0:129], d["imp"][:, g, 0:129], d["impt"][:, 0, 0:129], ALU.add)
                    em.tt(d["imp"][:, g, 0:129], d["imp"][:, g, 0:129], d["impt"][:, 1, 0:129], ALU.add)
        em.memset(d["imp"][:, :, 129:136], 0.0)
        for g in range(2):
            sc = d["score"][:, g, :]
            em.tt(sc, d["imp"][:, g, :], d["selb"], ALU.add)
            mx8 = b["max8"][0:8]
            em.op("dve", lambda e: e.max(out=mx8.ap, in_=sc.ap), reads=[sc], writes=[mx8])
            em.op("dve", lambda e: e.match_replace(out=d["score2"].ap, in_to_replace=mx8.ap, in_values=sc.ap,
                                                   imm_value=-3.0e38), reads=[sc, mx8], writes=[d["score2"]])
            em.op("dve", lambda e: e.max(out=mx8.ap, in_=d["score2"].ap), reads=[d["score2"]], writes=[mx8])
            em.ts(d["score2"], sc, mx8[:, 7:8], ALU.is_ge, -1.0, ALU.add)
            em.ts(d["selbias"][:, g, :], d["score2"], -NEG, ALU.mult)
        for kt in range(65):
            a = kt % 2
            newt = (kt == 64)
            if not newt:
                stg = d["stg"][:, a, :]
                self.gather(stg, cacheL, d["idx"][:, 1, kt:kt + 1], lane="g%d" % a)
                pgb = self.dv("pgb", a)[:, 0:128]
                em.cp(pgb, stg[:, 0:128], eng=("act" if a else "dve"))
                em.cp(self.dv("Vpg", a)[:, :, 0:64], stg[:, 128:256].re("p (g e) -> p g e", g=2), eng="pool")
                pt = self.nextT()
                em.tr(pt[:, 0:128], pgb, c["ident_b"])
                KT = self.dv("KTpg", a)
                em.cp(KT, pt[:, 0:128], eng=("dve" if a else "act"))
                nk = 128
            else:
                KT = d["KTnew"][:, 0, :]
                nk = 8
            ps_ = self.nextS()
            sx = self.dv("sx", a)
            for g in range(2):
                if not newt:
                    em.cp(sx[:, g], d["selbias"][:, g, 2 * kt:2 * kt + 2].un(2).bc([8, 2, 64]), eng="pool")
                    lt = sx[:, g].re("p a b -> p (a b)")
                else:
                    em.cp(sx[:, g, 0, 0:8], d["selbias"][:, g, 128:129].bc([8, 8]), eng="pool")
                    lt = sx[:, g, 0, 0:8]
                psv = ps_[0:nk, g * 32:(g + 1) * 32]
                em.mm(psv, KT, d["QTr"][:, g].re("p h i -> p (h i)"), start=True, stop=False)
                em.mm(psv, lt, d["id4_8"].re("p h i -> p (h i)"), start=False, stop=not newt)
                if newt:
                    em.mm(psv, c["ident_b"][0:8, 0:8], d["causal8"].re("p h i -> p (h i)"), start=False, stop=True)
            PT = self.dv("PTpg", a)[0:nk]
            em.act(PT, ps_[0:nk, 0:64], AF.Exp, scale=ATT_SCALE)
            for g in range(2):
                po = self.pO0 if g == 0 else self.pO1
                Vr = self.dv("Vpg", a)[:, g, :] if not newt else d["Vnew"][:, 0, g, :]
                em.mm(po[0:65, 0:32], Vr, PT[:, g * 32:(g + 1) * 32], start=(kt == 0), stop=newt)
        for g in range(2):
            self.dec_finish_branch(g, 8)
        for kt in range(5):
            a = kt % 2
            newt = (kt == 4)
            if not newt:
                stg = d["stg"][:, a, :]
                em.dma("sp", stg, self.st_win[L, s, kt * 128:(kt + 1) * 128, :], lane="w%d" % a)
                pgb = self.dv("pgb", a)[:, 0:128]
                em.cp(pgb, stg[:, 0:128], eng=("act" if a else "dve"))
                em.cp(self.dv("Vpg", a)[:, :, 0:64], stg[:, 128:256].re("p (g e) -> p g e", g=2), eng="pool")
                pt = self.nextT()
                em.tr(pt[:, 0:128], pgb, c["ident_b"])
                KT = self.dv("KTpg", a)
                em.cp(KT, pt[:, 0:128], eng=("dve" if a else "act"))
                nk = 128
            else:
                KT = d["KTnew"][:, 1, :]
                nk = 8
            ps_ = self.nextS()
            for g in range(2):
                psv = ps_[0:nk, g * 32:(g + 1) * 32]
                extra = (kt == 0) or newt
                em.mm(psv, KT, d["QTr"][:, g].re("p h i -> p (h i)"), start=True, stop=not extra)
                if kt == 0:
                    em.mm(psv, d["anti8"], d["id4_8"].re("p h i -> p (h i)"), start=False, stop=True)
                if newt:
                    em.mm(psv, c["ident_b"][0:8, 0:8], d["causal8"].re("p h i -> p (h i)"), start=False, stop=True)
            PT = self.dv("PTpg", a)[0:nk]
            em.act(PT, ps_[0:nk, 0:64], AF.Exp, scale=ATT_SCALE)
            for g in range(2):
                po = self.pO0 if g == 0 else self.pO1
                Vr = self.dv("Vpg", a)[:, g, :] if not newt else d["Vnew"][:, 1, g, :]
                em.mm(po[0:65, 0:32], Vr, PT[:, g * 32:(g + 1) * 32], start=(kt == 0), stop=newt)
        for g in range(2):
            self.dec_finish_branch(g, 16)
        sl = b["silu"][0:P]
        em.act(sl, u[0:P, O_NZ:O_NZ + 512], AF.Silu)
        em.tt(b["mixed"][0:P, 256:768], b["ob"][0:P].re("p h e -> p (h e)"), sl, ALU.mult)

    def gather(self, out_v, src_rows, idx_v, lane=None):
        em = self.em
        if em.n_ins >= em.limit:
            return
        em._deps("pool", [idx_v], [out_v])
        ins = self.nc.gpsimd.indirect_dma_start(out=out_v.ap, out_offset=None, in_=src_rows[:, :],
                                                in_offset=bass.IndirectOffsetOnAxis(ap=idx_v.ap, axis=0))
        k = em.lane_key("pool", lane)
        em.cnt[k] += 16
        ins.then_inc(em.sem[k], 16)
        em._mark((k, em.cnt[k]), [idx_v], [out_v])
        em.n_ins += 1


def _host_consts():
    half = 8
    inv = np.exp(-math.log(500000.0) * np.arange(half, dtype=np.float32) * 2.0 / 16).astype(np.float32)
    pos = np.arange(SEQ, dtype=np.float32)
    ang = (pos[:, None] * inv[None, :]).astype(np.float32)
    rope = np.concatenate([np.cos(ang), np.sin(ang)], axis=1).astype(np.float32)
    i = np.arange(256)[:, None]
    j = np.arange(64)[None, :]
    s = i - 4 * j + 1
    cnt = np.minimum(np.minimum(s + 1, 4 + 2 - 1 - s), 2)
    ovl = np.maximum(cnt, 0).astype(np.float32)
    ovl[255] = 0.0
    selb = np.zeros((32, 128, 64), np.float32)
    for t in range(32):
        p = t * 128 + np.arange(128)
        cur = (p // 64)[:, None]
        blk = np.arange(64)[None, :]
        valid = blk * 64 <= p[:, None]
        forced = (blk == 0) | (blk == cur) | (blk == cur - 1)
        bias = np.where(forced, 1.0e9 + 1.0e6 * blk, np.where(valid, 0.0, -1.0e9 - 1.0e6 * blk))
        selb[t] = bias
    return rope, ovl, selb


def _host_consts_dec():
    half = 8
    inv = np.exp(-math.log(500000.0) * np.arange(half, dtype=np.float32) * 2.0 / 16).astype(np.float32)
    pos = (8192 + np.arange(8)).astype(np.float32)
    ang = (pos[:, None] * inv[None, :]).astype(np.float32)
    rope = np.concatenate([np.cos(ang), np.sin(ang)], axis=1).astype(np.float32)
    i = np.arange(512)[:, None]
    j = np.arange(129)[None, :]
    s = i - 4 * j + 1
    cnt = np.minimum(np.minimum(s + 1, 4 + 2 - 1 - s), 2)
    ovl = np.maximum(cnt, 0).astype(np.float32)
    ovl[511] = 0.0
    valid = np.ones((512, 1), np.float32)
    valid[511] = 0.0
    ovl_d = np.concatenate([valid, ovl], axis=1).astype(np.float32)
    selb = np.zeros((8, 136), np.float32)
    p = 8192 + np.arange(8)
    cur = (p // 64)[:, None]
    blk = np.arange(136)[None, :]
    valid_b = (blk * 64 <= p[:, None]) & (blk < 129)
    forced = ((blk == 0) | (blk == cur) | (blk == cur - 1)) & (blk < 129)
    selb[:] = np.where(forced, 1.0e9 + 1.0e6 * blk, np.where(valid_b, 0.0, -1.0e9 - 1.0e6 * blk))
    return rope, ovl_d, selb


_CACHE = {}


def _get_nc(nt, ndec, layers):
    key = (nt, ndec, layers)
    if key not in _CACHE:
        kb = KB(nt, ndec, layers)
        _CACHE[key] = kb.build()
    return _CACHE[key]


def kernel(x_prompt, x_sample, cache_nsa_kv, state_nsa_win, state_gla, state_mlstm_C, state_mlstm_n,
           state_mlstm_m, state_mlstm_conv, page_table, norm_g, w_in, w_out, gla_w_gate, gla_b_gate,
           gla_norm_g, nsa_q_norm_g, nsa_k_norm_g, nsa_cmp_pos, nsa_cmp_w, ml_conv_w, ml_conv_b,
           ml_gate_b, ml_norm_g, _nt=32, _cores=8, _layers=2, _ndec=4):
    f = lambda a: np.ascontiguousarray(np.asarray(a, dtype=np.float32))
    rope, ovl, selb = _host_consts()
    rope_d, ovl_d, selb_d = _host_consts_dec()
    nc = _get_nc(_nt, _ndec, _layers)
    shared = {
        "w_in": f(w_in), "w_out": f(w_out), "norm_g": f(norm_g), "gla_w_gate": f(gla_w_gate),
        "gla_b_gate": f(gla_b_gate), "gla_norm_g": f(gla_norm_g), "nsa_q_norm_g": f(nsa_q_norm_g),
        "nsa_k_norm_g": f(nsa_k_norm_g), "nsa_cmp_pos": f(nsa_cmp_pos), "nsa_cmp_w": f(nsa_cmp_w),
        "ml_conv_w": f(ml_conv_w), "ml_conv_b": f(ml_conv_b), "ml_gate_b": f(ml_gate_b).reshape(DEPTH, 8),
        "ml_norm_g": f(ml_norm_g), "c_rope": rope, "c_ovl": ovl, "c_selb": selb,
        "c_rope_d": rope_d, "c_ovl_d": ovl_d, "c_selb_d": selb_d,
    }
    cache = f(cache_nsa_kv)
    for l_ in range(DEPTH):
        shared["cache_nsa_kv%d" % l_] = cache[l_].reshape(-1, 256)
    pt_ = np.ascontiguousarray(np.asarray(page_table, dtype=np.int32))
    in_maps = []
    for cidx in range(_cores):
        m = dict(shared)
        m["x_prompt"] = f(x_prompt[cidx % 4])
        sl = slice(4 * cidx, 4 * cidx + 4)
        m["x_sample"] = f(x_sample[sl])
        m["state_nsa_win"] = f(state_nsa_win[:, sl]).reshape(DEPTH, 4, 512, 256)
        m["state_gla"] = f(state_gla[:, sl]).reshape(DEPTH, 4, 128, 64)
        m["state_mlstm_C"] = f(state_mlstm_C[:, sl])
        m["state_mlstm_n"] = f(state_mlstm_n[:, sl])
        m["state_mlstm_m"] = f(state_mlstm_m[:, sl])
        m["state_mlstm_conv"] = f(state_mlstm_conv[:, sl])
        m["page_table"] = np.ascontiguousarray(pt_[sl])
        in_maps.append(m)
    res = run_bass_kernel_spmd(nc, in_maps, core_ids=list(range(_cores)))
    R = res.results
    nb = min(4, _cores)
    y_prompt = np.stack([R[i]["y_prompt"] for i in range(nb)])
    p_rows = np.stack([R[i]["p_rows"] for i in range(nb)], axis=1).reshape(DEPTH, nb, SEQ, 4, 2, 64)
    p_win = np.stack([R[i]["p_win"] for i in range(nb)], axis=1).reshape(DEPTH, nb, 512, 2, 2, 64)
    p_gla = np.stack([R[i]["p_gla"] for i in range(nb)], axis=1).reshape(DEPTH, nb, 4, 32, 64)
    p_c = np.stack([R[i]["p_c"] for i in range(nb)], axis=1)
    p_n = np.stack([R[i]["p_n"] for i in range(nb)], axis=1)
    p_m = np.stack([R[i]["p_m"] for i in range(nb)], axis=1)
    p_conv = np.stack([R[i]["p_conv"] for i in range(nb)], axis=1)
    cat = lambda k, ax: np.concatenate([R[i][k] for i in range(_cores)], axis=ax)
    nd = 4 * _cores
    y_sample = cat("y_sample", 0)
    s_rows = cat("s_rows", 1).reshape(DEPTH, nd, 8, 4, 2, 64)
    s_win = cat("s_win", 1).reshape(DEPTH, nd, 512, 2, 2, 64)
    s_gla = cat("s_gla", 1).reshape(DEPTH, nd, 4, 32, 64)
    s_c = cat("s_c", 1)
    s_n = cat("s_n", 1)
    s_m = cat("s_m", 1)
    s_conv = cat("s_conv", 1)
    return (y_prompt, y_sample, p_rows, s_rows, p_win, s_win, p_gla, s_gla, p_c, s_c, p_n, s_n, p_m, s_m,
            p_conv, s_conv)
```
